# Optimizing a Trainium2 kernel written in Bass

```python
import math
import jax, jax.numpy as jnp
from jax import lax
import numpy as np

D_MODEL = 2048
BATCH = 4
SEQ = 2048
DEPTH = 1
DEC_BATCH = 128
DEC_SEQ = 4
PAST_LEN = 2048
PAGE_SIZE = 128

POOL_WIDTH = D_MODEL // 2
POOL_WINDOWS = (2, 4, 8, 16)
POOL_GROUP = POOL_WIDTH // len(POOL_WINDOWS)
POOL_HIST = max(POOL_WINDOWS) - 1
N_HEADS = 16
HEAD_DIM = 64
N_KV_HEADS = 4
GROUP = N_HEADS // N_KV_HEADS
N_KV_SLOTS = 6
N_PAGED_SLOTS = 4
CMP_BLOCK = 32
SEL_BLOCK = 64
N_SELECT = 16
WINDOW = 512
FORCE_BONUS = 1.0e4
Q_BLOCK = 64
ATTN_SCALE = HEAD_DIM ** -0.5
N_BUCKETS = 32
MAX_DISTANCE = 128
D_FF = 5632
CONV_WIDTH = 3
N_BRANCHES = 2
EPS = 1e-6
Q_WIDTH = N_HEADS * HEAD_DIM
KV_WIDTH = N_KV_SLOTS * N_KV_HEADS * HEAD_DIM
SPLIT_SIZES = (POOL_WIDTH, Q_WIDTH, KV_WIDTH, 3 * N_HEADS, N_BRANCHES * D_MODEL)
IN_WIDTH = sum(SPLIT_SIZES)

kernel_name = 'hybrid_pool_nsa_convffn_decode_step'


def rmsnorm(x, g):
    xf = x.astype(jnp.float32)
    y = xf * lax.rsqrt(jnp.mean(xf * xf, axis=-1, keepdims=True) + EPS)
    return (y * g.astype(jnp.float32)).astype(x.dtype)


def rel_bucket(dist):
    n = jnp.maximum(dist, 0)
    max_exact = N_BUCKETS // 2
    nf = jnp.maximum(n, 1).astype(jnp.float32)
    large = max_exact + (jnp.log(nf / max_exact) / math.log(MAX_DISTANCE / max_exact)
                         * (N_BUCKETS - max_exact)).astype(jnp.int32)
    large = jnp.minimum(large, N_BUCKETS - 1)
    return jnp.where(n < max_exact, n, large)


def masked_softmax(s, mask):
    s = jnp.where(mask, s, -jnp.inf)
    m = jnp.max(s, axis=-1, keepdims=True)
    m = jnp.where(jnp.isfinite(m), m, 0.0)
    e = jnp.exp(s - m)
    d = jnp.sum(e, axis=-1, keepdims=True)
    return e / jnp.where(d > 0, d, 1.0)


def project_in(x, g, w_in):
    B, T, _ = x.shape
    z = rmsnorm(x, g) @ w_in
    cuts = [sum(SPLIT_SIZES[:i + 1]) for i in range(len(SPLIT_SIZES) - 1)]
    pool_in, q, kv, ng, mg = jnp.split(z, cuts, axis=-1)
    q = q.reshape(B, T, N_KV_HEADS, GROUP, HEAD_DIM)
    kv = kv.reshape(B, T, N_KV_SLOTS, N_KV_HEADS, HEAD_DIM)
    ng = ng.reshape(B, T, 3, N_KV_HEADS, GROUP)
    mg = mg.reshape(B, T, N_BRANCHES, D_MODEL)
    return pool_in, q, kv, ng, mg


def pool_mixer(x_new, hist, pos, w_grp, scale):
    T = x_new.shape[1]
    xcat = jnp.concatenate([hist, x_new], axis=1)
    xa = xcat.astype(jnp.float32)
    cs = jnp.concatenate([jnp.zeros_like(xa[:, :1]), jnp.cumsum(xa, axis=1)], axis=1)
    end = cs[:, POOL_HIST + 1:]
    outs = []
    for gi, w in enumerate(POOL_WINDOWS):
        c = slice(gi * POOL_GROUP, (gi + 1) * POOL_GROUP)
        win = end[..., c] - cs[:, POOL_HIST + 1 - w:POOL_HIST + 1 - w + T, c]
        cnt = jnp.minimum(w, pos + 1).astype(jnp.float32)[None, :, None]
        pooled = win / cnt - xa[:, POOL_HIST:, c]
        outs.append(jnp.einsum('btc,cd->btd', pooled.astype(x_new.dtype), w_grp[gi]))
    y = jnp.concatenate(outs, axis=-1) * scale
    return y, xcat[:, -POOL_HIST:]


def compress(x, pe, w1, w2):
    B, L = x.shape[:2]
    n = L // CMP_BLOCK
    blk = x[:, :n * CMP_BLOCK].reshape(B, n, CMP_BLOCK, N_KV_HEADS, HEAD_DIM) + pe[:, None, :]
    hid = jax.nn.gelu(jnp.einsum('bnlgd,lde->bnge', blk, w1), approximate=True)
    return jnp.einsum('bnge,ef->bngf', hid, w2)


def compressed_branch(q, q_pos, kc, vc, table):
    nc = kc.shape[1]
    s = jnp.einsum('btghd,bngd->bgthn', q, kc, preferred_element_type=jnp.float32) * ATTN_SCALE
    dist = q_pos[:, None] - (jnp.arange(nc) * CMP_BLOCK + CMP_BLOCK - 1)[None, :]
    bias = jnp.transpose(table[rel_bucket(dist)], (2, 0, 3, 1))[None]
    p = masked_softmax(s + bias, (dist >= 0)[None, None, :, None, :])
    o = jnp.einsum('bgthn,bngd->btghd', p.astype(vc.dtype), vc)
    return o, p


def select_blocks(p_cmp, q_pos, n_sb):
    imp = p_cmp.sum(axis=3)
    nc = imp.shape[-1]
    per = SEL_BLOCK // CMP_BLOCK
    imp = jnp.pad(imp, ((0, 0), (0, 0), (0, 0), (0, n_sb * per - nc)))
    imp = imp.reshape(imp.shape[:-1] + (n_sb, per)).sum(-1)
    blk = jnp.arange(n_sb)[None, :]
    cur = (q_pos // SEL_BLOCK)[:, None]
    forced = (blk == 0) | (blk == cur) | (blk == cur - 1)
    valid = blk * SEL_BLOCK <= q_pos[:, None]
    score = jnp.where(valid, imp + FORCE_BONUS * forced, -jnp.inf)
    _, idx = lax.top_k(score, min(N_SELECT, n_sb))
    return idx


def to_blocks(x, n_sb):
    B, L = x.shape[:2]
    x = jnp.pad(x, ((0, 0), (0, n_sb * SEL_BLOCK - L), (0, 0), (0, 0)))
    return jnp.transpose(x.reshape(B, n_sb, SEL_BLOCK, N_KV_HEADS, HEAD_DIM), (0, 3, 1, 2, 4))


def select_attend(q, q_pos, idx, ks_blk, vs_blk, table):
    B, T = q.shape[:2]
    k = idx.shape[-1]
    bi = jnp.arange(B)[:, None, None]
    gi = jnp.arange(N_KV_HEADS)[None, :, None]
    flat = idx.reshape(B, N_KV_HEADS, T * k)
    kg = ks_blk[bi, gi, flat].reshape(B, N_KV_HEADS, T, k * SEL_BLOCK, HEAD_DIM)
    vg = vs_blk[bi, gi, flat].reshape(B, N_KV_HEADS, T, k * SEL_BLOCK, HEAD_DIM)
    pos = (idx[..., None] * SEL_BLOCK + jnp.arange(SEL_BLOCK)).reshape(B, N_KV_HEADS, T, k * SEL_BLOCK)
    dist = q_pos[None, None, :, None] - pos
    s = jnp.einsum('btghd,bgtkd->bgthk', q, kg, preferred_element_type=jnp.float32) * ATTN_SCALE
    bias = table[rel_bucket(dist), gi[..., None]]
    p = masked_softmax(s + jnp.swapaxes(bias, -1, -2), (dist >= 0)[:, :, :, None, :])
    return jnp.einsum('bgthk,bgtkd->btghd', p.astype(vg.dtype), vg)


def window_attend(q, q_pos, kw, vw, k_pos, table):
    s = jnp.einsum('btghd,bkgd->bgthk', q, kw, preferred_element_type=jnp.float32) * ATTN_SCALE
    dist = q_pos[:, None] - k_pos[None, :]
    bias = jnp.transpose(table[rel_bucket(dist)], (2, 0, 3, 1))[None]
    mask = (dist >= 0) & (dist <= WINDOW) & (k_pos >= 0)[None, :]
    p = masked_softmax(s + bias, mask[None, None, :, None, :])
    return jnp.einsum('bgthk,bkgd->btghd', p.astype(vw.dtype), vw)


def nsa_core(q, q_pos, kv_full, cmp_params, table):
    pe_k, w1_k, w2_k, pe_v, w1_v, w2_v = cmp_params
    kc = compress(kv_full[:, :, 0], pe_k, w1_k, w2_k)
    vc = compress(kv_full[:, :, 1], pe_v, w1_v, w2_v)
    o_cmp, p_cmp = compressed_branch(q, q_pos, kc, vc, table)
    n_sb = -(-kv_full.shape[1] // SEL_BLOCK)
    idx = select_blocks(p_cmp, q_pos, n_sb)
    return o_cmp, idx, to_blocks(kv_full[:, :, 2], n_sb), to_blocks(kv_full[:, :, 3], n_sb)


def nsa_prompt(q, kv, cmp_params, table):
    B, T = q.shape[:2]
    q_pos = jnp.arange(T)
    o_cmp, idx, ks_blk, vs_blk = nsa_core(q, q_pos, kv[:, :, :N_PAGED_SLOTS], cmp_params, table)
    pad = ((0, 0), (WINDOW, 0), (0, 0), (0, 0))
    kw = jnp.pad(kv[:, :, 4], pad)
    vw = jnp.pad(kv[:, :, 5], pad)
    n_ch = T // Q_BLOCK
    n_top = idx.shape[-1]
    qs = jnp.moveaxis(q.reshape(B, n_ch, Q_BLOCK, N_KV_HEADS, GROUP, HEAD_DIM), 1, 0)
    ids = jnp.moveaxis(idx.reshape(B, N_KV_HEADS, n_ch, Q_BLOCK, n_top), 2, 0)

    def one_block(args):
        c, qc, ic = args
        start = c * Q_BLOCK
        qp = start + jnp.arange(Q_BLOCK)
        o_sel = select_attend(qc, qp, ic, ks_blk, vs_blk, table)
        kp = start - WINDOW + jnp.arange(Q_BLOCK + WINDOW)
        kwc = lax.dynamic_slice_in_dim(kw, start, Q_BLOCK + WINDOW, axis=1)
        vwc = lax.dynamic_slice_in_dim(vw, start, Q_BLOCK + WINDOW, axis=1)
        o_win = window_attend(qc, qp, kwc, vwc, kp, table)
        return o_sel, o_win

    o_sel, o_win = lax.map(one_block, (jnp.arange(n_ch), qs, ids))
    unblock = lambda o: jnp.moveaxis(o, 0, 1).reshape(B, T, N_KV_HEADS, GROUP, HEAD_DIM)
    return o_cmp, unblock(o_sel), unblock(o_win)


def nsa_sample(q, kv, past, win_buf, cmp_params, table):
    T = q.shape[1]
    p0 = past.shape[1]
    wb = win_buf.shape[1]
    q_pos = p0 + jnp.arange(T)
    full = jnp.concatenate([past, kv[:, :, :N_PAGED_SLOTS]], axis=1)
    o_cmp, idx, ks_blk, vs_blk = nsa_core(q, q_pos, full, cmp_params, table)
    o_sel = select_attend(q, q_pos, idx, ks_blk, vs_blk, table)
    win = jnp.concatenate([win_buf, kv[:, :, N_PAGED_SLOTS:]], axis=1)
    kp = p0 - wb + jnp.arange(wb + T)
    o_win = window_attend(q, q_pos, win[:, :, 0], win[:, :, 1], kp, table)
    return o_cmp, o_sel, o_win, win[:, -wb:]


def combine_nsa(o_cmp, o_sel, o_win, ng):
    g = jax.nn.sigmoid(ng.astype(jnp.float32)).astype(o_cmp.dtype)[..., None]
    o = g[:, :, 0] * o_cmp + g[:, :, 1] * o_sel + g[:, :, 2] * o_win
    return o.reshape(o.shape[0], o.shape[1], Q_WIDTH)


def merge_and_ffn(h, a_pool, o_nsa, mg, conv_hist, w_pool_proj, w_nsa_proj, w_out, g_post_mix,
                  g_pre_ffn, w_up, conv_w, conv_b, w_down, g_post_ffn):
    gates = jax.nn.sigmoid(mg.astype(jnp.float32)).astype(h.dtype)
    merged = gates[:, :, 0] * (a_pool @ w_pool_proj) + gates[:, :, 1] * (o_nsa @ w_nsa_proj)
    h = h + rmsnorm(merged @ w_out, g_post_mix)
    up = rmsnorm(h, g_pre_ffn) @ w_up
    val, gate = jnp.split(up, 2, axis=-1)
    T = h.shape[1]
    gcat = jnp.concatenate([conv_hist, gate], axis=1)
    conv = conv_b
    for j in range(CONV_WIDTH):
        conv = conv + conv_w[j] * gcat[:, j:j + T]
    y = (jax.nn.gelu(conv, approximate=True) * val) @ w_down
    h = h + rmsnorm(y, g_post_ffn)
    return h, gcat[:, -(CONV_WIDTH - 1):]


def setup_inputs(seed: int = 0) -> dict:
    key = jax.random.key(seed)
    k = jax.random.split(key, 32)
    f32 = jnp.float32
    n_pages = PAST_LEN // PAGE_SIZE
    n_used = DEC_BATCH * n_pages
    n_pool = n_used + max(1, n_used // 4)
    w_buf = min(WINDOW, PAST_LEN)
    nrm = lambda kk, shape, scale: scale * jax.random.normal(kk, shape, f32)
    gain = lambda kk: 1.0 + nrm(kk, (DEPTH, D_MODEL), 0.05)
    page_table = jax.random.permutation(k[0], n_pool)[:n_used].reshape(DEC_BATCH, n_pages).astype(jnp.int32)
    return {
        'x_prompt': nrm(k[1], (BATCH, SEQ, D_MODEL), 1.0),
        'x_sample': nrm(k[2], (DEC_BATCH, DEC_SEQ, D_MODEL), 1.0),
        'cache_kv': nrm(k[3], (DEPTH, n_pool, PAGE_SIZE, N_PAGED_SLOTS, N_KV_HEADS, HEAD_DIM), 1.0),
        'page_table': page_table,
        'state_kv_win': nrm(k[4], (DEPTH, DEC_BATCH, w_buf, 2, N_KV_HEADS, HEAD_DIM), 1.0),
        'state_pool': nrm(k[5], (DEPTH, DEC_BATCH, POOL_HIST, POOL_WIDTH), 1.0),
        'state_conv': nrm(k[6], (DEPTH, DEC_BATCH, CONV_WIDTH - 1, D_FF), 1.0),
        'g_pre_mix': gain(k[7]),
        'w_in': nrm(k[8], (DEPTH, D_MODEL, IN_WIDTH), D_MODEL ** -0.5),
        'pe_cmp_k': nrm(k[9], (DEPTH, CMP_BLOCK, HEAD_DIM), 0.2),
        'w1_cmp_k': nrm(k[10], (DEPTH, CMP_BLOCK, HEAD_DIM, HEAD_DIM), (CMP_BLOCK * HEAD_DIM) ** -0.5),
        'w2_cmp_k': nrm(k[11], (DEPTH, HEAD_DIM, HEAD_DIM), HEAD_DIM ** -0.5),
        'pe_cmp_v': nrm(k[12], (DEPTH, CMP_BLOCK, HEAD_DIM), 0.2),
        'w1_cmp_v': nrm(k[13], (DEPTH, CMP_BLOCK, HEAD_DIM, HEAD_DIM), (CMP_BLOCK * HEAD_DIM) ** -0.5),
        'w2_cmp_v': nrm(k[14], (DEPTH, HEAD_DIM, HEAD_DIM), HEAD_DIM ** -0.5),
        'rel_bias': nrm(k[15], (N_BUCKETS, N_HEADS), 0.5),
        'w_pool_grp': nrm(k[16], (DEPTH, len(POOL_WINDOWS), POOL_GROUP, POOL_GROUP), POOL_GROUP ** -0.5),
        'pool_scale': 1.0 + nrm(k[17], (DEPTH, POOL_WIDTH), 0.1),
        'w_pool_proj': nrm(k[18], (DEPTH, POOL_WIDTH, D_MODEL), POOL_WIDTH ** -0.5),
        'w_nsa_proj': nrm(k[19], (DEPTH, Q_WIDTH, D_MODEL), Q_WIDTH ** -0.5),
        'w_out': nrm(k[20], (DEPTH, D_MODEL, D_MODEL), D_MODEL ** -0.5),
        'g_post_mix': gain(k[21]),
        'g_pre_ffn': gain(k[22]),
        'w_up': nrm(k[23], (DEPTH, D_MODEL, 2 * D_FF), D_MODEL ** -0.5),
        'conv_w': nrm(k[24], (DEPTH, CONV_WIDTH, D_FF), CONV_WIDTH ** -0.5),
        'conv_b': nrm(k[25], (DEPTH, D_FF), 0.02),
        'w_down': nrm(k[26], (DEPTH, D_FF, D_MODEL), D_FF ** -0.5),
        'g_post_ffn': gain(k[27]),
    }


def reference(x_prompt, x_sample, cache_kv, page_table, state_kv_win, state_pool, state_conv,
              g_pre_mix, w_in, pe_cmp_k, w1_cmp_k, w2_cmp_k, pe_cmp_v, w1_cmp_v, w2_cmp_v,
              rel_bias, w_pool_grp, pool_scale, w_pool_proj, w_nsa_proj, w_out, g_post_mix,
              g_pre_ffn, w_up, conv_w, conv_b, w_down, g_post_ffn):
    table = rel_bias.reshape(N_BUCKETS, N_KV_HEADS, GROUP)
    B, T = x_prompt.shape[:2]
    Bs, Ts = x_sample.shape[:2]
    past_len = page_table.shape[1] * PAGE_SIZE
    pos_p = jnp.arange(T)
    pos_s = past_len + jnp.arange(Ts)
    hp, hs = x_prompt, x_sample
    kv_p_l, kv_s_l, win_p_l, win_s_l, pool_p_l, pool_s_l, conv_p_l, conv_s_l = ([] for _ in range(8))
    for l in range(DEPTH):
        cmp_params = (pe_cmp_k[l], w1_cmp_k[l], w2_cmp_k[l], pe_cmp_v[l], w1_cmp_v[l], w2_cmp_v[l])
        lw = (w_pool_proj[l], w_nsa_proj[l], w_out[l], g_post_mix[l], g_pre_ffn[l], w_up[l],
              conv_w[l], conv_b[l], w_down[l], g_post_ffn[l])
        pin, q, kv, ng, mg = project_in(hp, g_pre_mix[l], w_in[l])
        a, pool_new = pool_mixer(pin, jnp.zeros((B, POOL_HIST, POOL_WIDTH), pin.dtype), pos_p,
                                 w_pool_grp[l], pool_scale[l])
        o_cmp, o_sel, o_win = nsa_prompt(q, kv, cmp_params, table)
        o = combine_nsa(o_cmp, o_sel, o_win, ng)
        hp, conv_new = merge_and_ffn(hp, a, o, mg, jnp.zeros((B, CONV_WIDTH - 1, D_FF), hp.dtype), *lw)
        kv_p_l.append(kv[:, :, :N_PAGED_SLOTS])
        win_p_l.append(kv[:, T - min(WINDOW, T):, N_PAGED_SLOTS:])
        pool_p_l.append(pool_new)
        conv_p_l.append(conv_new)
        pin, q, kv, ng, mg = project_in(hs, g_pre_mix[l], w_in[l])
        a, pool_new = pool_mixer(pin, state_pool[l], pos_s, w_pool_grp[l], pool_scale[l])
        past = cache_kv[l, page_table].reshape(Bs, past_len, N_PAGED_SLOTS, N_KV_HEADS, HEAD_DIM)
        o_cmp, o_sel, o_win, win_new = nsa_sample(q, kv, past, state_kv_win[l], cmp_params, table)
        o = combine_nsa(o_cmp, o_sel, o_win, ng)
        hs, conv_new = merge_and_ffn(hs, a, o, mg, state_conv[l], *lw)
        kv_s_l.append(kv[:, :, :N_PAGED_SLOTS])
        win_s_l.append(win_new)
        pool_s_l.append(pool_new)
        conv_s_l.append(conv_new)
    return (hp, hs, jnp.stack(kv_p_l), jnp.stack(kv_s_l), jnp.stack(win_p_l), jnp.stack(win_s_l),
            jnp.stack(pool_p_l), jnp.stack(pool_s_l), jnp.stack(conv_p_l), jnp.stack(conv_s_l))
```

```python
import math
import os
import contextlib
import numpy as np
import concourse.bass as bass
import concourse.mybir as mybir
from concourse.bass_utils import run_bass_kernel_spmd

F32 = mybir.dt.float32
BF16 = mybir.dt.bfloat16
I32 = mybir.dt.int32
AF = mybir.ActivationFunctionType
ALU = mybir.AluOpType
AX = mybir.AxisListType

D = 2048
NPOOL_PAGES = 2560
IN_W = 7728
DFF = 5632
EPS = 1e-6
NEG = -30000.0
NSB = 16
WF = 4352
F0 = 2048


class Dep:
    __slots__ = ("name", "w", "r", "dsem", "dcnt", "excl")

    def __init__(self, name=""):
        self.name = name
        self.excl = False
        self.w = None
        self.r = {}
        self.dsem = None
        self.dcnt = 0


class Ctx:
    ENGS = ("pe", "act", "dve", "pool", "sp")

    def __init__(self, nc):
        self.nc = nc
        self.eng = {"pe": nc.tensor, "act": nc.scalar, "dve": nc.vector,
                    "pool": nc.gpsimd, "sp": nc.sync}
        self.sem = {e: nc.alloc_semaphore(name="c_" + e) for e in self.ENGS}
        self.cnt = {e: 0 for e in self.ENGS}
        self.seen = {e: {} for e in self.ENGS}
        self.nsem = 0
        self.out_waits = {}

    def new_sem(self, name):
        self.nsem += 1
        return self.nc.alloc_semaphore(name="d%d_%s" % (self.nsem, name))

    def _need(self, e, ev):
        if ev is None:
            return
        kind, key, val = ev
        if kind == "e":
            if key == e and key == "pe":
                return
            semh = self.sem[key]
            k = "e_" + key
        else:
            semh = key
            k = key.name
        if self.seen[e].get(k, 0) >= val:
            return
        self.seen[e][k] = val
        self.eng[e].wait_ge(semh, val)

    def _deps(self, e, reads, writes):
        for d in reads:
            self._need(e, d.w)
            if d.excl:
                for ev in d.r.values():
                    self._need(e, ev)
        for d in writes:
            self._need(e, d.w)
            for ev in d.r.values():
                self._need(e, ev)

    @staticmethod
    def _addr(d, ev):
        k = ev[1] if ev[0] == "e" else ev[1].name
        old = d.r.get(k)
        if old is None or old[2] < ev[2]:
            d.r[k] = ev

    def op(self, e, fn, reads=(), writes=()):
        self._deps(e, reads, writes)
        ins = fn()
        self.cnt[e] += 1
        ins.then_inc(self.sem[e], 1)
        ev = ("e", e, self.cnt[e])
        for d in reads:
            self._addr(d, ev)
        for d in writes:
            d.w = ev
            d.r = {}
        return ins

    def dma(self, e, out, in_, owner, reads=(), writes=(), is_output=False, group=False, fn=None, **kw):
        if group and owner.dsem is not None:
            for d in reads:
                self._need(e, d.w)
            for d in writes:
                if not (d.w is not None and d.w[0] == "d" and d.w[1] is owner.dsem):
                    self._need(e, d.w)
                for ev in d.r.values():
                    self._need(e, ev)
        else:
            self._deps(e, reads, writes)
        if owner.dsem is None:
            owner.dsem = self.new_sem(owner.name)
        if not group and owner.dcnt > 0:
            self._need(e, ("d", owner.dsem, owner.dcnt))
        if fn is not None:
            ins = fn()
        else:
            ins = self.eng[e].dma_start(out=out, in_=in_, **kw)
        owner.dcnt += 16
        ins.then_inc(owner.dsem, 16)
        ev = ("d", owner.dsem, owner.dcnt)
        for d in writes:
            d.w = ev
            d.r = {}
        for d in reads:
            self._addr(d, ev)
        if is_output:
            self.out_waits[owner.dsem.name] = (owner.dsem, owner.dcnt)
        return ins

    def finish(self):
        for s, c in self.out_waits.values():
            self.eng["sp"].wait_ge(s, c)
        for e in self.ENGS:
            if e != "sp" and self.cnt[e] > 0:
                self.eng["sp"].wait_ge(self.sem[e], self.cnt[e])


class View:
    def __init__(self, ap, d):
        self.ap = ap
        self.d = d

    def __getitem__(self, k):
        return self.ap[k]


class T:
    def __init__(self, t, name):
        self.t = t
        self.d = Dep(name)

    def __getitem__(self, k):
        return self.t[k]


def _rel_bucket(n):
    n = np.maximum(n, 0)
    nf = np.maximum(n, 1).astype(np.float32)
    large = 16 + (np.log(nf / np.float32(16)) / np.float32(math.log(8.0)) * np.float32(16)).astype(np.int32)
    large = np.minimum(large, 31)
    return np.where(n < 16, n, large)


def _const_tables(half):
    dist = np.arange(WF) - F0
    bk = _rel_bucket(dist)
    oh = np.zeros((33, WF), np.float32)
    valid = dist >= 0
    oh[bk[valid], np.nonzero(valid)[0]] += 1.0
    oh[31, valid] -= 1.0
    oh[32, ~valid] = 1.0
    far = np.where(dist <= 512, 0.0, NEG).astype(np.float32)[None]
    dead = 1024 if half == 0 else 0
    kb = np.zeros((1, 2048), np.float32)
    kb[0, :dead] = NEG
    kbc = np.zeros((64, 1), np.float32)
    kbc[: dead // 32, 0] = NEG
    apos = np.arange(1022, 2048)
    tpos = apos - dead
    blk = np.arange(32)
    tblk = blk - dead // 64
    cur = np.floor_divide(tpos, 64)
    bonus = np.zeros((1026, 32), np.float32)
    bonus += np.where(tblk[None, :] == 0, 3.0e4, 0.0)
    bonus += np.where((tblk[None, :] == cur[:, None]) & (tblk[None, :] != 0), 2.0e4, 0.0)
    bonus += np.where((tblk[None, :] == cur[:, None] - 1) & (tblk[None, :] != 0), 1.0e4, 0.0)
    invalid = (tblk[None, :] < 0) | (tblk[None, :] * 64 > tpos[:, None])
    bonus = np.where(invalid, -1.0e30, bonus).astype(np.float32)
    sdead = np.where(tblk < 0, NEG, 0.0).astype(np.float32)[None].repeat(128, 0)
    rc = np.zeros((4, 1026), np.float32)
    for gi, w in enumerate((2, 4, 8, 16)):
        rc[gi] = 1.0 / np.minimum(w, np.maximum(tpos, 0) + 1)
    return dict(oh=oh, far=far, kb=kb, kbc=kbc, bonus=bonus, sdead=sdead, rc=rc)


def _selb_table():
    t = np.zeros((16, 4, 16, 4, 4), np.float32)
    for b in range(16):
        for tt in range(4):
            t[b, tt, b, :, tt] = 1.0
    return np.ascontiguousarray(t.reshape(64, 256))


def _sample_bonus():
    b = np.zeros((1, 33), np.float32)
    b[0, 0] = 3.0e4
    b[0, 32] = 2.0e4
    b[0, 31] = 1.0e4
    return b


class Prog:
    def __init__(self, npool=NPOOL_PAGES, stage=99):
        self.stage = stage
        nc = self.nc = bass.Bass("TRN2", target_bir_lowering=False)
        self.cx = Ctx(nc)
        self.es = contextlib.ExitStack()
        self.npool = npool
        self.dram = {}
        self.ddep = {}

    def din(self, name, shape, dt=F32):
        t = self.nc.dram_tensor(name, list(shape), dt, kind="ExternalInput").ap()
        self.dram[name] = t
        self.ddep[name] = Dep(name)
        return t

    def dout(self, name, shape, dt=F32):
        t = self.nc.dram_tensor(name, list(shape), dt, kind="ExternalOutput").ap()
        self.dram[name] = t
        self.ddep[name] = Dep(name)
        return t

    def dscr(self, name, shape, dt=F32):
        t = self.nc.dram_tensor(name, list(shape), dt, kind="Internal").ap()
        self.dram[name] = t
        self.ddep[name] = Dep(name)
        return t

    def sb(self, name, shape, dt):
        return T(self.es.enter_context(self.nc.sbuf_tensor(name, list(shape), dt)), name)

    def ps(self, name, shape, dt):
        return T(self.es.enter_context(self.nc.psum_tensor(name, list(shape), dt)), name)

    def op(self, e, fn, reads=(), writes=()):
        return self.cx.op(e, fn, [getattr(x, "d", x) for x in reads], [getattr(x, "d", x) for x in writes])

    def load(self, e, dst, dst_ap, src_ap, reads=(), **kw):
        return self.cx.dma(e, dst_ap, src_ap, owner=dst.d, reads=[getattr(x, "d", x) for x in reads], writes=[dst.d], **kw)

    def store(self, e, dst_ap, src, src_ap, ddep=None, is_output=True, **kw):
        w = [ddep] if ddep is not None else []
        return self.cx.dma(e, dst_ap, src_ap, owner=src.d, reads=[src.d], writes=w, is_output=is_output, **kw)

    def mm(self, out, lhsT, rhs, start, stop, reads, writes):
        nc = self.nc
        return self.op("pe", lambda: nc.tensor.matmul(out, lhsT, rhs, start=start, stop=stop), reads, writes)

    def tr(self, out, in_, ident, reads, writes):
        nc = self.nc
        return self.op("pe", lambda: nc.tensor.transpose(out=out, in_=in_, identity=ident), reads, writes)


def dap(t, offset, pat):
    return bass.AP(t.tensor, offset, [list(p) for p in pat])


def build_program(npool=NPOOL_PAGES, stage=99):
    P = Prog(npool, stage)
    nc, cx = P.nc, P.cx
    V, A, G = nc.vector, nc.scalar, nc.gpsimd

    xb = P.din("xb", [2048, D]); xs = P.din("xs", [64, D])
    cache = P.din("cache", [npool * 128, 1024]); ptab = P.din("ptab", [1, 256], I32)
    swin = P.din("swin", [NSB * 512, 512]); spool = P.din("spool", [NSB * 15, 1024])
    sconv = P.din("sconv", [NSB * 2, DFF])
    w_in = P.din("w_in", [D, IN_W]); w_pp = P.din("w_pp", [1024, D]); w_np = P.din("w_np", [1024, D])
    w_out = P.din("w_out", [D, D]); w_up = P.din("w_up", [D, 2 * DFF]); w_down = P.din("w_down", [DFF, D])
    w1k = P.din("w1k", [32, 4096]); w1v = P.din("w1v", [32, 4096])
    w2k = P.din("w2k", [64, 64]); w2v = P.din("w2v", [64, 64])
    pek = P.din("pek", [32, 64]); pev = P.din("pev", [32, 64])
    relb = P.din("relb", [32, 16]); wgrp = P.din("wgrp", [1024, 256])
    g_pm = P.din("g_pm", [1, D]); g_qm = P.din("g_qm", [1, D]); g_pf = P.din("g_pf", [1, D]); g_qf = P.din("g_qf", [1, D])
    pscale = P.din("pscale", [1, 1024]); conv_w = P.din("conv_w", [3, DFF]); conv_b = P.din("conv_b", [1, DFF])
    t_oh = P.din("t_oh", [33, WF]); t_far = P.din("t_far", [1, WF]); t_kb = P.din("t_kb", [1, 2048])
    t_kbc = P.din("t_kbc", [64, 1]); t_bonus = P.din("t_bonus", [1026, 32]); t_sdead = P.din("t_sdead", [128, 32])
    t_rc = P.din("t_rc", [4, 1026]); t_sbonus = P.din("t_sbonus", [1, 33]); t_msame = P.din("t_msame", [16, 16]); t_selb = P.din("t_selb", [64, 256]); t_mi = P.din("t_mi", [16, 4])

    y_o = P.dout("y", [1088, D]); kv_o = P.dout("kvrows", [1088, 1024]); winp_o = P.dout("winp", [512, 512])
    wins_o = P.dout("wins", [NSB * 512, 512]); poolp_o = P.dout("poolp", [15, 1024])
    pools_o = P.dout("pools", [NSB * 15, 1024]); convp_o = P.dout("convp", [2, DFF]); convs_o = P.dout("convs", [NSB * 2, DFF])

    frow = P.dscr("frow", [17, WF]); kvs_scr = P.dscr("kvs_scr", [64, 1536])
    h_scr = P.dscr("h_scr", [1090, D]); y_scr = P.dscr("y_scr", [1090, D])

    B = [P.ps("bank%d" % i, [128, 512], F32) for i in range(8)]
    for b_ in B:
        b_.d.excl = True

    def bfv(bank):
        return bank.t[:].bitcast(BF16)

    ident_b = P.sb("ident_b", [128, 128], BF16); ident_f = P.sb("ident_f", [128, 128], F32)
    Jx = P.sb("Jx", [128, 128], F32)
    Nt = P.sb("Nt", [128, 16, 256], BF16)
    Tfar = P.sb("Tfar", [128, 128], BF16)
    Ns = P.sb("Ns", [16, 4, 132], BF16); Cs = P.sb("Cs", [16, 4, 64], BF16); Tfs = P.sb("Tfs", [16, 128], BF16)
    kbrep = P.sb("kbrep", [128, 16], F32)
    kbc = P.sb("kbc", [64, 1], F32)
    sdead = P.sb("sdead", [128, 32], F32)
    sbonus = P.sb("sbonus", [16, 33], F32)
    gcolA = P.sb("gcolA", [128, 16], F32)
    gcolB = P.sb("gcolB", [128, 16], F32)
    pscol = P.sb("pscol", [128, 8], F32)
    cwcol = P.sb("cwcol", [128, 44, 4], F32)
    grep = P.sb("grep", [128, D], F32)
    wgrp_sb = P.sb("wgrp_sb", [128, 8, 256], BF16)
    w1k_sb = P.sb("w1k_sb", [128, 32, 64], BF16); w1v_sb = P.sb("w1v_sb", [128, 32, 64], BF16)
    w2k_sb = P.sb("w2k_sb", [128, 128], BF16)
    w2v_sb = P.sb("w2v_sb", [128, 64], BF16)
    peT = P.sb("peT", [128, 2, 128], F32)
    kselT = P.sb("kselT", [128, 2, 2048], BF16); kwinT = P.sb("kwinT", [128, 2, 2048], BF16)
    vsel = P.sb("vsel", [128, 16, 256], BF16); vwin = P.sb("vwin", [128, 16, 256], BF16)
    kcT = P.sb("kcT", [128, 4, 64], BF16); vc = P.sb("vc", [64, 4, 64], BF16)
    ksT_new = P.sb("ksT_new", [128, 2, 64], BF16); kwT_new = P.sb("kwT_new", [128, 2, 64], BF16)

    RA = P.sb("RA", [128, 27904], BF16)
    RU = P.sb("RU", [128, 16 * 592], BF16)
    xt = P.sb("xt", [128, D], F32)
    xt2 = P.sb("xt2", [128, D], F32)
    slabs = [P.sb("slab%d" % i, [128, 5632], BF16) for i in range(2)]
    small = P.sb("small", [128, 64], F32)
    ub = P.sb("ub", [128, D], BF16)
    tmpF = [P.sb("tmpF%d" % i, [128, 512], F32) for i in range(2)]
    tmpB = [P.sb("tmpB%d" % i, [128, 512], BF16) for i in range(2)]
    PT = [P.sb("PT%d" % i, [128, 512], BF16) for i in range(2)]
    rk = P.sb("rk", [128, 33 * 33], BF16)

    P.op("pool", lambda: G.memset(ident_f[:], 0.0), writes=[ident_f])
    P.op("pool", lambda: G.affine_select(out=ident_f[:], in_=ident_f[:], pattern=[[-1, 128]], compare_op=ALU.not_equal,
                                         fill=1.0, base=0, channel_multiplier=1), reads=[ident_f], writes=[ident_f])
    P.op("pool", lambda: G.tensor_copy(out=ident_b[:], in_=ident_f[:]), reads=[ident_f], writes=[ident_b])
    P.op("pool", lambda: G.memset(Jx[:], 0.0), writes=[Jx])
    P.op("pool", lambda: G.affine_select(out=Jx[:], in_=Jx[:], pattern=[[1, 128]], compare_op=ALU.not_equal,
                                         fill=1.0, base=-127, channel_multiplier=1), reads=[Jx], writes=[Jx])

    P.load("sp", kbrep, kbrep[:], dap(t_kb, 0, [[0, 128], [128, 16]]), allow_slow_non_contiguous=True)
    P.load("sp", kbc, kbc[:], t_kbc[:, :])
    P.load("sp", sdead, sdead[:], t_sdead[:, :])
    P.load("sp", sbonus, sbonus[:], dap(t_sbonus, 0, [[0, 16], [1, 33]]))
    P.load("sp", gcolA, gcolA[:], dap(g_pm, 0, [[1, 128], [128, 16]]), allow_slow_non_contiguous=True)
    P.load("sp", gcolB, gcolB[:], dap(g_pf, 0, [[1, 128], [128, 16]]), allow_slow_non_contiguous=True)
    P.load("sp", pscol, pscol[:], dap(pscale, 0, [[1, 128], [128, 8]]), allow_slow_non_contiguous=True)
    for j in range(3):
        P.load("sp", cwcol, cwcol[:, :, j], dap(conv_w, j * DFF, [[1, 128], [128, 44]]), group=(j > 0),
               allow_slow_non_contiguous=True)
    P.load("sp", cwcol, cwcol[:, :, 3], dap(conv_b, 0, [[1, 128], [128, 44]]), group=True, allow_slow_non_contiguous=True)
    P.load("pool", wgrp_sb, wgrp_sb[:], dap(wgrp, 0, [[256, 128], [128 * 256, 8], [1, 256]]))
    for hf in range(2):
        P.load("pool", w1k_sb, w1k_sb[64 * hf:64 * hf + 64, :, :], dap(w1k, 0, [[64, 64], [4096, 32], [1, 64]]), group=(hf > 0))
        P.load("pool", w1v_sb, w1v_sb[64 * hf:64 * hf + 64, :, :], dap(w1v, 0, [[64, 64], [4096, 32], [1, 64]]), group=(hf > 0))
        P.load("pool", w2k_sb, w2k_sb[64 * hf:64 * hf + 64, :].rearrange("p (a b) -> p a b", a=2),
               dap(w2k, 0, [[64, 64], [0, 2], [1, 64]]), group=(hf > 0))
        P.load("pool", w2v_sb, w2v_sb[64 * hf:64 * hf + 64, :], w2v[:, :], group=(hf > 0))

    if stage <= 1:
        return P, locals()
    pe_sb = P.sb("pe_sb", [32, 128], F32)
    P.load("sp", pe_sb, pe_sb[:, 0:64], pek[:, :])
    P.load("sp", pe_sb, pe_sb[:, 64:128], pev[:, :], group=True)
    for s_ in range(2):
        P.tr(B[0][0:64, 32 * s_:32 * s_ + 32], pe_sb[:, 64 * s_:64 * s_ + 64], ident_f[0:32, 0:32], [pe_sb, ident_f], [B[0]])
    for s_ in range(2):
        src = B[0][0:64, 32 * s_:32 * s_ + 32]
        for hf in range(2):
            P.op("dve", lambda hf=hf, s_=s_, src=src: V.tensor_copy(
                out=peT[64 * hf:64 * hf + 64, s_, :].rearrange("p (r l) -> p r l", r=4),
                in_=bass.AP(src.tensor, src.offset, [list(src.ap[0]), [0, 4], [1, 32]])), [B[0]], [peT])

    if stage <= 2:
        return P, locals()
    rbext = P.sb("rbext", [33, 16], F32)
    P.load("sp", rbext, rbext[0:32, :], relb[:, :])
    P.op("pool", lambda: G.memset(rbext[32:33, :], NEG), writes=[rbext])
    fdep = P.ddep["frow"]
    for ch in range(9):
        n = min(512, WF - ch * 512)
        P.load("sp", tmpF[0], tmpF[0][0:33, 0:n], t_oh[:, ch * 512:ch * 512 + n])
        P.mm(B[1][0:16, 0:n], rbext[0:33, 0:16], tmpF[0][0:33, 0:n], True, True, [rbext, tmpF[0]], [B[1]])
        P.op("dve", lambda n=n: V.tensor_copy(out=tmpF[1][0:16, 0:n], in_=B[1][0:16, 0:n]), [B[1]], [tmpF[1]])
        P.store("sp", frow[0:16, ch * 512:ch * 512 + n], tmpF[1], tmpF[1][0:16, 0:n], ddep=fdep, is_output=False)
    cx.dma("sp", frow[16:17, :], t_far[:, :], owner=P.ddep["t_far"], writes=[fdep])

    if stage <= 3:
        return P, locals()
    Hk = View(xt2.t[:, :].rearrange("p (a b) -> p a b", a=16), xt2.d)
    for blk, c0 in ((0, F0 + 1), (1, F0 - 127)):
        P.load("sp", Hk, Hk[:], dap(frow, c0, [[1, 128], [WF, 16], [1, 128]]), reads=[fdep])
        for h4 in range(4):
            bk = B[2 + (h4 % 2)]
            for i in range(4):
                P.mm(bk[:, 128 * i:128 * i + 128], Hk[:, 4 * h4 + i, :], Jx[:], True, True, [Hk, Jx], [bk])
            P.op("dve", lambda bk=bk, h4=h4, blk=blk: V.tensor_copy(
                out=Nt[:, 4 * h4:4 * h4 + 4, 128 * blk:128 * blk + 128],
                in_=bk[:, :].rearrange("p (a b) -> p a b", a=4)), [bk], [Nt])
    P.load("sp", Hk, Hk[:, 0, :], dap(frow, 16 * WF + F0 + 385, [[1, 128], [1, 128]]), reads=[fdep])
    P.mm(B[2][:, 0:128], Hk[:, 0, :], Jx[:], True, True, [Hk, Jx], [B[2]])
    P.op("dve", lambda: V.tensor_copy(out=Tfar[:], in_=B[2][:, 0:128]), [B[2]], [Tfar])
    if stage <= 4:
        return P, locals()
    for g in range(4):
        for blk, c0, ny in ((0, F0 + 1, 128), (1, F0 - 127, 4)):
            P.load("sp", Hk, Hk[:, 0, 0:16].rearrange("p (a b) -> p a b", a=4),
                   dap(frow, 4 * g * WF + c0, [[1, 128], [WF, 4], [1, 4]]), reads=[fdep], allow_slow_non_contiguous=True)
            P.mm(B[2][0:16, 0:ny], Hk[:, 0, 0:16], Jx[:, 0:ny], True, True, [Hk, Jx], [B[2]])
            P.op("dve", lambda g=g, blk=blk, ny=ny: V.tensor_copy(out=Ns[:, g, 128 * blk:128 * blk + ny], in_=B[2][0:16, 0:ny]),
                 [B[2]], [Ns])
        P.load("sp", Hk, Hk[0:64, 0, 0:16].rearrange("p (a b) -> p a b", a=4),
               dap(frow, 4 * g * WF + F0 + 1, [[32, 64], [WF, 4], [1, 4]]), reads=[fdep], allow_slow_non_contiguous=True)
        P.mm(B[2][0:16, 0:64], Hk[0:64, 0, 0:16], Jx[0:64, 64:128], True, True, [Hk, Jx], [B[2]])
        P.op("dve", lambda g=g: V.tensor_copy(out=Cs[:, g, :], in_=B[2][0:16, 0:64]), [B[2]], [Cs])
    P.load("sp", Hk, Hk[:, 0, 0:16].rearrange("p (a b) -> p a b", a=4),
           dap(frow, 16 * WF + F0 + 385, [[1, 128], [0, 4], [1, 4]]), reads=[fdep], allow_slow_non_contiguous=True)
    P.mm(B[2][0:16, 0:128], Hk[:, 0, 0:16], Jx[:], True, True, [Hk, Jx], [B[2]])
    P.op("dve", lambda: V.tensor_copy(out=Tfs[:], in_=B[2][0:16, 0:128]), [B[2]], [Tfs])

    if stage <= 5:
        return P, locals()
    def bc(ap, n):
        return bass.AP(ap.tensor, ap.offset, [list(p) for p in ap.ap] + [[0, n]])

    def bcmid(ap, n):
        pat = [list(p) for p in ap.ap]
        return bass.AP(ap.tensor, ap.offset, [pat[0], [0, n]] + pat[1:])

    def norm_T(x_ap, n, dst3, gcol, bkA, bkB, x_reads=(), xt=xt):
        P.load("sp", xt, xt[0:n, :], x_ap, reads=list(x_reads))
        P.op("act", lambda: A.activation(out=ub[0:n, :], in_=xt[0:n, :], func=AF.Square, accum_out=small[0:n, 0:1]),
             [xt], [ub, small])
        P.op("dve", lambda: V.tensor_scalar(out=small[0:n, 1:2], in0=small[0:n, 0:1], scalar1=1.0 / D, scalar2=EPS,
                                            op0=ALU.mult, op1=ALU.add), [small], [small])
        P.op("act", lambda: A.activation(out=small[0:n, 3:4], in_=small[0:n, 1:2], func=AF.Sqrt), [small], [small])
        P.op("dve", lambda: V.reciprocal(out=small[0:n, 2:3], in_=small[0:n, 3:4]), [small], [small])
        P.op("dve", lambda: V.tensor_scalar(out=ub[0:n, :], in0=xt[0:n, :], scalar1=small[0:n, 2:3], scalar2=None,
                                            op0=ALU.mult), [xt, small], [ub])
        for half in range(2):
            bk = (bkA, bkB)[half]
            bv = bfv(bk)
            for i in range(8):
                kc = 8 * half + i
                P.tr(bv[:, 128 * i:128 * i + n], ub[0:n, 128 * kc:128 * kc + 128], ident_b[0:n, 0:n], [ub, ident_b], [bk])
            P.op("dve", lambda bv=bv, half=half: V.tensor_tensor(
                out=dst3[:, 8 * half:8 * half + 8, 0:n],
                in0=bv[:, :].rearrange("p (a b) -> p a b", a=8)[:, :, 0:n],
                in1=bc(gcol[:, 8 * half:8 * half + 8], n), op=ALU.mult), [bk, gcol], [dst3_dep[0]])

    dst3_dep = [None]

    def wslab_load(sl, w_ap_t, row0, nk, col0, ncols, row_stride):
        v = sl.t[:, 0:nk * ncols].rearrange("p (a b) -> p a b", a=nk)
        P.load("pool", sl, v, dap(w_ap_t, row0 * row_stride + col0, [[row_stride, 128], [128 * row_stride, nk], [1, ncols]]))
        return v

    wkv = RA.t[:, 0:24576].rearrange("p (a b) -> p a b", a=16)
    P.load("pool", RA, wkv, dap(w_in, 2048, [[IN_W, 128], [128 * IN_W, 16], [1, 1536]]))
    kvt = RA.t[:, 24576:24576 + 3072].bitcast(F32)
    kvt_d = Dep("kvt")
    kvb2 = [slabs[1].t[:, 0:1536], slabs[1].t[:, 1536:3072]]
    kvb_d2 = [Dep("kvb0"), Dep("kvb1")]
    uTk2 = [slabs[0].t[:, 0:2048].rearrange("p (a b) -> p a b", a=16), slabs[0].t[:, 2048:4096].rearrange("p (a b) -> p a b", a=16)]
    uTk_d2 = [Dep("uTk0"), Dep("uTk1")]
    kcmpT = RU.t[:, 0:4096].rearrange("p (a b) -> p a b", a=2)
    vcmpT = RU.t[:, 4096:8192].rearrange("p (a b) -> p a b", a=2)
    kcmp_d = Dep("kcmpT"); vcmp_d = Dep("vcmpT")
    kvsd = P.ddep["kvs_scr"]

    cx.dma("sp", dap(wins_o, 0, [[512 * 512, NSB], [1, 508 * 512]]), dap(swin, 4 * 512, [[512 * 512, NSB], [1, 508 * 512]]),
           owner=P.ddep["wins"], is_output=True)
    cx.dma("sp", dap(pools_o, 0, [[15 * 1024, NSB], [1, 11 * 1024]]), dap(spool, 4 * 1024, [[15 * 1024, NSB], [1, 11 * 1024]]),
           owner=P.ddep["pools"], is_output=True)

    if stage <= 6:
        return P, locals()
    KD = int(os.environ.get('KDBG', '99'))
    for a in (range(17) if KD >= 9 else range(16, 17) if KD == 8 else range(1)):
        n = 128 if a < 16 else 64
        own = a >= 8
        kvb, kvb_d, uTk, uTk_d = kvb2[a % 2], kvb_d2[a % 2], uTk2[a % 2], uTk_d2[a % 2]
        dst3_dep[0] = uTk_d
        norm_T(xb[128 * a:128 * a + 128, :] if a < 16 else xs[:, :], n, uTk, gcolA, B[0], B[1], xt=(xt if a % 2 == 0 else xt2))
        if KD < 2:
            continue
        for cg in range(3):
            bk = B[4 + cg]
            for kc in range(16):
                P.mm(bk[0:n, :], uTk[:, kc, 0:n], wkv[:, kc, 512 * cg:512 * cg + 512], kc == 0, kc == 15, [uTk_d, RA], [bk])
            KS = int(os.environ.get('KSUB', '3'))
            if KS >= 1:
                P.op("act", lambda bk=bk, cg=cg, n=n: A.copy(out=kvt[0:n, 512 * cg:512 * cg + 512], in_=bk[0:n, :]), [bk], [kvt_d])
            if KS >= 2:
                P.op("dve", lambda bk=bk, cg=cg, n=n: V.tensor_copy(out=kvb[0:n, 512 * cg:512 * cg + 512], in_=bk[0:n, :]), [bk], [kvb_d])
        if KD < 3:
            continue
        if own:
            r0 = 128 * (a - 8) if a < 16 else 1024
            cx.dma("sp", kv_o[r0:r0 + n, :], kvt[0:n, 0:1024], owner=kvt_d, reads=[kvt_d], is_output=True)
        if 12 <= a < 16:
            cx.dma("sp", winp_o[128 * (a - 12):128 * (a - 12) + 128, :], kvt[:, 1024:1536], owner=kvt_d, reads=[kvt_d],
                   is_output=True, group=True)
        if a == 16:
            cx.dma("sp", kvs_scr[:, :], kvt[0:64, :], owner=kvsd, reads=[kvt_d], writes=[kvsd])
            cx.dma("sp", dap(wins_o, 508 * 512, [[512 * 512, NSB], [512, 4], [1, 512]]),
                   dap(kvs_scr, 1024, [[4 * 1536, NSB], [1536, 4], [1, 512]]), owner=P.ddep["wins"], reads=[kvsd], is_output=True)
        if a < 16:
            P.op("pool", lambda a=a: G.tensor_copy(out=vsel[:, a, :], in_=kvb[:, 768:1024]), [kvb_d], [vsel])
            P.op("pool", lambda a=a: G.tensor_copy(out=vwin[:, a, :], in_=kvb[:, 1280:1536]), [kvb_d], [vwin])
        if KD < 4:
            continue
        bv = bfv(B[7])
        for si, s_ in enumerate((0, 1, 2, 4)):
            for gp in range(2):
                i = 2 * si + gp
                P.tr(bv[:, 128 * i:128 * i + n], kvb[0:n, 256 * s_ + 128 * gp:256 * s_ + 128 * gp + 128], ident_b[0:n, 0:n],
                     [kvb_d, ident_b], [B[7]])
        def pview(si, n=n, bv=bv):
            return bv[:, 256 * si:256 * si + 256].rearrange("p (a b) -> p a b", a=2)[:, :, 0:n]
        if a < 16:
            P.op("dve", lambda a=a: V.tensor_tensor(out=kcmpT[:, :, 128 * a:128 * a + 128], in0=pview(0),
                                                    in1=bcmid(peT[:, 0, :], 2), op=ALU.add), [B[7], peT], [kcmp_d])
            P.op("dve", lambda a=a: V.tensor_tensor(out=vcmpT[:, :, 128 * a:128 * a + 128], in0=pview(1),
                                                    in1=bcmid(peT[:, 1, :], 2), op=ALU.add), [B[7], peT], [vcmp_d])
            P.op("act", lambda a=a: A.copy(out=kselT[:, :, 128 * a:128 * a + 128], in_=pview(2)), [B[7]], [kselT])
            P.op("act", lambda a=a: A.copy(out=kwinT[:, :, 128 * a:128 * a + 128], in_=pview(3)), [B[7]], [kwinT])
        else:
            P.op("act", lambda: A.copy(out=ksT_new[:, :, :], in_=pview(2)), [B[7]], [ksT_new])
            P.op("act", lambda: A.copy(out=kwT_new[:, :, :], in_=pview(3)), [B[7]], [kwT_new])

    if stage <= 7:
        return P, locals()
    P.op("pool", lambda: G.memset(small[0:1, 61:62], 0.0), [], kvb_d2 + uTk_d2 + [slabs[0].d, slabs[1].d, small.d])
    hidT = tmpB[0].t[0:64, 0:256].rearrange("p (a b) -> p a b", a=2)

    def compress(srcT, src_d, w1sb, is_k, out_t, nblk=64):
        for gpar in range(2):
            hp = B[2][0:64, 0:2 * nblk]
            for l in range(32):
                rhs = srcT[64 * gpar:64 * gpar + 64, :, l:32 * nblk:32]
                P.mm(hp, w1sb[64 * gpar:64 * gpar + 64, l, :], rhs, l == 0, l == 31, [src_d, w1sb], [B[2]])
            P.op("act", lambda gpar=gpar, hp=hp: A.activation(
                out=hidT[:, gpar, 0:2 * nblk], in_=hp, func=AF.Gelu_apprx_tanh), [B[2]], [tmpB[0]])
        for g in range(4):
            gpar, gp = g % 2, g // 2
            hsl = hidT[:, gpar, nblk * gp:nblk * gp + nblk]
            if is_k:
                P.mm(B[3][:, 64 * g:64 * g + nblk], w2k_sb[0:64, :], hsl, True, True, [tmpB[0], w2k_sb], [B[3]])
            else:
                P.mm(B[3][0:nblk, 64 * g:64 * g + 64], hsl, w2v_sb[0:64, :], True, True, [tmpB[0], w2v_sb], [B[3]])
        if is_k:
            P.op("dve", lambda: V.tensor_copy(out=out_t[:, :, 0:nblk],
                                              in_=B[3][:, 0:256].rearrange("p (a b) -> p a b", a=4)[:, :, 0:nblk]), [B[3]], [out_t])
        else:
            P.op("dve", lambda: V.tensor_copy(out=out_t[0:nblk, :, :],
                                              in_=B[3][0:nblk, 0:256].rearrange("p (a b) -> p a b", a=4)), [B[3]], [out_t])

    compress(kcmpT, kcmp_d, w1k_sb, True, kcT)
    compress(vcmpT, vcmp_d, w1v_sb, False, vc)

    if stage <= 8:
        return P, locals()
    dA, dO, dQ, dM = Dep("rA"), Dep("rO"), Dep("rQ"), Dep("rM")
    RAall = [dA, dO, dQ, dM]
    gates = P.sb("gates", [128, 5, 48], F32)
    ghist = P.sb("ghist", [128, 44, 2], F32)
    shist = P.sb("shist", [128, 44, 32], BF16)
    oS = P.sb("oS", [128, 8, 64], BF16)
    poolst = P.sb("poolst", [128, 8, 15], F32)

    def splits(N):
        if N <= 512:
            return [(0, N)]
        h = (N + 1) // 2
        return [(0, h), (h, N - h)]

    def rmsnorm_stats(src, n):
        P.op("act", lambda: A.activation(out=ub[0:n, :], in_=src[0:n, :], func=AF.Square, accum_out=small[0:n, 0:1]),
             [src], [ub, small])
        P.op("dve", lambda: V.tensor_scalar(out=small[0:n, 1:2], in0=small[0:n, 0:1], scalar1=1.0 / D, scalar2=EPS,
                                            op0=ALU.mult, op1=ALU.add), [small], [small])
        P.op("act", lambda: A.activation(out=small[0:n, 3:4], in_=small[0:n, 1:2], func=AF.Sqrt), [small], [small])
        P.op("dve", lambda: V.reciprocal(out=small[0:n, 2:3], in_=small[0:n, 3:4]), [small], [small])

    def transpose_to(src_bf, n, dst3, c0, gcol, dstdep):
        for half in range(2):
            bk = B[half]
            bv = bfv(bk)
            for i in range(8):
                kc = 8 * half + i
                P.tr(bv[:, 128 * i:128 * i + n], src_bf[0:n, 128 * kc:128 * kc + 128], ident_b[0:n, 0:n], [ub, ident_b], [bk])
            P.op("dve", lambda bv=bv, half=half: V.tensor_tensor(
                out=dst3[:, 8 * half:8 * half + 8, c0:c0 + n],
                in0=bv[:, :].rearrange("p (a b) -> p a b", a=8)[:, :, 0:n],
                in1=bc(gcol[:, 8 * half:8 * half + 8], n), op=ALU.mult), [bk, gcol], [dstdep])

    slab_i = [0]

    def next_slab():
        slab_i[0] ^= 1
        return slabs[slab_i[0]]

    def fm_linear(sl, wv, nk, rhs3, rdeps, c0, N, nchunks, evac):
        for oc in range(nchunks):
            for si, (s0, sn) in enumerate(splits(N)):
                bk = B[fm_linear.rot % 8]
                fm_linear.rot += 1
                for kc in range(nk):
                    P.mm(bk[:, 0:sn], wv[:, kc, 128 * oc:128 * oc + 128], rhs3[:, kc, c0 + s0:c0 + s0 + sn],
                         kc == 0, kc == nk - 1, [sl] + list(rdeps), [bk])
                evac(oc, s0, sn, bk[:, 0:sn], bk)
    fm_linear.rot = 0

    def run_pass(pi):
        if pi == 0:
            units = [("h", 1022, 2, 15)] + [("p", 1024 + 128 * j, 128, 17 + 128 * j) for j in range(4)]
            NU, c0, NC, NCp = 529, 15, 514, 514
            yrow0 = -2
        else:
            units = [("p", 1536 + 128 * j, 128, 128 * j) for j in range(4)] + [("s", 0, 64, 512)]
            NU, c0, NC, NCp = 576, 0, 576, 512
            yrow0 = 512
        uT = RU.t[:, 0:16 * NU].rearrange("p (a b) -> p a b", a=16)
        dU = RU.d
        dst3_dep[0] = dU
        if pi == 0:
            norm_T(xb[1007:1022, :], 15, uT, gcolA, B[0], B[1])
        for (kind, r0, n, cc) in units:
            src = xs[:, :] if kind == "s" else xb[r0:r0 + n, :]
            norm_T_at(src, n, uT, cc)
        if pi == 0:
            NPI = 529
            PI = RA.t[:, 13824:13824 + 2 * 8 * NPI].bitcast(F32).rearrange("p (a b) -> p a b", a=8)
            PIs = None
        else:
            NPI = 527 + 304
            PI = RA.t[:, 13824:13824 + 2 * 8 * NPI].bitcast(F32).rearrange("p (a b) -> p a b", a=8)
            PIs = PI[:, :, 527:831].rearrange("p a (b t) -> p a b t", b=16)
            P.op("dve", lambda: V.tensor_copy(out=PI[:, :, 0:15], in_=poolst[:, :, :]), [poolst], [dM])
            for hb in range(2):
                P.load("sp", xt, xt[0:120, 0:1024], spool[120 * hb:120 * hb + 120, :])
                for oc in range(8):
                    P.tr(B[2][:, 128 * (oc % 4):128 * (oc % 4) + 120], xt[0:120, 128 * oc:128 * oc + 128], ident_f[0:120, 0:120],
                         [xt, ident_f], [B[2]])
                    if oc % 4 == 3:
                        o4 = oc - 3
                        P.op("dve", lambda hb=hb, o4=o4: V.tensor_copy(
                            out=PIs[:, o4:o4 + 4, 8 * hb:8 * hb + 8, 0:15],
                            in_=B[2][:, :].rearrange("p (a b) -> p a b", a=4)[:, :, 0:120].rearrange("p a (b t) -> p a b t", b=8)),
                            [B[2]], [dM])
        pooled = RA.t[:, 9216:9216 + 8 * NC].rearrange("p (a b) -> p a b", a=8)
        A_T = RA.t[:, 0:8 * NC].rearrange("p (a b) -> p a b", a=8)
        o_T = RA.t[:, 4608:4608 + 8 * NC].rearrange("p (a b) -> p a b", a=8)
        q_T = RA.t[:, 9216:9216 + 8 * NC].rearrange("p (a b) -> p a b", a=8)
        mrg = RA.t[:, 13824:13824 + 16 * NC].rearrange("p (a b) -> p a b", a=16)

        def ev_pool(base_oc):
            def f(oc, s0, sn, ps, bk):
                o = base_oc + oc
                if pi == 0:
                    P.op("act", lambda: A.copy(out=PI[:, o, s0:s0 + sn], in_=ps), [bk], [dM])
                else:
                    pe_ = min(s0 + sn, 512)
                    if s0 < 512:
                        P.op("act", lambda: A.copy(out=PI[:, o, 15 + s0:15 + pe_], in_=ps[:, 0:pe_ - s0]), [bk], [dM])
                    if s0 + sn > 512:
                        a0 = max(s0, 512)
                        P.op("act", lambda: A.copy(out=PIs[:, o, (a0 - 512) // 4:16, 15:19],
                                                   in_=ps[:, a0 - s0:sn].rearrange("p (b t) -> p b t", t=4)), [bk], [dM])
            return f

        Nlin = NU if pi == 0 else NC
        for sidx in range(4):
            sl = next_slab()
            wv = wslab_load(sl, w_in, 0, 16, 256 * sidx, 256, IN_W)
            fm_linear(sl, wv, 16, uT, [dU], 0, Nlin, 2, ev_pool(2 * sidx))
        if pi == 0:
            P.op("dve", lambda: V.tensor_copy(out=poolst[:, :, :], in_=PI[:, :, NPI - 15:NPI]), [dM], [poolst])
        else:
            stg = xt2
            for oc in range(8):
                P.tr(B[3][0:15, 128 * (oc % 4):128 * (oc % 4) + 128], PI[:, oc, 512:527], ident_f[:, :], [dM, ident_f], [B[3]])
                if oc % 4 == 3:
                    P.op("dve", lambda oc=oc: V.tensor_copy(out=stg[0:15, 128 * (oc - 3):128 * (oc - 3) + 512], in_=B[3][0:15, :]), [B[3]], [stg])
            P.store("sp", poolp_o[:, :], stg, stg[0:15, 0:1024])
            for oc in range(8):
                P.op("dve", lambda oc=oc: V.tensor_copy(out=tmpF[0][:, 0:64].rearrange("p (b t) -> p b t", t=4), in_=PIs[:, oc, :, 15:19]), [dM], [tmpF[0]])
                P.tr(B[3][0:64, 128 * (oc % 4):128 * (oc % 4) + 128], tmpF[0][:, 0:64], ident_f[:, :], [tmpF[0], ident_f], [B[3]])
                if oc % 4 == 3:
                    P.op("dve", lambda oc=oc: V.tensor_copy(out=stg[0:64, 1024 + 128 * (oc - 3):1024 + 128 * (oc - 3) + 512], in_=B[3][0:64, :]), [B[3]], [stg])
            for b_ in range(NSB):
                P.store("sp", pools_o[15 * b_ + 11:15 * b_ + 15, :], stg, stg[4 * b_:4 * b_ + 4, 1024:2048], group=(b_ > 0))
        t1 = RA.t[:, 0:2 * 2 * NPI].bitcast(F32).rearrange("p (a b) -> p a b", a=2)
        t2 = RA.t[:, 4608:4608 + 2 * 2 * NPI].bitcast(F32).rearrange("p (a b) -> p a b", a=2)
        segs = [(0, 15 + NCp, 0, NCp)]
        if pi == 0:
            segs = [(0, NPI, 0, NC)]
        for gi in range(4):
            w = 2 << gi
            P.load("sp", xt, xt[:, 0:NCp], dap(t_rc, gi * 1026 + 514 * pi, [[0, 128], [1, NCp]]))
            X = PI[:, 2 * gi:2 * gi + 2, :]
            cur, curd = X, dM
            for k in range(gi + 1):
                sh = 1 << k
                nxt, nxtd = (t1, dA) if k % 2 == 0 else (t2, dO)
                def add_seg(lo, L):
                    P.op("dve", lambda lo=lo, L=L, cur=cur, nxt=nxt, sh=sh: V.tensor_tensor(
                        out=nxt[:, :, lo + sh:lo + L], in0=cur[:, :, lo + sh:lo + L], in1=cur[:, :, lo:lo + L - sh], op=ALU.add),
                        [curd], [nxtd])
                add_seg(0, segs[0][1])
                if pi == 1:
                    P.op("dve", lambda cur=cur, nxt=nxt, sh=sh: V.tensor_tensor(
                        out=nxt[:, :, 527:831].rearrange("p a (b t) -> p a b t", b=16)[:, :, :, sh:19],
                        in0=cur[:, :, 527:831].rearrange("p a (b t) -> p a b t", b=16)[:, :, :, sh:19],
                        in1=cur[:, :, 527:831].rearrange("p a (b t) -> p a b t", b=16)[:, :, :, 0:19 - sh], op=ALU.add),
                        [curd], [nxtd])
                cur, curd = nxt, nxtd
            st, L, po, nt = segs[0]
            other, otherd = (t2, dO) if cur is t1 else (t1, dA)
            P.op("dve", lambda cur=cur, other=other, L=L, nt=nt: V.tensor_tensor(
                out=other[:, :, L - nt:L], in0=cur[:, :, L - nt:L], in1=bcmid(xt[:, 0:nt], 2), op=ALU.mult),
                [curd, xt], [otherd])
            P.op("dve", lambda other=other, X=X, L=L, nt=nt, gi=gi: V.tensor_tensor(
                out=pooled[:, 2 * gi:2 * gi + 2, 0:nt], in0=other[:, :, L - nt:L], in1=X[:, :, L - nt:L], op=ALU.subtract),
                [otherd, dM], [dQ])
            if pi == 1:
                for a_ in range(2):
                    P.op("dve", lambda cur=cur, X=X, gi=gi, w=w, a_=a_: V.scalar_tensor_tensor(
                        out=pooled[:, 2 * gi + a_, 512:576].rearrange("p (b t) -> p b t", b=16),
                        in0=cur[:, a_, 527:831].rearrange("p (b t) -> p b t", b=16)[:, :, 15:19], scalar=1.0 / w,
                        in1=X[:, a_, 527:831].rearrange("p (b t) -> p b t", b=16)[:, :, 15:19],
                        op0=ALU.mult, op1=ALU.subtract), [curd, dM], [dQ])
        for gi in range(4):
            for j in range(2):
                for (s0, sn) in splits(NC):
                    bk = B[fm_linear.rot % 8]; fm_linear.rot += 1
                    for k2 in range(2):
                        P.mm(bk[:, 0:sn], wgrp_sb[:, 2 * gi + k2, 128 * j:128 * j + 128], pooled[:, 2 * gi + k2, s0:s0 + sn],
                             k2 == 0, k2 == 1, [wgrp_sb, dQ], [bk])
                    P.op("dve", lambda bk=bk, gi=gi, j=j, s0=s0, sn=sn: V.tensor_scalar(
                        out=A_T[:, 2 * gi + j, s0:s0 + sn], in0=bk[:, 0:sn], scalar1=pscol[:, 2 * gi + j:2 * gi + j + 1],
                        scalar2=None, op0=ALU.mult), [bk, pscol], [dA])
        return units, uT, dU, NU, c0, NC, NCp, A_T, o_T, q_T, mrg, yrow0

    def norm_T_at(src, n, uT, cc):
        norm_T(src, n, uT[:, :, cc:], gcolA, B[0], B[1])

    pass_state = run_pass(0)

    def pass_qg(pi, st):
        units, uT, dU, NU, c0, NC, NCp, A_T, o_T, q_T, mrg, yrow0 = st
        for gp in range(2):
            for ip in range(2):
                sl = next_slab()
                wv = sl.t[:, 0:4096].rearrange("p (a b) -> p a b", a=16)
                for ii in range(2):
                    i = 2 * ip + ii
                    for hh in range(2):
                        P.load("pool", sl, wv[:, :, 128 * ii + 64 * hh:128 * ii + 64 * hh + 64],
                               dap(w_in, 1024 + (8 * gp + i) * 64 + 256 * hh, [[IN_W, 128], [128 * IN_W, 16], [1, 64]]), group=(ii + hh > 0))
                def ev_q(oc, s0, sn, ps, bk, gp=gp, ip=ip):
                    ch = 4 * gp + 2 * ip + oc
                    P.op("act", lambda: A.copy(out=q_T[:, ch, s0:s0 + sn], in_=ps), [bk], [dQ])
                fm_linear(sl, wv, 16, uT, [dU], c0, NC, 2, ev_q)
        sl = next_slab()
        wv = wslab_load(sl, w_in, 0, 16, 3584, 48, IN_W)
        for ui, (kind, r0, n, cc) in enumerate(units):
            bk = B[fm_linear.rot % 8]; fm_linear.rot += 1
            for kc in range(16):
                P.mm(bk[0:n, 0:48], uT[:, kc, cc:cc + n], wv[:, kc, :], kc == 0, kc == 15, [dU, sl], [bk])
            P.op("act", lambda bk=bk, ui=ui, n=n: A.activation(out=gates[0:n, ui, :], in_=bk[0:n, 0:48], func=AF.Sigmoid), [bk], [gates])

    def pass_merge_ffn(pi, st):
        units, uT, dU, NU, c0, NC, NCp, A_T, o_T, q_T, mrg, yrow0 = st
        for j in range(16):
            s1 = next_slab()
            wg = s1.t[:, 0:4096].rearrange("p (a b) -> p a b", a=16)
            P.load("pool", s1, wg[:, :, 0:128], dap(w_in, 3632 + 128 * j, [[IN_W, 128], [128 * IN_W, 16], [1, 128]]))
            P.load("pool", s1, wg[:, :, 128:256], dap(w_in, 3632 + 2048 + 128 * j, [[IN_W, 128], [128 * IN_W, 16], [1, 128]]), group=True)
            s2 = next_slab()
            wp = s2.t[:, 0:2048].rearrange("p (a b) -> p a b", a=8)
            P.load("pool", s2, wp[:, :, 0:128], dap(w_pp, 128 * j, [[D, 128], [128 * D, 8], [1, 128]]))
            P.load("pool", s2, wp[:, :, 128:256], dap(w_np, 128 * j, [[D, 128], [128 * D, 8], [1, 128]]), group=True)
            for si, (s0, sn) in enumerate(splits(NC)):
                bga, bgb, bp1, bp2 = B[4 * si], B[4 * si + 1], B[4 * si + 2], B[4 * si + 3]
                for kc in range(16):
                    P.mm(bga[:, 0:sn], wg[:, kc, 0:128], uT[:, kc, c0 + s0:c0 + s0 + sn], kc == 0, kc == 15, [s1, dU], [bga])
                for kc in range(16):
                    P.mm(bgb[:, 0:sn], wg[:, kc, 128:256], uT[:, kc, c0 + s0:c0 + s0 + sn], kc == 0, kc == 15, [s1, dU], [bgb])
                for kc in range(8):
                    P.mm(bp1[:, 0:sn], wp[:, kc, 0:128], A_T[:, kc, s0:s0 + sn], kc == 0, kc == 7, [s2, dA], [bp1])
                for kc in range(8):
                    P.mm(bp2[:, 0:sn], wp[:, kc, 128:256], o_T[:, kc, s0:s0 + sn], kc == 0, kc == 7, [s2, dO], [bp2])
                P.op("act", lambda: A.activation(out=tmpF[0][:, 0:sn], in_=bga[:, 0:sn], func=AF.Sigmoid), [bga], [tmpF[0]])
                P.op("act", lambda: A.activation(out=tmpF[1][:, 0:sn], in_=bgb[:, 0:sn], func=AF.Sigmoid), [bgb], [tmpF[1]])
                P.op("dve", lambda: V.tensor_tensor(out=tmpF[0][:, 0:sn], in0=bp1[:, 0:sn], in1=tmpF[0][:, 0:sn], op=ALU.mult),
                     [bp1, tmpF[0]], [tmpF[0]])
                P.op("dve", lambda: V.tensor_tensor(out=tmpF[1][:, 0:sn], in0=bp2[:, 0:sn], in1=tmpF[1][:, 0:sn], op=ALU.mult),
                     [bp2, tmpF[1]], [tmpF[1]])
                P.op("dve", lambda: V.tensor_tensor(out=mrg[:, j, s0:s0 + sn], in0=tmpF[0][:, 0:sn], in1=tmpF[1][:, 0:sn], op=ALU.add),
                     [tmpF[0], tmpF[1]], [dM])
        ysd = P.ddep["y_scr"]; hsd = P.ddep["h_scr"]
        P.load("sp", grep, grep[:], dap(g_qm, 0, [[0, 128], [1, D]]))
        for s_ in range(8):
            sl = next_slab()
            wv = wslab_load(sl, w_out, 0, 16, 256 * s_, 256, D)
            for ui, (kind, r0, n, cc) in enumerate(units):
                bk = B[fm_linear.rot % 8]; fm_linear.rot += 1
                for kc in range(16):
                    P.mm(bk[0:n, 0:256], mrg[:, kc, cc - c0:cc - c0 + n], wv[:, kc, :], kc == 0, kc == 15, [dM, sl], [bk])
                tf = tmpF[ui % 2]
                P.op("act", lambda bk=bk, tf=tf, n=n: A.copy(out=tf[0:n, 0:256], in_=bk[0:n, 0:256]), [bk], [tf])
                cx.dma("sp", y_scr[cc:cc + n, 256 * s_:256 * s_ + 256], tf[0:n, 0:256], owner=tf.d, reads=[tf.d], writes=[ysd])
        nT = uT
        for ui, (kind, r0, n, cc) in enumerate(units):
            P.load("sp", xt, xt[0:n, :], y_scr[cc:cc + n, :], reads=[ysd])
            P.load("sp", xt2, xt2[0:n, :], xs[:, :] if kind == "s" else xb[r0:r0 + n, :])
            rmsnorm_stats(xt, n)
            P.op("dve", lambda n=n: V.scalar_tensor_tensor(out=xt[0:n, :], in0=xt[0:n, :], scalar=small[0:n, 2:3], in1=grep[0:n, :],
                                                          op0=ALU.mult, op1=ALU.mult), [xt, small, grep], [xt])
            P.op("dve", lambda n=n: V.tensor_tensor(out=xt2[0:n, :], in0=xt2[0:n, :], in1=xt[0:n, :], op=ALU.add), [xt, xt2], [xt2])
            cx.dma("sp", h_scr[cc:cc + n, :], xt2[0:n, :], owner=xt2.d, reads=[xt2.d], writes=[hsd])
            rmsnorm_stats(xt2, n)
            P.op("dve", lambda n=n: V.tensor_scalar(out=ub[0:n, :], in0=xt2[0:n, :], scalar1=small[0:n, 2:3], scalar2=None,
                                                    op0=ALU.mult), [xt2, small], [ub])
            transpose_to(ub, n, nT, cc, gcolB, dU)
        actT = RA.t[:, 0:44 * NC].rearrange("p (a b) -> p a b", a=44)
        gbuf = xt.t[:, 0:640]
        gb_d = xt.d
        if pi == 1:
            P.load("sp", xt2, xt2[0:32, :], sconv[:, 0:2048])
            for q4 in range(11):
                if q4 == 4:
                    P.load("sp", xt2, xt2[0:32, :], sconv[:, 2048:4096])
                if q4 == 8:
                    P.load("sp", xt2, xt2[0:32, 0:1536], sconv[:, 4096:5632])
                for i in range(4):
                    fc = 4 * q4 + i
                    lc = fc * 128 - (0 if q4 < 4 else 2048 if q4 < 8 else 4096)
                    P.tr(B[3][:, 32 * i:32 * i + 32], xt2[0:32, lc:lc + 128], ident_f[0:32, 0:32], [xt2, ident_f], [B[3]])
                P.op("dve", lambda q4=q4: V.tensor_copy(out=shist[:, 4 * q4:4 * q4 + 4, :],
                                                        in_=B[3][:, 0:128].rearrange("p (a b) -> p a b", a=4)), [B[3]], [shist])
        for fc in range(44):
            sl = next_slab()
            wv = sl.t[:, 0:4096].rearrange("p (a b) -> p a b", a=16)
            P.load("pool", sl, wv[:, :, 0:128], dap(w_up, 128 * fc, [[2 * DFF, 128], [128 * 2 * DFF, 16], [1, 128]]))
            P.load("pool", sl, wv[:, :, 128:256], dap(w_up, DFF + 128 * fc, [[2 * DFF, 128], [128 * 2 * DFF, 16], [1, 128]]), group=True)
            spl = splits(NC)
            vb = []
            for si, (s0, sn) in enumerate(spl):
                bv_, bg_ = B[4 * (fc % 2) + 2 * si], B[4 * (fc % 2) + 2 * si + 1]
                for kc in range(16):
                    P.mm(bv_[:, 0:sn], wv[:, kc, 0:128], nT[:, kc, c0 + s0:c0 + s0 + sn], kc == 0, kc == 15, [sl, dU], [bv_])
                for kc in range(16):
                    P.mm(bg_[:, 0:sn], wv[:, kc, 128:256], nT[:, kc, c0 + s0:c0 + s0 + sn], kc == 0, kc == 15, [sl, dU], [bg_])
                vb.append((bv_, bg_, s0, sn))
                if pi == 0:
                    P.op("act", lambda bg_=bg_, s0=s0, sn=sn: A.copy(out=gbuf[:, s0:s0 + sn], in_=bg_[:, 0:sn]), [bg_], [gb_d])
                else:
                    pe_ = min(s0 + sn, 512)
                    if s0 < 512:
                        P.op("act", lambda bg_=bg_, s0=s0, pe_=pe_: A.copy(out=gbuf[:, 2 + s0:2 + pe_], in_=bg_[:, 0:pe_ - s0]), [bg_], [gb_d])
                    if s0 + sn > 512:
                        a0 = max(s0, 512)
                        P.op("act", lambda bg_=bg_, s0=s0, sn=sn, a0=a0: A.copy(
                            out=gbuf[:, 514:610].rearrange("p (b t) -> p b t", t=6)[:, (a0 - 512) // 4:16, 2:6],
                            in_=bg_[:, a0 - s0:sn].rearrange("p (b t) -> p b t", t=4)), [bg_], [gb_d])
            if pi == 1:
                P.op("act", lambda fc=fc: A.copy(out=gbuf[:, 0:2], in_=ghist[:, fc, :]), [ghist], [gb_d])
                P.op("act", lambda fc=fc: A.copy(out=gbuf[:, 514:610].rearrange("p (b t) -> p b t", t=6)[:, :, 0:2],
                                                 in_=shist[:, fc, :].rearrange("p (b t) -> p b t", t=2)), [shist], [gb_d])
            else:
                P.op("act", lambda fc=fc: A.copy(out=ghist[:, fc, :], in_=gbuf[:, 512:514]), [gb_d], [ghist])
            cv = xt.t[:, 640:640 + NC]
            def conv_seg(out_ap, g0, g1, g2):
                P.op("dve", lambda: V.tensor_scalar(out=out_ap, in0=g2, scalar1=cwcol[:, fc, 2:3], scalar2=cwcol[:, fc, 3:4],
                                                    op0=ALU.mult, op1=ALU.add), [gb_d, cwcol], [gb_d])
                P.op("dve", lambda: V.scalar_tensor_tensor(out=out_ap, in0=g1, scalar=cwcol[:, fc, 1:2], in1=out_ap,
                                                           op0=ALU.mult, op1=ALU.add), [gb_d, cwcol], [gb_d])
                P.op("dve", lambda: V.scalar_tensor_tensor(out=out_ap, in0=g0, scalar=cwcol[:, fc, 0:1], in1=out_ap,
                                                           op0=ALU.mult, op1=ALU.add), [gb_d, cwcol], [gb_d])
            if pi == 0:
                conv_seg(cv[:, 2:514], gbuf[:, 0:512], gbuf[:, 1:513], gbuf[:, 2:514])
                P.op("dve", lambda: V.memset(cv[:, 0:2], 0.0), [], [gb_d])
            else:
                conv_seg(cv[:, 0:512], gbuf[:, 0:512], gbuf[:, 1:513], gbuf[:, 2:514])
                g6 = gbuf[:, 514:610].rearrange("p (b t) -> p b t", t=6)
                conv_seg(cv[:, 512:576].rearrange("p (b t) -> p b t", t=4), g6[:, :, 0:4], g6[:, :, 1:5], g6[:, :, 2:6])
            P.op("act", lambda: A.activation(out=cv, in_=cv, func=AF.Gelu_apprx_tanh), [gb_d], [gb_d])
            for (bv_, bg_, s0, sn) in vb:
                P.op("dve", lambda bv_=bv_, s0=s0, sn=sn: V.tensor_tensor(out=actT[:, fc, s0:s0 + sn], in0=bv_[:, 0:sn],
                                                                        in1=cv[:, s0:s0 + sn], op=ALU.mult), [bv_, gb_d], RAall)
            if pi == 1:
                P.tr(B[7][0:2, 0:128], gbuf[:, 512:514], ident_f[:, :], [gb_d, ident_f], [B[7]])
                P.op("act", lambda: A.copy(out=tmpF[1][:, 0:32].rearrange("p (b t) -> p b t", t=2),
                                           in_=gbuf[:, 514:610].rearrange("p (b t) -> p b t", t=6)[:, :, 4:6]), [gb_d], [tmpF[1]])
                P.tr(B[7][0:32, 128:256], tmpF[1][:, 0:32], ident_f[:, :], [tmpF[1], ident_f], [B[7]])
                P.op("dve", lambda: V.tensor_copy(out=tmpF[0][0:32, 0:256], in_=B[7][0:32, 0:256]),
                     [B[7]], [tmpF[0]])
                P.store("sp", convp_o[:, 128 * fc:128 * fc + 128], tmpF[0], tmpF[0][0:2, 0:128])
                P.store("sp", convs_o[:, 128 * fc:128 * fc + 128], tmpF[0], tmpF[0][0:32, 128:256], group=True)
        for cg in range(8):
            for kh in range(2):
                sl = next_slab()
                wv = sl.t[:, 0:5632].rearrange("p (a b) -> p a b", a=22)
                P.load("pool", sl, wv, dap(w_down, (22 * kh * 128) * D + 256 * cg, [[D, 128], [128 * D, 22], [1, 256]]))
                for ui, (kind, r0, n, cc) in enumerate(units):
                    bk = B[ui]
                    for k in range(22):
                        P.mm(bk[0:n, 0:256], actT[:, 22 * kh + k, cc - c0:cc - c0 + n], wv[:, k, :], kh == 0 and k == 0, kh == 1 and k == 21,
                             RAall + [sl], [bk])
            for ui, (kind, r0, n, cc) in enumerate(units):
                tf = tmpF[ui % 2]
                P.op("act", lambda ui=ui, tf=tf, n=n: A.copy(out=tf[0:n, 0:256], in_=B[ui][0:n, 0:256]), [B[ui]], [tf])
                cx.dma("sp", y_scr[cc:cc + n, 256 * cg:256 * cg + 256], tf[0:n, 0:256], owner=tf.d, reads=[tf.d], writes=[ysd])
        P.load("sp", grep, grep[:], dap(g_qf, 0, [[0, 128], [1, D]]))
        for ui, (kind, r0, n, cc) in enumerate(units):
            if kind == "h":
                continue
            P.load("sp", xt, xt[0:n, :], y_scr[cc:cc + n, :], reads=[ysd])
            P.load("sp", xt2, xt2[0:n, :], h_scr[cc:cc + n, :], reads=[hsd])
            rmsnorm_stats(xt, n)
            P.op("dve", lambda n=n: V.scalar_tensor_tensor(out=xt[0:n, :], in0=xt[0:n, :], scalar=small[0:n, 2:3], in1=grep[0:n, :],
                                                          op0=ALU.mult, op1=ALU.mult), [xt, small, grep], [xt])
            P.op("dve", lambda n=n: V.tensor_tensor(out=xt2[0:n, :], in0=xt2[0:n, :], in1=xt[0:n, :], op=ALU.add), [xt, xt2], [xt2])
            orow = (cc - c0 + yrow0) if kind == "p" else 1024
            P.store("sp", y_o[orow:orow + n, :], xt2, xt2[0:n, :])


    dens = P.sb("dens", [128, 12, 8], F32)
    den12 = P.sb("den12", [128, 12], F32)
    sc12 = P.sb("sc12", [128, 12], F32)
    selb = P.sb("selb", [128, 40], F32)
    score = P.sb("score", [128, 40], F32)
    impt = P.sb("impt", [128, 64], F32)
    bonus_sb = P.sb("bonus_sb", [128, 32], F32)
    Ct = P.sb("Ct", [128, 16, 64], BF16)
    kbcrep = P.sb("kbcrep", [128, 64], F32)
    msame = P.sb("msame", [16, 16], F32)
    G16 = P.sb("G16", [16, 12], F32); selB = P.sb("selB", [64, 256], F32); Mi = P.sb("Mi", [16, 4], F32)
    kcTb = P.sb("kcTb", [128, 4, 64], BF16); vcb = P.sb("vcb", [64, 4, 64], BF16)
    idx_t = P.sb("idx_t", [128, 256], I32); iot_i = P.sb("iot_i", [128, 1], I32)
    P.load("sp", kbcrep, kbcrep[:], dap(t_kbc, 0, [[0, 128], [1, 64]]))
    P.load("sp", msame, msame[:], t_msame[:, :])
    P.load("sp", selB, selB[:], t_selb[:, :])
    P.load("sp", Mi, Mi[:], t_mi[:, :])
    att_rot = [0]

    tF3 = View(xt.t[:, 0:512], Dep("tF3"))
    tB3 = View(xt.t[:, 1024:1280].bitcast(BF16), Dep("tB3"))
    SBK = [B[0], B[1], B[6]]
    TFS = [tmpF[0], tmpF[1], tF3]
    TBS = [tmpB[0], tmpB[1], tB3]
    xdeps = [xt.d, tF3.d, tB3.d]

    def branch(qh, qdeps, n, groups, o_dst, o_bank, dcol, first_last=(True, True)):
        ng_ = len(groups)
        items = []
        for gi, gr in enumerate(groups):
            def stA(ka, gr=gr, gi=gi):
                Sb, tF, tB = SBK[ka], TFS[ka], TBS[ka]
                nk = gr["nk"]
                P.mm(Sb[0:n, 0:nk], qh, gr["kT"], True, True, list(qdeps) + list(gr["kdeps"]), [Sb])
                gr["bias"](Sb, tF, n, nk)
                P.op("act", lambda: A.activation(out=tB[0:n, 0:nk], in_=tF[0:n, 0:nk], func=AF.Exp,
                                                 accum_out=dens[0:n, dcol, gi:gi + 1]), [tF], [tB, dens])
            def stB1(ka, k, gr=gr, gi=gi):
                tB, pT, PTb = TBS[ka], PT[k], B[2 + k]
                pv = bfv(PTb)
                koff = 0
                for ci, (vap, nkc, vdeps) in enumerate(gr["v"]):
                    P.tr(pv[0:nkc, 128 * ci:128 * ci + n], tB[0:n, koff:koff + nkc], ident_b[0:n, 0:n], [tB, ident_b], [PTb])
                    koff += nkc
                nch = len(gr["v"])
                P.op("act", lambda: A.copy(out=pT[:, 0:128 * nch], in_=pv[:, 0:128 * nch]), [PTb], [pT])
            def stB2(ka, k, gr=gr, gi=gi):
                pT = PT[k]
                nch = len(gr["v"])
                for ci, (vap, nkc, vdeps) in enumerate(gr["v"]):
                    P.mm(o_dst, pT[0:nkc, 128 * ci:128 * ci + n], vap, gi == 0 and ci == 0, gi == ng_ - 1 and ci == nch - 1,
                         [pT] + list(vdeps), [o_bank])
            items.append((stA, stB1, stB2))
        return items

    def run_items(items):
        base, ni = att_rot[0], len(items)
        for i in range(-2, ni):
            if i >= 0:
                items[i][1]((base + i) % 3, (base + i) % 2)
            if i + 2 < ni:
                items[i + 2][0]((base + i + 2) % 3)
            if i >= 0:
                items[i][2]((base + i) % 3, (base + i) % 2)
        att_rot[0] = base + ni

    def rank_select(n, nb, dead_ap, out_ap=None, out_dep=None, sc_ap=None, sc_dep=None):
        sc = score[0:n, 0:nb] if sc_ap is None else sc_ap
        scd = score if sc_dep is None else sc_dep
        rkv = rk.t[0:n, 0:nb * nb].rearrange("p (a b) -> p a b", a=nb)
        P.op("dve", lambda: V.tensor_tensor(out=rkv, in0=bcmid(sc, nb), in1=bc(sc, nb), op=ALU.is_gt), [scd], [rk])
        P.op("dve", lambda: V.tensor_reduce(out=selb[0:n, 0:nb], in_=rkv, axis=AX.X, op=ALU.add), [rk], [selb])
        P.op("dve", lambda: V.tensor_scalar(out=selb[0:n, 0:nb], in0=selb[0:n, 0:nb], scalar1=15.5, scalar2=NEG,
                                            op0=ALU.is_gt, op1=ALU.mult), [selb], [selb])
        if dead_ap is not None:
            P.op("dve", lambda: V.tensor_tensor(out=selb[0:n, 0:nb], in0=selb[0:n, 0:nb], in1=dead_ap, op=ALU.add), [selb, sdead], [selb])
        if out_ap is not None:
            P.op("dve", lambda: V.tensor_copy(out=out_ap, in_=selb[0:n, 0:nb]), [selb], [out_dep])

    def finalize(n, gate_ap, gdeps, o_tok_ap, otdep):
        P.op("dve", lambda: V.tensor_reduce(out=den12[0:n, :], in_=dens[0:n, :, :], axis=AX.X, op=ALU.add), [dens], [den12])
        P.op("dve", lambda: V.tensor_scalar(out=den12[0:n, :], in0=den12[0:n, :], scalar1=1e-30, scalar2=None, op0=ALU.max), [den12], [den12])
        P.op("dve", lambda: V.reciprocal(out=den12[0:n, :], in_=den12[0:n, :]), [den12], [den12])
        P.op("dve", lambda: V.tensor_tensor(out=sc12[0:n, :].rearrange("p (b i) -> p b i", b=3), in0=den12[0:n, :].rearrange("p (b i) -> p b i", b=3),
                                            in1=gate_ap, op=ALU.mult), [den12] + list(gdeps), [sc12])
        t12 = tmpF[0].t[0:n, :].rearrange("p (a b) -> p a b", a=8)
        t4 = tmpF[1].t[0:n, 0:256].rearrange("p (a b) -> p a b", a=4)
        P.op("dve", lambda: V.tensor_tensor(out=t12, in0=B[4][0:n, :].rearrange("p (a b) -> p a b", a=8), in1=bc(sc12[0:n, 0:8], 64), op=ALU.mult),
             [B[4], sc12], [tmpF[0]])
        P.op("dve", lambda: V.tensor_tensor(out=t4, in0=B[5][0:n, 0:256].rearrange("p (a b) -> p a b", a=4), in1=bc(sc12[0:n, 8:12], 64), op=ALU.mult),
             [B[5], sc12], [tmpF[1]])
        P.op("dve", lambda: V.tensor_tensor(out=t4, in0=t4, in1=t12[:, 0:4, :], op=ALU.add), [tmpF[0], tmpF[1]], [tmpF[1]])
        P.op("dve", lambda: V.tensor_tensor(out=o_tok_ap, in0=t4, in1=t12[:, 4:8, :], op=ALU.add), [tmpF[0], tmpF[1]], [otdep])

    def cmp_branch(n, qhs, qdeps, kc_t, g, gpar, bias_ap, bdeps, vc_t, nb_sel, is_sample):
        Sb = B[6]
        for i in range(4):
            P.mm(Sb[0:n, 64 * i:64 * i + 64], qhs[i], kc_t[64 * gpar:64 * gpar + 64, g, :], True, True, list(qdeps) + [kc_t], [Sb])
        E = tmpF[0]
        Ev = E.t[0:n, 0:256].rearrange("p (a b) -> p a b", a=4)
        P.op("dve", lambda: V.scalar_tensor_tensor(out=Ev, in0=Sb[0:n, 0:256].rearrange("p (a b) -> p a b", a=4), scalar=0.125, in1=bias_ap,
                                                   op0=ALU.mult, op1=ALU.add), [Sb] + list(bdeps), [E])
        P.op("act", lambda: A.activation(out=E[0:n, 0:256], in_=E[0:n, 0:256], func=AF.Exp), [E], [E])
        P.op("dve", lambda: V.tensor_reduce(out=small[0:n, 8:12], in_=Ev, axis=AX.X, op=ALU.add), [E], [small])
        P.op("dve", lambda: V.tensor_scalar(out=small[0:n, 8:12], in0=small[0:n, 8:12], scalar1=1e-30, scalar2=None, op0=ALU.max), [small], [small])
        P.op("dve", lambda: V.reciprocal(out=small[0:n, 12:16], in_=small[0:n, 8:12]), [small], [small])
        P.op("dve", lambda: V.tensor_tensor(out=Ev, in0=Ev, in1=bc(small[0:n, 12:16], 64), op=ALU.mult), [E, small], [E])
        Pb = tmpB[0]
        P.op("act", lambda: A.copy(out=Pb[0:n, 0:256], in_=E[0:n, 0:256]), [E], [Pb])
        P.op("dve", lambda: V.tensor_reduce(out=impt[0:n, 0:64], in_=E.t[0:n, 0:256].rearrange("p (a b) -> p b a", a=4), axis=AX.X, op=ALU.add),
             [E], [impt])
        pv = bfv(B[2])
        for i in range(4):
            P.tr(pv[0:64, 128 * i:128 * i + n], Pb[0:n, 64 * i:64 * i + 64], ident_b[0:n, 0:n], [Pb, ident_b], [B[2]])
        P.op("dve", lambda: V.tensor_copy(out=PT[0][0:64, 0:512], in_=pv[0:64, 0:512]), [B[2]], [PT[0]])
        for i in range(4):
            P.mm(B[4][0:n, 64 * i:64 * i + 64], PT[0][0:64, 128 * i:128 * i + n], vc_t[0:64, g, :], True, True, [PT[0], vc_t], [B[4]])

    def attention(pi, st):
        units, uT, dU, NU, c0, NC, NCp, A_T, o_T, q_T, mrg, yrow0 = st
        join(xdeps)
        for ui, (kind, r0, n, cc) in enumerate(units):
            if kind == "s":
                sample_attention(st, ui)
                continue
            cq, ro = r0 // 128, r0 % 128
            qc = cc - c0
            def add_near(tF, n_, h, ro=ro):
                if ro == 0:
                    P.op("dve", lambda: V.tensor_tensor(out=tF[0:n_, 0:256], in0=tF[0:n_, 0:256], in1=Nt[0:n_, h, :], op=ALU.add), [tF, Nt], [tF])
                else:
                    P.op("dve", lambda: V.tensor_tensor(out=tF[0:n_, ro:256], in0=tF[0:n_, ro:256], in1=Nt[0:n_, h, 0:256 - ro], op=ALU.add), [tF, Nt], [tF])
            def add_far(tF, n_, ro=ro):
                if ro == 0:
                    P.op("dve", lambda: V.tensor_tensor(out=tF[0:n_, 0:128], in0=tF[0:n_, 0:128], in1=Tfar[0:n_, :], op=ALU.add), [tF, Tfar], [tF])
                else:
                    P.op("dve", lambda: V.tensor_scalar(out=tF[0:n_, 0:ro], in0=tF[0:n_, 0:ro], scalar1=NEG, scalar2=None, op0=ALU.add), [tF], [tF])
                    P.op("dve", lambda: V.tensor_tensor(out=tF[0:n_, ro:128], in0=tF[0:n_, ro:128], in1=Tfar[0:n_, 0:128 - ro], op=ALU.add), [tF, Tfar], [tF])
            P.load("sp", bonus_sb, bonus_sb[0:n, :], t_bonus[r0 - 1022:r0 - 1022 + n, :])
            cc0 = F0 + 128 * cq - 2047 + ro
            P.load("sp", Hk, Hk[0:64, :, 0:n], dap(frow, cc0, [[32, 64], [WF, 16], [1, n]]), reads=[fdep])
            for h4 in range(4):
                for i in range(4):
                    P.mm(B[6][0:n, 64 * i:64 * i + 64], Hk[0:64, 4 * h4 + i, 0:n], Jx[0:64, 64:128], True, True, [Hk, Jx], [B[6]])
                P.op("dve", lambda h4=h4: V.tensor_tensor(out=Ct[0:n, 4 * h4:4 * h4 + 4, :], in0=B[6][0:n, 0:256].rearrange("p (a b) -> p a b", a=4),
                                                          in1=bcmid(kbcrep[0:n, :], 4), op=ALU.add), [B[6], kbcrep], [Ct])
            o_tok = ub
            for g in range(4):
                gpar, gp = g % 2, g // 2
                qhs = [q_T[64 * gpar:64 * gpar + 64, 4 * gp + i, qc:qc + n] for i in range(4)]
                P.op("dve", lambda: V.memset(dens[:, :, :], 0.0), [], [dens])
                P.op("dve", lambda: V.memset(dens[:, 0:4, 0:1], 1.0), [], [dens])
                cmp_branch(n, qhs, [dQ], kcT, g, gpar, Ct[0:n, 4 * g:4 * g + 4, :], [Ct], vc, 32, False)
                P.op("dve", lambda: V.tensor_reduce(out=score[0:n, 0:32], in_=impt[0:n, 0:64].rearrange("p (a b) -> p a b", b=2), axis=AX.X, op=ALU.add),
                     [impt], [score])
                P.op("dve", lambda: V.tensor_tensor(out=score[0:n, 0:32], in0=score[0:n, 0:32], in1=bonus_sb[0:n, :], op=ALU.add), [score, bonus_sb], [score])
                rank_select(n, 32, sdead[0:n, :])
                items = []
                for i in range(4):
                    h = 4 * g + i
                    groups = []
                    far_chunks = list(range(0, cq - 1))
                    for k0 in range(0, len(far_chunks), 4):
                        chs = far_chunks[k0:k0 + 4]
                        nch = len(chs)
                        def bias_far(Sb, tF, n_, nk, chs=chs, nch=nch):
                            P.op("dve", lambda: V.scalar_tensor_tensor(
                                out=tF[0:n_, 0:nk].rearrange("p (a b) -> p a b", b=64), in0=Sb[0:n_, 0:nk].rearrange("p (a b) -> p a b", b=64),
                                scalar=0.125, in1=bc(selb[0:n_, 2 * chs[0]:2 * chs[0] + 2 * nch], 64), op0=ALU.mult, op1=ALU.add), [Sb, selb], [tF])
                        groups.append(dict(nk=128 * nch, kT=kselT[64 * gpar:64 * gpar + 64, gp, 128 * chs[0]:128 * chs[0] + 128 * nch], kdeps=[kselT],
                                           bias=bias_far, v=[(vsel[:, c, 64 * g:64 * g + 64], 128, [vsel]) for c in chs]))
                    def bias_near(Sb, tF, n_, nk, h=h):
                        P.op("dve", lambda: V.scalar_tensor_tensor(
                            out=tF[0:n_, 0:256].rearrange("p (a b) -> p a b", b=64), in0=Sb[0:n_, 0:256].rearrange("p (a b) -> p a b", b=64),
                            scalar=0.125, in1=bc(selb[0:n_, 2 * cq - 2:2 * cq + 2], 64), op0=ALU.mult, op1=ALU.add), [Sb, selb], [tF])
                        add_near(tF, n_, h)
                    groups.append(dict(nk=256, kT=kselT[64 * gpar:64 * gpar + 64, gp, 128 * (cq - 1):128 * (cq + 1)], kdeps=[kselT],
                                       bias=bias_near, v=[(vsel[:, c, 64 * g:64 * g + 64], 128, [vsel]) for c in (cq - 1, cq)]))
                    items += branch(qhs[i], [dQ], n, groups, B[4][0:n, 256 + 64 * i:256 + 64 * i + 64], B[4], 4 + i)
                    def bias_wfar(Sb, tF, n_, nk):
                        P.op("dve", lambda: V.scalar_tensor_tensor(out=tF[0:n_, 0:384].rearrange("p (a b) -> p a b", b=128), in0=Sb[0:n_, 0:384].rearrange("p (a b) -> p a b", b=128), scalar=0.125,
                                                                   in1=bc(kbrep[0:n_, cq - 4:cq - 1], 128), op0=ALU.mult, op1=ALU.add), [Sb, kbrep], [tF])
                        add_far(tF, n_)
                    def bias_wnear(Sb, tF, n_, nk, h=h):
                        P.op("dve", lambda: V.scalar_tensor_tensor(out=tF[0:n_, 0:256].rearrange("p (a b) -> p a b", b=128),
                                                                   in0=Sb[0:n_, 0:256].rearrange("p (a b) -> p a b", b=128), scalar=0.125,
                                                                   in1=bc(kbrep[0:n_, cq - 1:cq + 1], 128), op0=ALU.mult, op1=ALU.add), [Sb, kbrep], [tF])
                        add_near(tF, n_, h)
                    wg = [dict(nk=384, kT=kwinT[64 * gpar:64 * gpar + 64, gp, 128 * (cq - 4):128 * (cq - 1)], kdeps=[kwinT], bias=bias_wfar,
                               v=[(vwin[:, c, 64 * g:64 * g + 64], 128, [vwin]) for c in (cq - 4, cq - 3, cq - 2)]),
                          dict(nk=256, kT=kwinT[64 * gpar:64 * gpar + 64, gp, 128 * (cq - 1):128 * (cq + 1)], kdeps=[kwinT], bias=bias_wnear,
                               v=[(vwin[:, c, 64 * g:64 * g + 64], 128, [vwin]) for c in (cq - 1, cq)])]
                    items += branch(qhs[i], [dQ], n, wg, B[5][0:n, 64 * i:64 * i + 64], B[5], 8 + i)
                run_items(items)
                gate_ap = gates[0:n, ui, :].rearrange("p (b g i) -> p b g i", b=3, g=4)[:, :, g, :]
                finalize(n, gate_ap, [gates], o_tok[0:n, 256 * g:256 * g + 256].rearrange("p (a b) -> p a b", a=4), o_tok.d)
            pv = bfv(B[6])
            for kc in range(8):
                P.tr(pv[:, 128 * kc:128 * kc + n], o_tok[0:n, 128 * kc:128 * kc + 128], ident_b[0:n, 0:n], [o_tok, ident_b], [B[6]])
            P.op("act", lambda pv=pv, qc=qc, n=n: A.copy(out=o_T[:, :, qc:qc + n], in_=pv[:, :].rearrange("p (a b) -> p a b", a=8)[:, :, 0:n]), [B[6]], [dO])

    def join(deps):
        P.op("pool", lambda: G.memset(small[0:1, 60:61], 0.0), [], list(deps) + [small])

    def sample_attention(st, ui):
        units, uT, dU, NU, c0, NC, NCp, A_T, o_T, q_T, mrg, yrow0 = st
        P.load("sp", idx_t, idx_t[:], dap(ptab, 0, [[0, 128], [1, 256]]))
        iof = tmpF[0]
        P.op("pool", lambda: G.iota(iot_i[:, :], pattern=[[0, 1]], base=0, channel_multiplier=1), [], [iot_i])
        P.op("dve", lambda: V.tensor_copy(out=small[:, 20:21], in_=iot_i[:, :]), [iot_i], [small])
        P.op("dve", lambda: V.tensor_copy(out=iof[:, 0:256], in_=idx_t[:, :]), [idx_t], [iof])
        P.op("dve", lambda: V.tensor_scalar(out=iof[:, 0:256], in0=iof[:, 0:256], scalar1=128.0, scalar2=small[:, 20:21], op0=ALU.mult, op1=ALU.add),
             [iof, small], [iof])
        P.op("dve", lambda: V.tensor_copy(out=idx_t[:, :], in_=iof[:, 0:256]), [iof], [idx_t])
        kcmpTb = RA.t[:, 13824:17920].rearrange("p (a b) -> p a b", a=2)
        vcmpTb = RA.t[:, 17920:22016].rearrange("p (a b) -> p a b", a=2)
        ksTb = RA.t[:, 22016:26128].rearrange("p (a b) -> p a b", a=2)
        kwTb = RA.t[:, 26128:27168].rearrange("p (a b) -> p a b", a=2)
        vs_b = slabs[0].t[:, 0:4096].rearrange("p (a b) -> p a b", a=16)
        kwt = slabs[0].t[:, 4096:5120].rearrange("p (a b) -> p a b", a=4)
        pgb = [slabs[1].t[:, 0:1024], slabs[1].t[:, 1024:2048], slabs[1].t[:, 4608:5632]]
        vw_b = slabs[1].t[:, 2048:3072].rearrange("p (a b) -> p a b", a=4)
        vnew = slabs[1].t[:, 3072:4608]
        d_pg = [Dep("pg0"), Dep("pg1"), Dep("pg2")]; d_vs = Dep("vs"); d_kwt = Dep("kwt"); d_vw = Dep("vw"); d_vn = Dep("vnew")
        d_kcm = Dep("kcmb"); d_vcm = Dep("vcmb"); d_ks = Dep("ksb"); d_kw = Dep("kwb")
        newdeps = d_pg + [d_vs, d_kwt, d_vw, d_vn, d_kcm, d_vcm, d_ks, d_kw]
        join([slabs[0].d, slabs[1].d, dM] + newdeps)
        o16 = tmpB[1]
        qS = ub.t[:, 0:512].rearrange("p (g b x) -> p g b x", g=2, b=16)
        for gp_ in range(2):
            P.op("dve", lambda gp_=gp_: V.tensor_copy(
                out=qS[:, gp_, :, :].rearrange("p b (i t) -> p b i t", i=4),
                in_=q_T[:, 4 * gp_:4 * gp_ + 4, 512:576].rearrange("p i (b t) -> p b i t", t=4)), [dQ], [ub])
        for b_ in range(NSB):
            def gather(pg):
                k = pg % 3
                j = 16 * b_ + pg
                cx.dma("pool", None, None, owner=d_pg[k], reads=[idx_t.d], writes=[d_pg[k]],
                       fn=lambda: G.indirect_dma_start(out=pgb[k], out_offset=None, in_=cache[:, :],
                                                       in_offset=bass.IndirectOffsetOnAxis(ap=idx_t[:, j:j + 1], axis=0)))
            gather(0)
            gather(1)
            for pg in range(16):
                k = pg % 3
                if pg + 2 < 16:
                    gather(pg + 2)
                bX, bY = (B[7], B[6]) if pg % 2 == 0 else (B[5], B[4])
                pvx, pvy = bfv(bX), bfv(bY)
                for si in range(2):
                    for gp in range(2):
                        ii = 2 * si + gp
                        P.tr(pvx[:, 128 * ii:128 * ii + 128], pgb[k][:, 256 * si + 128 * gp:256 * si + 128 * gp + 128], ident_b[:, :],
                             [d_pg[k], ident_b], [bX])
                for gp in range(2):
                    P.tr(pvy[:, 128 * gp:128 * gp + 128], pgb[k][:, 512 + 128 * gp:512 + 128 * gp + 128], ident_b[:, :], [d_pg[k], ident_b], [bY])
                def pview(si, pvx=pvx):
                    return pvx[:, 256 * si:256 * si + 256].rearrange("p (a b) -> p a b", a=2)
                P.op("dve", lambda pg=pg, pview=pview: V.tensor_tensor(out=kcmpTb[:, :, 128 * pg:128 * pg + 128], in0=pview(0),
                                                                       in1=bcmid(peT[:, 0, :], 2), op=ALU.add), [bX, peT], [d_kcm])
                P.op("dve", lambda pg=pg, pview=pview: V.tensor_tensor(out=vcmpTb[:, :, 128 * pg:128 * pg + 128], in0=pview(1),
                                                                       in1=bcmid(peT[:, 1, :], 2), op=ALU.add), [bX, peT], [d_vcm])
                P.op("act", lambda pg=pg, pvy=pvy: A.copy(out=ksTb[:, :, 128 * pg:128 * pg + 128],
                                                          in_=pvy[:, 0:256].rearrange("p (a b) -> p a b", a=2)), [bY], [d_ks])
                P.op("act", lambda pg=pg, k=k: A.copy(out=vs_b[:, pg, :], in_=pgb[k][:, 768:1024]), [d_pg[k]], [d_vs])
            cx.dma("pool", kwt, dap(swin, 512 * 512 * b_, [[512, 128], [128 * 512, 4], [1, 256]]), owner=d_kwt, writes=[d_kwt])
            cx.dma("pool", vw_b, dap(swin, 512 * 512 * b_ + 256, [[512, 128], [128 * 512, 4], [1, 256]]), owner=d_vw, writes=[d_vw])
            cx.dma("pool", vnew[0:4, :], kvs_scr[4 * b_:4 * b_ + 4, :], owner=d_vn, reads=[kvsd], writes=[d_vn])
            pv = bfv(B[7])
            for c in range(4):
                for gp in range(2):
                    ii = 2 * c + gp
                    P.tr(pv[:, 128 * ii:128 * ii + 128], kwt[:, c, 128 * gp:128 * gp + 128], ident_b[:, :], [d_kwt, ident_b], [B[7]])
            P.op("act", lambda pv=pv: A.copy(out=kwTb[:, :, 0:512].rearrange("p g (c k) -> p c g k", c=4),
                                             in_=pv[:, 0:1024].rearrange("p (c g k) -> p c g k", c=4, g=2)), [B[7]], [d_kw])
            P.op("dve", lambda b_=b_: V.tensor_copy(out=ksTb[:, :, 2048:2052], in_=ksT_new[:, :, 4 * b_:4 * b_ + 4]), [ksT_new], [d_ks])
            P.op("dve", lambda b_=b_: V.tensor_copy(out=kwTb[:, :, 512:516], in_=kwT_new[:, :, 4 * b_:4 * b_ + 4]), [kwT_new], [d_kw])
            P.mm(B[7][0:16, 0:48], selB[:, 16 * b_:16 * b_ + 16], gates[0:64, ui, :], True, True, [selB, gates], [B[7]])
            P.op("dve", lambda: V.tensor_tensor(out=tmpF[1][0:16, 0:48].rearrange("p (a b) -> p a b", b=4),
                                                in0=B[7][0:16, 0:48].rearrange("p (a b) -> p a b", b=4), in1=bcmid(Mi[:, :], 12), op=ALU.mult),
                 [B[7], Mi], [tmpF[1]])
            P.op("dve", lambda: V.tensor_reduce(out=G16[:, :], in_=tmpF[1][0:16, 0:48].rearrange("p (a b) -> p a b", b=4), axis=AX.X, op=ALU.add),
                 [tmpF[1]], [G16])
            compress(kcmpTb, d_kcm, w1k_sb, True, kcTb)
            compress(vcmpTb, d_vcm, w1v_sb, False, vcb)
            selb4 = Ct.t[0:16, :, :].rearrange("p a b -> p (a b)").bitcast(F32)[:, 0:160].rearrange("p (g x) -> p g x", g=4)
            P.op("dve", lambda: V.memset(dens[0:16, :, :], 0.0), [], [dens])
            P.op("dve", lambda: V.memset(dens[0:16, 0:12:3, 0:1], 1.0), [], [dens])
            items = []
            Sb = B[7]
            E = tmpF[0]
            G4 = range(4)
            SbG = [B[7], B[3]]
            for g in G4:
                gpar, gp = g % 2, g // 2
                P.mm(SbG[gpar][0:16, 64 * g:64 * g + 64], qS[64 * gpar:64 * gpar + 64, gp, b_, :], kcTb[64 * gpar:64 * gpar + 64, g, :], True, True,
                     [ub, kcTb], [SbG[gpar]])
            for g in G4:
                P.op("dve", lambda g=g: V.scalar_tensor_tensor(out=E[0:16, 64 * g:64 * g + 64], in0=SbG[g % 2][0:16, 64 * g:64 * g + 64], scalar=0.125,
                                                               in1=Cs[:, g, :], op0=ALU.mult, op1=ALU.add), [SbG[g % 2], Cs], [E])
            for g in G4:
                P.op("act", lambda g=g: A.activation(out=E[0:16, 64 * g:64 * g + 64], in_=E[0:16, 64 * g:64 * g + 64], func=AF.Exp,
                                                     accum_out=small[0:16, 8 + g:9 + g]), [E], [E, small])
            for g in G4:
                P.op("dve", lambda g=g: V.reciprocal(out=small[0:16, 12 + g:13 + g], in_=small[0:16, 8 + g:9 + g]), [small], [small])
            for g in G4:
                P.op("dve", lambda g=g: V.tensor_scalar(out=E[0:16, 64 * g:64 * g + 64], in0=E[0:16, 64 * g:64 * g + 64], scalar1=small[0:16, 12 + g:13 + g],
                                                        scalar2=None, op0=ALU.mult), [E, small], [E])
            for g in G4:
                P.op("act", lambda g=g: A.copy(out=tmpB[0][0:16, 64 * g:64 * g + 64], in_=E[0:16, 64 * g:64 * g + 64]), [E], [tmpB[0]])
            for g in G4:
                P.mm(B[7][0:16, 256 + 64 * g:256 + 64 * g + 64], msame[:, :], E[0:16, 64 * g:64 * g + 64], True, True, [msame, E], [B[7]])
            scs = [tmpF[1].t[0:16, 40 * g:40 * g + 33] for g in G4]
            for g in G4:
                P.op("dve", lambda g=g: V.tensor_reduce(out=scs[g][:, 0:32], in_=B[7][0:16, 256 + 64 * g:256 + 64 * g + 64].rearrange("p (a b) -> p a b", b=2),
                                                        axis=AX.X, op=ALU.add), [B[7]], [tmpF[1]])
            for g in G4:
                P.op("dve", lambda g=g: V.memset(scs[g][:, 32:33], 0.0), [], [tmpF[1]])
            for g in G4:
                P.op("dve", lambda g=g: V.tensor_tensor(out=scs[g], in0=scs[g], in1=sbonus[:, :], op=ALU.add), [tmpF[1], sbonus], [tmpF[1]])
            pv2 = bfv(B[2])
            for g in G4:
                P.tr(pv2[0:64, 16 * g:16 * g + 16], tmpB[0][0:16, 64 * g:64 * g + 64], ident_b[0:16, 0:16], [tmpB[0], ident_b], [B[2]])
            for g in G4:
                P.op("dve", lambda g=g, pv2=pv2: V.tensor_copy(out=PT[0][0:64, 16 * g:16 * g + 16], in_=pv2[0:64, 16 * g:16 * g + 16]), [B[2]], [PT[0]])
            for g in G4:
                P.mm(B[4 + g // 2][0:16, 192 * (g % 2):192 * (g % 2) + 64], PT[0][0:64, 16 * g:16 * g + 16], vcb[0:64, g, :], True, True,
                     [PT[0], vcb], [B[4 + g // 2]])
            for g in G4:
                rank_select(16, 33, None, selb4[:, g, 0:33], Ct.d, sc_ap=scs[g], sc_dep=tmpF[1])
            for g in range(4):
                gpar, gp = g % 2, g // 2
                q16 = qS[64 * gpar:64 * gpar + 64, gp, b_, :]
                n = 16
                Bacc = B[4 + g // 2]
                ab = 192 * (g % 2)
                groups = []
                for chs in ([0, 1, 2, 3], [4, 5, 6, 7], [8, 9, 10, 11], [12, 13, 14]):
                    nch = len(chs)
                    def bias_far(Sb, tF, n_, nk, chs=chs, nch=nch, g=g):
                        P.op("dve", lambda: V.scalar_tensor_tensor(
                            out=tF[0:n_, 0:nk].rearrange("p (a b) -> p a b", b=64), in0=Sb[0:n_, 0:nk].rearrange("p (a b) -> p a b", b=64),
                            scalar=0.125, in1=bc(selb4[0:n_, g, 2 * chs[0]:2 * chs[0] + 2 * nch], 64), op0=ALU.mult, op1=ALU.add), [Sb, Ct], [tF])
                    groups.append(dict(nk=128 * nch, kT=ksTb[64 * gpar:64 * gpar + 64, gp, 128 * chs[0]:128 * chs[0] + 128 * nch], kdeps=[d_ks],
                                       bias=bias_far, v=[(vs_b[:, c, 64 * g:64 * g + 64], 128, [d_vs]) for c in chs]))
                def bias_near(Sb, tF, n_, nk, g=g):
                    P.op("dve", lambda: V.scalar_tensor_tensor(
                        out=tF[0:n_, 0:128].rearrange("p (a b) -> p a b", b=64), in0=Sb[0:n_, 0:128].rearrange("p (a b) -> p a b", b=64),
                        scalar=0.125, in1=bc(selb4[0:n_, g, 30:32], 64), op0=ALU.mult, op1=ALU.add), [Sb, Ct], [tF])
                    P.op("dve", lambda: V.scalar_tensor_tensor(out=tF[0:n_, 128:132], in0=Sb[0:n_, 128:132], scalar=0.125,
                                                               in1=bc(selb4[0:n_, g, 32:33], 4), op0=ALU.mult, op1=ALU.add), [Sb, Ct], [tF])
                    P.op("dve", lambda: V.tensor_tensor(out=tF[0:n_, 0:132], in0=tF[0:n_, 0:132], in1=Ns[:, g, :], op=ALU.add), [tF, Ns], [tF])
                groups.append(dict(nk=132, kT=ksTb[64 * gpar:64 * gpar + 64, gp, 1920:2052], kdeps=[d_ks], bias=bias_near,
                                   v=[(vs_b[:, 15, 64 * g:64 * g + 64], 128, [d_vs]), (vnew[0:4, 768 + 64 * g:768 + 64 * g + 64], 4, [d_vn])]))
                items += branch(q16, [ub], 16, groups, Bacc[0:16, ab + 64:ab + 128], Bacc, 3 * g + 1)
                def bias_wfar(Sb, tF, n_, nk):
                    P.op("dve", lambda: V.tensor_scalar(out=tF[0:n_, 0:384], in0=Sb[0:n_, 0:384], scalar1=0.125, scalar2=None, op0=ALU.mult), [Sb], [tF])
                    P.op("dve", lambda: V.tensor_tensor(out=tF[0:n_, 0:128], in0=tF[0:n_, 0:128], in1=Tfs[:, :], op=ALU.add), [tF, Tfs], [tF])
                def bias_wnear(Sb, tF, n_, nk, g=g):
                    P.op("dve", lambda: V.scalar_tensor_tensor(out=tF[0:n_, 0:132], in0=Sb[0:n_, 0:132], scalar=0.125, in1=Ns[:, g, :],
                                                               op0=ALU.mult, op1=ALU.add), [Sb, Ns], [tF])
                wg = [dict(nk=384, kT=kwTb[64 * gpar:64 * gpar + 64, gp, 0:384], kdeps=[d_kw], bias=bias_wfar,
                           v=[(vw_b[:, c, 64 * g:64 * g + 64], 128, [d_vw]) for c in range(3)]),
                      dict(nk=132, kT=kwTb[64 * gpar:64 * gpar + 64, gp, 384:516], kdeps=[d_kw], bias=bias_wnear,
                           v=[(vw_b[:, 3, 64 * g:64 * g + 64], 128, [d_vw]), (vnew[0:4, 1280 + 64 * g:1280 + 64 * g + 64], 4, [d_vn])])]
                items += branch(q16, [ub], 16, wg, Bacc[0:16, ab + 128:ab + 192], Bacc, 3 * g + 2)
            run_items(items)
            P.op("dve", lambda: V.tensor_reduce(out=den12[0:16, :], in_=dens[0:16, :, :], axis=AX.X, op=ALU.add), [dens], [den12])
            P.op("dve", lambda: V.tensor_scalar(out=den12[0:16, :], in0=den12[0:16, :], scalar1=1e-30, scalar2=None, op0=ALU.max), [den12], [den12])
            P.op("dve", lambda: V.reciprocal(out=den12[0:16, :], in_=den12[0:16, :]), [den12], [den12])
            P.op("dve", lambda: V.tensor_tensor(out=sc12[0:16, :].rearrange("p (g r) -> p g r", g=4), in0=den12[0:16, :].rearrange("p (g r) -> p g r", g=4),
                                                in1=G16[:, :].rearrange("p (r g) -> p g r", r=3), op=ALU.mult), [den12, G16], [sc12])
            for hb in range(2):
                t6 = tmpF[hb].t[0:16, 0:384].rearrange("p (a b) -> p a b", a=6)
                P.op("dve", lambda hb=hb, t6=t6: V.tensor_tensor(out=t6, in0=B[4 + hb][0:16, 0:384].rearrange("p (a b) -> p a b", a=6),
                                                               in1=bc(sc12[0:16, 6 * hb:6 * hb + 6], 64), op=ALU.mult), [B[4 + hb], sc12], [tmpF[hb]])
                t23 = tmpF[hb].t[0:16, 0:384].rearrange("p (g r d) -> p g r d", g=2, r=3)
                P.op("dve", lambda t23=t23, hb=hb: V.tensor_tensor(out=t23[:, :, 0, :], in0=t23[:, :, 0, :], in1=t23[:, :, 1, :], op=ALU.add), [tmpF[hb]], [tmpF[hb]])
                P.op("dve", lambda t23=t23, hb=hb: V.tensor_tensor(out=o16[0:16, 128 * hb:128 * hb + 128].rearrange("p (g d) -> p g d", g=2),
                                                                  in0=t23[:, :, 0, :], in1=t23[:, :, 2, :], op=ALU.add), [tmpF[hb]], [o16])
            pv3 = bfv(B[3])
            for g in range(4):
                P.tr(pv3[0:64, 16 * g:16 * g + 16], o16[0:16, 64 * g:64 * g + 64], ident_b[0:16, 0:16], [o16, ident_b], [B[3]])
            pq = pv3[0:64, 0:64].rearrange("p (gj two t) -> p gj two t", two=2, t=4)
            P.op("dve", lambda b_=b_, pq=pq: V.tensor_copy(out=oS[0:64, :, 4 * b_:4 * b_ + 4], in_=pq[:, :, 0, :]), [B[3]], [oS])
            P.op("dve", lambda b_=b_, pq=pq: V.tensor_copy(out=oS[64:128, :, 4 * b_:4 * b_ + 4], in_=pq[:, :, 1, :]), [B[3]], [oS])
        join([slabs[0].d, slabs[1].d, dM] + newdeps)
        P.op("act", lambda: A.copy(out=o_T[:, :, 512:576], in_=oS[:, :, :]), [oS], [dO])

    ATT = int(os.environ.get("KATT", "1"))
    for pi in range(2):
        st = pass_state if pi == 0 else run_pass(1)
        pass_qg(pi, st)
        if ATT:
            attention(pi, st)
            join(xdeps)
        else:
            P.op("pool", lambda: G.memset(RA.t[:, 4608:9216], 0.0), [], [dO])
        pass_merge_ffn(pi, st)
    return P, locals()


def _shared_inputs(inp):
    f = lambda a: np.ascontiguousarray(np.asarray(a, dtype=np.float32))
    sh = dict(
        w_in=f(inp["w_in"][0]), w_pp=f(inp["w_pool_proj"][0]), w_np=f(inp["w_nsa_proj"][0]),
        w_out=f(inp["w_out"][0]), w_up=f(inp["w_up"][0]), w_down=f(inp["w_down"][0]),
        w1k=f(inp["w1_cmp_k"][0]).reshape(32, 4096), w1v=f(inp["w1_cmp_v"][0]).reshape(32, 4096),
        w2k=f(inp["w2_cmp_k"][0]), w2v=f(inp["w2_cmp_v"][0]),
        pek=f(inp["pe_cmp_k"][0]), pev=f(inp["pe_cmp_v"][0]),
        relb=f(inp["rel_bias"]), wgrp=f(inp["w_pool_grp"][0]).reshape(1024, 256),
        g_pm=f(inp["g_pre_mix"]), g_qm=f(inp["g_post_mix"]), g_pf=f(inp["g_pre_ffn"]), g_qf=f(inp["g_post_ffn"]),
        pscale=f(inp["pool_scale"]), conv_w=f(inp["conv_w"][0]), conv_b=f(inp["conv_b"]),
        t_sbonus=_sample_bonus(),
        t_selb=_selb_table(), t_mi=np.ascontiguousarray((np.arange(16)[:, None] // 4 == np.arange(4)[None, :]).astype(np.float32)),
        t_msame=np.ascontiguousarray((np.arange(16)[:, None] % 4 == np.arange(16)[None, :] % 4).astype(np.float32)),
    )
    return sh


def _core_inputs(inp, c, shared, tables, cache2d):
    b, half = c // 2, c % 2
    xp = np.asarray(inp["x_prompt"], dtype=np.float32)
    if half == 0:
        xb = np.concatenate([np.zeros((1024, D), np.float32), xp[b, :1024]], axis=0)
    else:
        xb = xp[b]
    sl = slice(NSB * c, NSB * c + NSB)
    m = dict(shared)
    m.update(
        xb=np.ascontiguousarray(xb),
        xs=np.ascontiguousarray(np.asarray(inp["x_sample"], dtype=np.float32)[sl].reshape(64, D)),
        cache=cache2d,
        ptab=np.ascontiguousarray(np.asarray(inp["page_table"], dtype=np.int32)[sl].reshape(1, 256)),
        swin=np.ascontiguousarray(np.asarray(inp["state_kv_win"], dtype=np.float32)[0, sl].reshape(NSB * 512, 512)),
        spool=np.ascontiguousarray(np.asarray(inp["state_pool"], dtype=np.float32)[0, sl].reshape(NSB * 15, 1024)),
        sconv=np.ascontiguousarray(np.asarray(inp["state_conv"], dtype=np.float32)[0, sl].reshape(NSB * 2, DFF)),
    )
    t = tables[half]
    m.update(t_oh=t["oh"], t_far=t["far"], t_kb=t["kb"], t_kbc=t["kbc"], t_bonus=t["bonus"], t_sdead=t["sdead"], t_rc=t["rc"])
    return m


def _assemble(res):
    y_p = np.zeros((4, 2048, D), np.float32); y_s = np.zeros((128, 4, D), np.float32)
    kv_p = np.zeros((1, 4, 2048, 4, 4, 64), np.float32); kv_s = np.zeros((1, 128, 4, 4, 4, 64), np.float32)
    win_p = np.zeros((1, 4, 512, 2, 4, 64), np.float32); win_s = np.zeros((1, 128, 512, 2, 4, 64), np.float32)
    pool_p = np.zeros((1, 4, 15, 1024), np.float32); pool_s = np.zeros((1, 128, 15, 1024), np.float32)
    conv_p = np.zeros((1, 4, 2, DFF), np.float32); conv_s = np.zeros((1, 128, 2, DFF), np.float32)
    for c, r in res.items():
        b, half = c // 2, c % 2
        sl = slice(NSB * c, NSB * c + NSB)
        y_p[b, 1024 * half:1024 * half + 1024] = r["y"][:1024]
        y_s[sl] = r["y"][1024:].reshape(NSB, 4, D)
        kv_p[0, b, 1024 * half:1024 * half + 1024] = r["kvrows"][:1024].reshape(1024, 4, 4, 64)
        kv_s[0, sl] = r["kvrows"][1024:].reshape(NSB, 4, 4, 4, 64)
        win_s[0, sl] = r["wins"].reshape(NSB, 512, 2, 4, 64)
        pool_s[0, sl] = r["pools"].reshape(NSB, 15, 1024)
        conv_s[0, sl] = r["convs"].reshape(NSB, 2, DFF)
        if half == 1:
            win_p[0, b] = r["winp"].reshape(512, 2, 4, 64)
            pool_p[0, b] = r["poolp"]
            conv_p[0, b] = r["convp"]
    return (y_p, y_s, kv_p, kv_s, win_p, win_s, pool_p, pool_s, conv_p, conv_s)


def kernel(**inputs):
    P, _ = build_program(NPOOL_PAGES)
    P.cx.finish()
    shared = _shared_inputs(inputs)
    tables = [_const_tables(0), _const_tables(1)]
    cache2d = np.ascontiguousarray(np.asarray(inputs["cache_kv"], dtype=np.float32)[0].reshape(NPOOL_PAGES * 128, 1024))
    in_maps = [_core_inputs(inputs, c, shared, tables, cache2d) for c in range(8)]
    res = run_bass_kernel_spmd(P.nc, in_maps, core_ids=list(range(8)))
    return _assemble({c: res.results[c] for c in range(8)})
```

```python
import math
import os
import contextlib
import numpy as np
import concourse.bass as bass
import concourse.mybir as mybir
from concourse.bass_utils import run_bass_kernel_spmd

F32 = mybir.dt.float32
BF16 = mybir.dt.bfloat16
I32 = mybir.dt.int32
AF = mybir.ActivationFunctionType
ALU = mybir.AluOpType
AX = mybir.AxisListType

D = 2048
NPOOL_PAGES = 2560
IN_W = 7728
DFF = 5632
EPS = 1e-6
NEG = -30000.0
NSB = 16
WF = 4352
F0 = 2048


class Dep:
    __slots__ = ("name", "w", "r", "dsem", "dcnt", "excl")

    def __init__(self, name=""):
        self.name = name
        self.excl = False
        self.w = None
        self.r = {}
        self.dsem = None
        self.dcnt = 0


class Ctx:
    ENGS = ("pe", "act", "dve", "pool", "sp")

    def __init__(self, nc):
        self.nc = nc
        self.eng = {"pe": nc.tensor, "act": nc.scalar, "dve": nc.vector,
                    "pool": nc.gpsimd, "sp": nc.sync}
        self.sem = {e: nc.alloc_semaphore(name="c_" + e) for e in self.ENGS}
        self.cnt = {e: 0 for e in self.ENGS}
        self.seen = {e: {} for e in self.ENGS}
        self.nsem = 0
        self.out_waits = {}

    def new_sem(self, name):
        self.nsem += 1
        return self.nc.alloc_semaphore(name="d%d_%s" % (self.nsem, name))

    def _need(self, e, ev):
        if ev is None:
            return
        kind, key, val = ev
        if kind == "e":
            if key == e and key == "pe":
                return
            semh = self.sem[key]
            k = "e_" + key
        else:
            semh = key
            k = key.name
        if self.seen[e].get(k, 0) >= val:
            return
        self.seen[e][k] = val
        self.eng[e].wait_ge(semh, val)

    def _deps(self, e, reads, writes):
        for d in reads:
            self._need(e, d.w)
            if d.excl:
                for ev in d.r.values():
                    self._need(e, ev)
        for d in writes:
            self._need(e, d.w)
            for ev in d.r.values():
                self._need(e, ev)

    @staticmethod
    def _addr(d, ev):
        k = ev[1] if ev[0] == "e" else ev[1].name
        old = d.r.get(k)
        if old is None or old[2] < ev[2]:
            d.r[k] = ev

    def op(self, e, fn, reads=(), writes=()):
        self._deps(e, reads, writes)
        ins = fn()
        self.cnt[e] += 1
        ins.then_inc(self.sem[e], 1)
        ev = ("e", e, self.cnt[e])
        for d in reads:
            self._addr(d, ev)
        for d in writes:
            d.w = ev
            d.r = {}
        return ins

    def dma(self, e, out, in_, owner, reads=(), writes=(), is_output=False, group=False, fn=None, **kw):
        if group and owner.dsem is not None:
            for d in reads:
                self._need(e, d.w)
            for d in writes:
                if not (d.w is not None and d.w[0] == "d" and d.w[1] is owner.dsem):
                    self._need(e, d.w)
                for ev in d.r.values():
                    self._need(e, ev)
        else:
            self._deps(e, reads, writes)
        if owner.dsem is None:
            owner.dsem = self.new_sem(owner.name)
        if not group and owner.dcnt > 0:
            self._need(e, ("d", owner.dsem, owner.dcnt))
        if fn is not None:
            ins = fn()
        else:
            ins = self.eng[e].dma_start(out=out, in_=in_, **kw)
        owner.dcnt += 16
        ins.then_inc(owner.dsem, 16)
        ev = ("d", owner.dsem, owner.dcnt)
        for d in writes:
            d.w = ev
            d.r = {}
        for d in reads:
            self._addr(d, ev)
        if is_output:
            self.out_waits[owner.dsem.name] = (owner.dsem, owner.dcnt)
        return ins

    def finish(self):
        for s, c in self.out_waits.values():
            self.eng["sp"].wait_ge(s, c)
        for e in self.ENGS:
            if e != "sp" and self.cnt[e] > 0:
                self.eng["sp"].wait_ge(self.sem[e], self.cnt[e])


class View:
    def __init__(self, ap, d):
        self.ap = ap
        self.d = d

    def __getitem__(self, k):
        return self.ap[k]


class T:
    def __init__(self, t, name):
        self.t = t
        self.d = Dep(name)

    def __getitem__(self, k):
        return self.t[k]


def _rel_bucket(n):
    n = np.maximum(n, 0)
    nf = np.maximum(n, 1).astype(np.float32)
    large = 16 + (np.log(nf / np.float32(16)) / np.float32(math.log(8.0)) * np.float32(16)).astype(np.int32)
    large = np.minimum(large, 31)
    return np.where(n < 16, n, large)


def _const_tables(half):
    dist = np.arange(WF) - F0
    bk = _rel_bucket(dist)
    oh = np.zeros((33, WF), np.float32)
    valid = dist >= 0
    oh[bk[valid], np.nonzero(valid)[0]] += 1.0
    oh[31, valid] -= 1.0
    oh[32, ~valid] = 1.0
    far = np.where(dist <= 512, 0.0, NEG).astype(np.float32)[None]
    dead = 1024 if half == 0 else 0
    kb = np.zeros((1, 2048), np.float32)
    kb[0, :dead] = NEG
    kbc = np.zeros((64, 1), np.float32)
    kbc[: dead // 32, 0] = NEG
    apos = np.arange(1022, 2048)
    tpos = apos - dead
    blk = np.arange(32)
    tblk = blk - dead // 64
    cur = np.floor_divide(tpos, 64)
    bonus = np.zeros((1026, 32), np.float32)
    bonus += np.where(tblk[None, :] == 0, 3.0e4, 0.0)
    bonus += np.where((tblk[None, :] == cur[:, None]) & (tblk[None, :] != 0), 2.0e4, 0.0)
    bonus += np.where((tblk[None, :] == cur[:, None] - 1) & (tblk[None, :] != 0), 1.0e4, 0.0)
    invalid = (tblk[None, :] < 0) | (tblk[None, :] * 64 > tpos[:, None])
    bonus = np.where(invalid, -1.0e30, bonus).astype(np.float32)
    sdead = np.where(tblk < 0, NEG, 0.0).astype(np.float32)[None].repeat(128, 0)
    rc = np.zeros((4, 1026), np.float32)
    for gi, w in enumerate((2, 4, 8, 16)):
        rc[gi] = 1.0 / np.minimum(w, np.maximum(tpos, 0) + 1)
    return dict(oh=oh, far=far, kb=kb, kbc=kbc, bonus=bonus, sdead=sdead, rc=rc)


def _selb_table():
    t = np.zeros((16, 4, 16, 4, 4), np.float32)
    for b in range(16):
        for tt in range(4):
            t[b, tt, b, :, tt] = 1.0
    return np.ascontiguousarray(t.reshape(64, 256))


def _sample_bonus():
    b = np.zeros((1, 33), np.float32)
    b[0, 0] = 3.0e4
    b[0, 32] = 2.0e4
    b[0, 31] = 1.0e4
    return b


class Prog:
    def __init__(self, npool=NPOOL_PAGES, stage=99):
        self.stage = stage
        nc = self.nc = bass.Bass("TRN2", target_bir_lowering=False)
        self.cx = Ctx(nc)
        self.es = contextlib.ExitStack()
        self.npool = npool
        self.dram = {}
        self.ddep = {}

    def din(self, name, shape, dt=F32):
        t = self.nc.dram_tensor(name, list(shape), dt, kind="ExternalInput").ap()
        self.dram[name] = t
        self.ddep[name] = Dep(name)
        return t

    def dout(self, name, shape, dt=F32):
        t = self.nc.dram_tensor(name, list(shape), dt, kind="ExternalOutput").ap()
        self.dram[name] = t
        self.ddep[name] = Dep(name)
        return t

    def dscr(self, name, shape, dt=F32):
        t = self.nc.dram_tensor(name, list(shape), dt, kind="Internal").ap()
        self.dram[name] = t
        self.ddep[name] = Dep(name)
        return t

    def sb(self, name, shape, dt):
        return T(self.es.enter_context(self.nc.sbuf_tensor(name, list(shape), dt)), name)

    def ps(self, name, shape, dt):
        return T(self.es.enter_context(self.nc.psum_tensor(name, list(shape), dt)), name)

    def op(self, e, fn, reads=(), writes=()):
        return self.cx.op(e, fn, [getattr(x, "d", x) for x in reads], [getattr(x, "d", x) for x in writes])

    def load(self, e, dst, dst_ap, src_ap, reads=(), **kw):
        return self.cx.dma(e, dst_ap, src_ap, owner=dst.d, reads=[getattr(x, "d", x) for x in reads], writes=[dst.d], **kw)

    def store(self, e, dst_ap, src, src_ap, ddep=None, is_output=True, **kw):
        w = [ddep] if ddep is not None else []
        return self.cx.dma(e, dst_ap, src_ap, owner=src.d, reads=[src.d], writes=w, is_output=is_output, **kw)

    def mm(self, out, lhsT, rhs, start, stop, reads, writes):
        nc = self.nc
        return self.op("pe", lambda: nc.tensor.matmul(out, lhsT, rhs, start=start, stop=stop), reads, writes)

    def tr(self, out, in_, ident, reads, writes):
        nc = self.nc
        return self.op("pe", lambda: nc.tensor.transpose(out=out, in_=in_, identity=ident), reads, writes)


def dap(t, offset, pat):
    return bass.AP(t.tensor, offset, [list(p) for p in pat])


def build_program(npool=NPOOL_PAGES, stage=99):
    P = Prog(npool, stage)
    nc, cx = P.nc, P.cx
    V, A, G = nc.vector, nc.scalar, nc.gpsimd

    xb = P.din("xb", [2048, D]); xs = P.din("xs", [64, D])
    cache = P.din("cache", [npool * 128, 1024]); ptab = P.din("ptab", [1, 256], I32)
    swin = P.din("swin", [NSB * 512, 512]); spool = P.din("spool", [NSB * 15, 1024])
    sconv = P.din("sconv", [NSB * 2, DFF])
    w_in = P.din("w_in", [D, IN_W]); w_pp = P.din("w_pp", [1024, D]); w_np = P.din("w_np", [1024, D])
    w_out = P.din("w_out", [D, D]); w_up = P.din("w_up", [D, 2 * DFF]); w_down = P.din("w_down", [DFF, D])
    w1k = P.din("w1k", [32, 4096]); w1v = P.din("w1v", [32, 4096])
    w2k = P.din("w2k", [64, 64]); w2v = P.din("w2v", [64, 64])
    pek = P.din("pek", [32, 64]); pev = P.din("pev", [32, 64])
    relb = P.din("relb", [32, 16]); wgrp = P.din("wgrp", [1024, 256])
    g_pm = P.din("g_pm", [1, D]); g_qm = P.din("g_qm", [1, D]); g_pf = P.din("g_pf", [1, D]); g_qf = P.din("g_qf", [1, D])
    pscale = P.din("pscale", [1, 1024]); conv_w = P.din("conv_w", [3, DFF]); conv_b = P.din("conv_b", [1, DFF])
    t_oh = P.din("t_oh", [33, WF]); t_far = P.din("t_far", [1, WF]); t_kb = P.din("t_kb", [1, 2048])
    t_kbc = P.din("t_kbc", [64, 1]); t_bonus = P.din("t_bonus", [1026, 32]); t_sdead = P.din("t_sdead", [128, 32])
    t_rc = P.din("t_rc", [4, 1026]); t_sbonus = P.din("t_sbonus", [1, 33]); t_msame = P.din("t_msame", [16, 16]); t_selb = P.din("t_selb", [64, 256]); t_mi = P.din("t_mi", [16, 4])

    y_o = P.dout("y", [1088, D]); kv_o = P.dout("kvrows", [1088, 1024]); winp_o = P.dout("winp", [512, 512])
    wins_o = P.dout("wins", [NSB * 512, 512]); poolp_o = P.dout("poolp", [15, 1024])
    pools_o = P.dout("pools", [NSB * 15, 1024]); convp_o = P.dout("convp", [2, DFF]); convs_o = P.dout("convs", [NSB * 2, DFF])

    frow = P.dscr("frow", [17, WF]); kvs_scr = P.dscr("kvs_scr", [64, 1536])
    h_scr = P.dscr("h_scr", [1090, D]); y_scr = P.dscr("y_scr", [1090, D])

    B = [P.ps("bank%d" % i, [128, 512], F32) for i in range(8)]
    for b_ in B:
        b_.d.excl = True

    def bfv(bank):
        return bank.t[:].bitcast(BF16)

    ident_b = P.sb("ident_b", [128, 128], BF16); ident_f = P.sb("ident_f", [128, 128], F32)
    Jx = P.sb("Jx", [128, 128], F32)
    Nt = P.sb("Nt", [128, 16, 256], BF16)
    Tfar = P.sb("Tfar", [128, 128], BF16)
    Ns = P.sb("Ns", [16, 4, 132], BF16); Cs = P.sb("Cs", [16, 4, 64], BF16); Tfs = P.sb("Tfs", [16, 128], BF16)
    kbrep = P.sb("kbrep", [128, 16], F32)
    kbc = P.sb("kbc", [64, 1], F32)
    sdead = P.sb("sdead", [128, 32], F32)
    sbonus = P.sb("sbonus", [16, 33], F32)
    gcolA = P.sb("gcolA", [128, 16], F32)
    gcolB = P.sb("gcolB", [128, 16], F32)
    pscol = P.sb("pscol", [128, 8], F32)
    cwcol = P.sb("cwcol", [128, 44, 4], F32)
    grep = P.sb("grep", [128, D], F32)
    wgrp_sb = P.sb("wgrp_sb", [128, 8, 256], BF16)
    w1k_sb = P.sb("w1k_sb", [128, 32, 64], BF16); w1v_sb = P.sb("w1v_sb", [128, 32, 64], BF16)
    w2k_sb = P.sb("w2k_sb", [128, 128], BF16)
    w2v_sb = P.sb("w2v_sb", [128, 64], BF16)
    peT = P.sb("peT", [128, 2, 128], F32)
    kselT = P.sb("kselT", [128, 2, 2048], BF16); kwinT = P.sb("kwinT", [128, 2, 2048], BF16)
    vsel = P.sb("vsel", [128, 16, 256], BF16); vwin = P.sb("vwin", [128, 16, 256], BF16)
    kcT = P.sb("kcT", [128, 4, 64], BF16); vc = P.sb("vc", [64, 4, 64], BF16)
    ksT_new = P.sb("ksT_new", [128, 2, 64], BF16); kwT_new = P.sb("kwT_new", [128, 2, 64], BF16)

    RA = P.sb("RA", [128, 27904], BF16)
    RU = P.sb("RU", [128, 16 * 592], BF16)
    xt = P.sb("xt", [128, D], F32)
    xt2 = P.sb("xt2", [128, D], F32)
    slabs = [P.sb("slab%d" % i, [128, 5632], BF16) for i in range(2)]
    small = P.sb("small", [128, 64], F32)
    ub = P.sb("ub", [128, D], BF16)
    tmpF = [P.sb("tmpF%d" % i, [128, 512], F32) for i in range(2)]
    tmpB = [P.sb("tmpB%d" % i, [128, 512], BF16) for i in range(2)]
    PT = [P.sb("PT%d" % i, [128, 512], BF16) for i in range(2)]
    rk = P.sb("rk", [128, 33 * 33], BF16)

    P.op("pool", lambda: G.memset(ident_f[:], 0.0), writes=[ident_f])
    P.op("pool", lambda: G.affine_select(out=ident_f[:], in_=ident_f[:], pattern=[[-1, 128]], compare_op=ALU.not_equal,
                                         fill=1.0, base=0, channel_multiplier=1), reads=[ident_f], writes=[ident_f])
    P.op("pool", lambda: G.tensor_copy(out=ident_b[:], in_=ident_f[:]), reads=[ident_f], writes=[ident_b])
    P.op("pool", lambda: G.memset(Jx[:], 0.0), writes=[Jx])
    P.op("pool", lambda: G.affine_select(out=Jx[:], in_=Jx[:], pattern=[[1, 128]], compare_op=ALU.not_equal,
                                         fill=1.0, base=-127, channel_multiplier=1), reads=[Jx], writes=[Jx])

    P.load("sp", kbrep, kbrep[:], dap(t_kb, 0, [[0, 128], [128, 16]]), allow_slow_non_contiguous=True)
    P.load("sp", kbc, kbc[:], t_kbc[:, :])
    P.load("sp", sdead, sdead[:], t_sdead[:, :])
    P.load("sp", sbonus, sbonus[:], dap(t_sbonus, 0, [[0, 16], [1, 33]]))
    P.load("sp", gcolA, gcolA[:], dap(g_pm, 0, [[1, 128], [128, 16]]), allow_slow_non_contiguous=True)
    P.load("sp", gcolB, gcolB[:], dap(g_pf, 0, [[1, 128], [128, 16]]), allow_slow_non_contiguous=True)
    P.load("sp", pscol, pscol[:], dap(pscale, 0, [[1, 128], [128, 8]]), allow_slow_non_contiguous=True)
    for j in range(3):
        P.load("sp", cwcol, cwcol[:, :, j], dap(conv_w, j * DFF, [[1, 128], [128, 44]]), group=(j > 0),
               allow_slow_non_contiguous=True)
    P.load("sp", cwcol, cwcol[:, :, 3], dap(conv_b, 0, [[1, 128], [128, 44]]), group=True, allow_slow_non_contiguous=True)
    P.load("pool", wgrp_sb, wgrp_sb[:], dap(wgrp, 0, [[256, 128], [128 * 256, 8], [1, 256]]))
    for hf in range(2):
        P.load("pool", w1k_sb, w1k_sb[64 * hf:64 * hf + 64, :, :], dap(w1k, 0, [[64, 64], [4096, 32], [1, 64]]), group=(hf > 0))
        P.load("pool", w1v_sb, w1v_sb[64 * hf:64 * hf + 64, :, :], dap(w1v, 0, [[64, 64], [4096, 32], [1, 64]]), group=(hf > 0))
        P.load("pool", w2k_sb, w2k_sb[64 * hf:64 * hf + 64, :].rearrange("p (a b) -> p a b", a=2),
               dap(w2k, 0, [[64, 64], [0, 2], [1, 64]]), group=(hf > 0))
        P.load("pool", w2v_sb, w2v_sb[64 * hf:64 * hf + 64, :], w2v[:, :], group=(hf > 0))

    if stage <= 1:
        return P, locals()
    pe_sb = P.sb("pe_sb", [32, 128], F32)
    P.load("sp", pe_sb, pe_sb[:, 0:64], pek[:, :])
    P.load("sp", pe_sb, pe_sb[:, 64:128], pev[:, :], group=True)
    for s_ in range(2):
        P.tr(B[0][0:64, 32 * s_:32 * s_ + 32], pe_sb[:, 64 * s_:64 * s_ + 64], ident_f[0:32, 0:32], [pe_sb, ident_f], [B[0]])
    for s_ in range(2):
        src = B[0][0:64, 32 * s_:32 * s_ + 32]
        for hf in range(2):
            P.op("dve", lambda hf=hf, s_=s_, src=src: V.tensor_copy(
                out=peT[64 * hf:64 * hf + 64, s_, :].rearrange("p (r l) -> p r l", r=4),
                in_=bass.AP(src.tensor, src.offset, [list(src.ap[0]), [0, 4], [1, 32]])), [B[0]], [peT])

    if stage <= 2:
        return P, locals()
    rbext = P.sb("rbext", [33, 16], F32)
    P.load("sp", rbext, rbext[0:32, :], relb[:, :])
    P.op("pool", lambda: G.memset(rbext[32:33, :], NEG), writes=[rbext])
    fdep = P.ddep["frow"]
    for ch in range(9):
        n = min(512, WF - ch * 512)
        P.load("sp", tmpF[0], tmpF[0][0:33, 0:n], t_oh[:, ch * 512:ch * 512 + n])
        P.mm(B[1][0:16, 0:n], rbext[0:33, 0:16], tmpF[0][0:33, 0:n], True, True, [rbext, tmpF[0]], [B[1]])
        P.op("dve", lambda n=n: V.tensor_copy(out=tmpF[1][0:16, 0:n], in_=B[1][0:16, 0:n]), [B[1]], [tmpF[1]])
        P.store("sp", frow[0:16, ch * 512:ch * 512 + n], tmpF[1], tmpF[1][0:16, 0:n], ddep=fdep, is_output=False)
    cx.dma("sp", frow[16:17, :], t_far[:, :], owner=P.ddep["t_far"], writes=[fdep])

    if stage <= 3:
        return P, locals()
    Hk = View(xt2.t[:, :].rearrange("p (a b) -> p a b", a=16), xt2.d)
    for blk, c0 in ((0, F0 + 1), (1, F0 - 127)):
        P.load("sp", Hk, Hk[:], dap(frow, c0, [[1, 128], [WF, 16], [1, 128]]), reads=[fdep])
        for h4 in range(4):
            bk = B[2 + (h4 % 2)]
            for i in range(4):
                P.mm(bk[:, 128 * i:128 * i + 128], Hk[:, 4 * h4 + i, :], Jx[:], True, True, [Hk, Jx], [bk])
            P.op("dve", lambda bk=bk, h4=h4, blk=blk: V.tensor_copy(
                out=Nt[:, 4 * h4:4 * h4 + 4, 128 * blk:128 * blk + 128],
                in_=bk[:, :].rearrange("p (a b) -> p a b", a=4)), [bk], [Nt])
    P.load("sp", Hk, Hk[:, 0, :], dap(frow, 16 * WF + F0 + 385, [[1, 128], [1, 128]]), reads=[fdep])
    P.mm(B[2][:, 0:128], Hk[:, 0, :], Jx[:], True, True, [Hk, Jx], [B[2]])
    P.op("dve", lambda: V.tensor_copy(out=Tfar[:], in_=B[2][:, 0:128]), [B[2]], [Tfar])
    if stage <= 4:
        return P, locals()
    for g in range(4):
        for blk, c0, ny in ((0, F0 + 1, 128), (1, F0 - 127, 4)):
            P.load("sp", Hk, Hk[:, 0, 0:16].rearrange("p (a b) -> p a b", a=4),
                   dap(frow, 4 * g * WF + c0, [[1, 128], [WF, 4], [1, 4]]), reads=[fdep], allow_slow_non_contiguous=True)
            P.mm(B[2][0:16, 0:ny], Hk[:, 0, 0:16], Jx[:, 0:ny], True, True, [Hk, Jx], [B[2]])
            P.op("dve", lambda g=g, blk=blk, ny=ny: V.tensor_copy(out=Ns[:, g, 128 * blk:128 * blk + ny], in_=B[2][0:16, 0:ny]),
                 [B[2]], [Ns])
        P.load("sp", Hk, Hk[0:64, 0, 0:16].rearrange("p (a b) -> p a b", a=4),
               dap(frow, 4 * g * WF + F0 + 1, [[32, 64], [WF, 4], [1, 4]]), reads=[fdep], allow_slow_non_contiguous=True)
        P.mm(B[2][0:16, 0:64], Hk[0:64, 0, 0:16], Jx[0:64, 64:128], True, True, [Hk, Jx], [B[2]])
        P.op("dve", lambda g=g: V.tensor_copy(out=Cs[:, g, :], in_=B[2][0:16, 0:64]), [B[2]], [Cs])
    P.load("sp", Hk, Hk[:, 0, 0:16].rearrange("p (a b) -> p a b", a=4),
           dap(frow, 16 * WF + F0 + 385, [[1, 128], [0, 4], [1, 4]]), reads=[fdep], allow_slow_non_contiguous=True)
    P.mm(B[2][0:16, 0:128], Hk[:, 0, 0:16], Jx[:], True, True, [Hk, Jx], [B[2]])
    P.op("dve", lambda: V.tensor_copy(out=Tfs[:], in_=B[2][0:16, 0:128]), [B[2]], [Tfs])

    if stage <= 5:
        return P, locals()
    def bc(ap, n):
        return bass.AP(ap.tensor, ap.offset, [list(p) for p in ap.ap] + [[0, n]])

    def bcmid(ap, n):
        pat = [list(p) for p in ap.ap]
        return bass.AP(ap.tensor, ap.offset, [pat[0], [0, n]] + pat[1:])

    def norm_T(x_ap, n, dst3, gcol, bkA, bkB, x_reads=(), xt=xt):
        P.load("sp", xt, xt[0:n, :], x_ap, reads=list(x_reads))
        P.op("act", lambda: A.activation(out=ub[0:n, :], in_=xt[0:n, :], func=AF.Square, accum_out=small[0:n, 0:1]),
             [xt], [ub, small])
        P.op("dve", lambda: V.tensor_scalar(out=small[0:n, 1:2], in0=small[0:n, 0:1], scalar1=1.0 / D, scalar2=EPS,
                                            op0=ALU.mult, op1=ALU.add), [small], [small])
        P.op("act", lambda: A.activation(out=small[0:n, 3:4], in_=small[0:n, 1:2], func=AF.Sqrt), [small], [small])
        P.op("dve", lambda: V.reciprocal(out=small[0:n, 2:3], in_=small[0:n, 3:4]), [small], [small])
        P.op("dve", lambda: V.tensor_scalar(out=ub[0:n, :], in0=xt[0:n, :], scalar1=small[0:n, 2:3], scalar2=None,
                                            op0=ALU.mult), [xt, small], [ub])
        for half in range(2):
            bk = (bkA, bkB)[half]
            bv = bfv(bk)
            for i in range(8):
                kc = 8 * half + i
                P.tr(bv[:, 128 * i:128 * i + n], ub[0:n, 128 * kc:128 * kc + 128], ident_b[0:n, 0:n], [ub, ident_b], [bk])
            P.op("dve", lambda bv=bv, half=half: V.tensor_tensor(
                out=dst3[:, 8 * half:8 * half + 8, 0:n],
                in0=bv[:, :].rearrange("p (a b) -> p a b", a=8)[:, :, 0:n],
                in1=bc(gcol[:, 8 * half:8 * half + 8], n), op=ALU.mult), [bk, gcol], [dst3_dep[0]])

    dst3_dep = [None]

    def wslab_load(sl, w_ap_t, row0, nk, col0, ncols, row_stride):
        v = sl.t[:, 0:nk * ncols].rearrange("p (a b) -> p a b", a=nk)
        P.load("pool", sl, v, dap(w_ap_t, row0 * row_stride + col0, [[row_stride, 128], [128 * row_stride, nk], [1, ncols]]))
        return v

    wkv = RA.t[:, 0:24576].rearrange("p (a b) -> p a b", a=16)
    P.load("pool", RA, wkv, dap(w_in, 2048, [[IN_W, 128], [128 * IN_W, 16], [1, 1536]]))
    kvt = RA.t[:, 24576:24576 + 3072].bitcast(F32)
    kvt_d = Dep("kvt")
    kvb2 = [slabs[1].t[:, 0:1536], slabs[1].t[:, 1536:3072]]
    kvb_d2 = [Dep("kvb0"), Dep("kvb1")]
    uTk2 = [slabs[0].t[:, 0:2048].rearrange("p (a b) -> p a b", a=16), slabs[0].t[:, 2048:4096].rearrange("p (a b) -> p a b", a=16)]
    uTk_d2 = [Dep("uTk0"), Dep("uTk1")]
    kcmpT = RU.t[:, 0:4096].rearrange("p (a b) -> p a b", a=2)
    vcmpT = RU.t[:, 4096:8192].rearrange("p (a b) -> p a b", a=2)
    kcmp_d = Dep("kcmpT"); vcmp_d = Dep("vcmpT")
    kvsd = P.ddep["kvs_scr"]


    if stage <= 6:
        return P, locals()
    KD = int(os.environ.get('KDBG', '99'))
    for a in (range(17) if KD >= 9 else range(16, 17) if KD == 8 else range(1)):
        n = 128 if a < 16 else 64
        own = a >= 8
        kvb, kvb_d, uTk, uTk_d = kvb2[a % 2], kvb_d2[a % 2], uTk2[a % 2], uTk_d2[a % 2]
        dst3_dep[0] = uTk_d
        norm_T(xb[128 * a:128 * a + 128, :] if a < 16 else xs[:, :], n, uTk, gcolA, B[0], B[1], xt=(xt if a % 2 == 0 else xt2))
        if KD < 2:
            continue
        for cg in range(3):
            bk = B[4 + cg]
            for kc in range(16):
                P.mm(bk[0:n, :], uTk[:, kc, 0:n], wkv[:, kc, 512 * cg:512 * cg + 512], kc == 0, kc == 15, [uTk_d, RA], [bk])
            KS = int(os.environ.get('KSUB', '3'))
            if KS >= 1:
                P.op("act", lambda bk=bk, cg=cg, n=n: A.copy(out=kvt[0:n, 512 * cg:512 * cg + 512], in_=bk[0:n, :]), [bk], [kvt_d])
            if KS >= 2:
                P.op("dve", lambda bk=bk, cg=cg, n=n: V.tensor_copy(out=kvb[0:n, 512 * cg:512 * cg + 512], in_=bk[0:n, :]), [bk], [kvb_d])
        if KD < 3:
            continue
        if own:
            r0 = 128 * (a - 8) if a < 16 else 1024
            cx.dma("sp", kv_o[r0:r0 + n, :], kvt[0:n, 0:1024], owner=kvt_d, reads=[kvt_d], is_output=True)
        if 12 <= a < 16:
            cx.dma("sp", winp_o[128 * (a - 12):128 * (a - 12) + 128, :], kvt[:, 1024:1536], owner=kvt_d, reads=[kvt_d],
                   is_output=True, group=True)
        if a == 16:
            cx.dma("sp", kvs_scr[:, :], kvt[0:64, :], owner=kvsd, reads=[kvt_d], writes=[kvsd])
            cx.dma("sp", dap(wins_o, 508 * 512, [[512 * 512, NSB], [512, 4], [1, 512]]),
                   dap(kvs_scr, 1024, [[4 * 1536, NSB], [1536, 4], [1, 512]]), owner=P.ddep["wins"], reads=[kvsd], is_output=True)
        if a < 16:
            P.op("pool", lambda a=a: G.tensor_copy(out=vsel[:, a, :], in_=kvb[:, 768:1024]), [kvb_d], [vsel])
            P.op("pool", lambda a=a: G.tensor_copy(out=vwin[:, a, :], in_=kvb[:, 1280:1536]), [kvb_d], [vwin])
        if KD < 4:
            continue
        bv = bfv(B[7])
        for si, s_ in enumerate((0, 1, 2, 4)):
            for gp in range(2):
                i = 2 * si + gp
                P.tr(bv[:, 128 * i:128 * i + n], kvb[0:n, 256 * s_ + 128 * gp:256 * s_ + 128 * gp + 128], ident_b[0:n, 0:n],
                     [kvb_d, ident_b], [B[7]])
        def pview(si, n=n, bv=bv):
            return bv[:, 256 * si:256 * si + 256].rearrange("p (a b) -> p a b", a=2)[:, :, 0:n]
        if a < 16:
            P.op("dve", lambda a=a: V.tensor_tensor(out=kcmpT[:, :, 128 * a:128 * a + 128], in0=pview(0),
                                                    in1=bcmid(peT[:, 0, :], 2), op=ALU.add), [B[7], peT], [kcmp_d])
            P.op("dve", lambda a=a: V.tensor_tensor(out=vcmpT[:, :, 128 * a:128 * a + 128], in0=pview(1),
                                                    in1=bcmid(peT[:, 1, :], 2), op=ALU.add), [B[7], peT], [vcmp_d])
            P.op("act", lambda a=a: A.copy(out=kselT[:, :, 128 * a:128 * a + 128], in_=pview(2)), [B[7]], [kselT])
            P.op("act", lambda a=a: A.copy(out=kwinT[:, :, 128 * a:128 * a + 128], in_=pview(3)), [B[7]], [kwinT])
        else:
            P.op("act", lambda: A.copy(out=ksT_new[:, :, :], in_=pview(2)), [B[7]], [ksT_new])
            P.op("act", lambda: A.copy(out=kwT_new[:, :, :], in_=pview(3)), [B[7]], [kwT_new])

    if stage <= 7:
        return P, locals()
    P.op("pool", lambda: G.memset(small[0:1, 61:62], 0.0), [], kvb_d2 + uTk_d2 + [slabs[0].d, slabs[1].d, small.d])
    hidT = tmpB[0].t[0:64, 0:256].rearrange("p (a b) -> p a b", a=2)

    def compress(srcT, src_d, w1sb, is_k, out_t, nblk=64):
        for gpar in range(2):
            hb = B[2 + gpar]
            hp = hb[0:64, 256:256 + 2 * nblk]
            for l in range(32):
                rhs = srcT[64 * gpar:64 * gpar + 64, :, l:32 * nblk:32]
                P.mm(hp, w1sb[64 * gpar:64 * gpar + 64, l, :], rhs, l == 0, l == 31, [src_d, w1sb], [hb])
        for gpar in range(2):
            hb = B[2 + gpar]
            hp = hb[0:64, 256:256 + 2 * nblk]
            P.op("act", lambda gpar=gpar, hp=hp: A.activation(
                out=hidT[:, gpar, 0:2 * nblk], in_=hp, func=AF.Gelu_apprx_tanh), [hb], [tmpB[0]])
        for g in range(4):
            gpar, gp = g % 2, g // 2
            hsl = hidT[:, gpar, nblk * gp:nblk * gp + nblk]
            if is_k:
                P.mm(B[3][:, 64 * g:64 * g + nblk], w2k_sb[0:64, :], hsl, True, True, [tmpB[0], w2k_sb], [B[3]])
            else:
                P.mm(B[3][0:nblk, 64 * g:64 * g + 64], hsl, w2v_sb[0:64, :], True, True, [tmpB[0], w2v_sb], [B[3]])
        if is_k:
            P.op("dve", lambda: V.tensor_copy(out=out_t[:, :, 0:nblk],
                                              in_=B[3][:, 0:256].rearrange("p (a b) -> p a b", a=4)[:, :, 0:nblk]), [B[3]], [out_t])
        else:
            P.op("dve", lambda: V.tensor_copy(out=out_t[0:nblk, :, :],
                                              in_=B[3][0:nblk, 0:256].rearrange("p (a b) -> p a b", a=4)), [B[3]], [out_t])

    compress(kcmpT, kcmp_d, w1k_sb, True, kcT)
    compress(vcmpT, vcmp_d, w1v_sb, False, vc)

    if stage <= 8:
        return P, locals()
    dA, dO, dQ, dM = Dep("rA"), Dep("rO"), Dep("rQ"), Dep("rM")
    RAall = [dA, dO, dQ, dM]
    gates = P.sb("gates", [128, 5, 48], F32)
    ghist = P.sb("ghist", [128, 44, 2], F32)
    shist = P.sb("shist", [128, 44, 32], BF16)
    oS = P.sb("oS", [128, 8, 64], BF16)
    poolst = P.sb("poolst", [128, 8, 15], F32)

    def splits(N):
        if N <= 512:
            return [(0, N)]
        h = (N + 1) // 2
        return [(0, h), (h, N - h)]

    def rmsnorm_stats(src, n):
        P.op("act", lambda: A.activation(out=ub[0:n, :], in_=src[0:n, :], func=AF.Square, accum_out=small[0:n, 0:1]),
             [src], [ub, small])
        P.op("dve", lambda: V.tensor_scalar(out=small[0:n, 1:2], in0=small[0:n, 0:1], scalar1=1.0 / D, scalar2=EPS,
                                            op0=ALU.mult, op1=ALU.add), [small], [small])
        P.op("act", lambda: A.activation(out=small[0:n, 3:4], in_=small[0:n, 1:2], func=AF.Sqrt), [small], [small])
        P.op("dve", lambda: V.reciprocal(out=small[0:n, 2:3], in_=small[0:n, 3:4]), [small], [small])

    def transpose_to(src_bf, n, dst3, c0, gcol, dstdep):
        for half in range(2):
            bk = B[half]
            bv = bfv(bk)
            for i in range(8):
                kc = 8 * half + i
                P.tr(bv[:, 128 * i:128 * i + n], src_bf[0:n, 128 * kc:128 * kc + 128], ident_b[0:n, 0:n], [ub, ident_b], [bk])
            P.op("dve", lambda bv=bv, half=half: V.tensor_tensor(
                out=dst3[:, 8 * half:8 * half + 8, c0:c0 + n],
                in0=bv[:, :].rearrange("p (a b) -> p a b", a=8)[:, :, 0:n],
                in1=bc(gcol[:, 8 * half:8 * half + 8], n), op=ALU.mult), [bk, gcol], [dstdep])

    slab_i = [0]

    def next_slab():
        slab_i[0] ^= 1
        return slabs[slab_i[0]]

    def fm_linear(sl, wv, nk, rhs3, rdeps, c0, N, nchunks, evac):
        for oc in range(nchunks):
            for si, (s0, sn) in enumerate(splits(N)):
                bk = B[fm_linear.rot % 8]
                fm_linear.rot += 1
                for kc in range(nk):
                    P.mm(bk[:, 0:sn], wv[:, kc, 128 * oc:128 * oc + 128], rhs3[:, kc, c0 + s0:c0 + s0 + sn],
                         kc == 0, kc == nk - 1, [sl] + list(rdeps), [bk])
                evac(oc, s0, sn, bk[:, 0:sn], bk)
    fm_linear.rot = 0

    def run_pass(pi):
        if pi == 0:
            units = [("h", 1022, 2, 15)] + [("p", 1024 + 128 * j, 128, 17 + 128 * j) for j in range(4)]
            NU, c0, NC, NCp = 529, 15, 514, 514
            yrow0 = -2
        else:
            units = [("p", 1536 + 128 * j, 128, 128 * j) for j in range(4)] + [("s", 0, 64, 512)]
            NU, c0, NC, NCp = 576, 0, 576, 512
            yrow0 = 512
        uT = RU.t[:, 0:16 * NU].rearrange("p (a b) -> p a b", a=16)
        dU = RU.d
        dst3_dep[0] = dU
        if pi == 0:
            norm_T(xb[1007:1022, :], 15, uT, gcolA, B[0], B[1])
        for (kind, r0, n, cc) in units:
            src = xs[:, :] if kind == "s" else xb[r0:r0 + n, :]
            norm_T_at(src, n, uT, cc)
        if pi == 0:
            NPI = 529
            PI = RA.t[:, 13824:13824 + 2 * 8 * NPI].bitcast(F32).rearrange("p (a b) -> p a b", a=8)
            PIs = None
        else:
            NPI = 527 + 304
            PI = RA.t[:, 13824:13824 + 2 * 8 * NPI].bitcast(F32).rearrange("p (a b) -> p a b", a=8)
            PIs = PI[:, :, 527:831].rearrange("p a (b t) -> p a b t", b=16)
            P.op("dve", lambda: V.tensor_copy(out=PI[:, :, 0:15], in_=poolst[:, :, :]), [poolst], [dM])
            for hb in range(2):
                P.load("sp", xt, xt[0:120, 0:1024], spool[120 * hb:120 * hb + 120, :])
                for oc in range(8):
                    P.tr(B[2][:, 128 * (oc % 4):128 * (oc % 4) + 120], xt[0:120, 128 * oc:128 * oc + 128], ident_f[0:120, 0:120],
                         [xt, ident_f], [B[2]])
                    if oc % 4 == 3:
                        o4 = oc - 3
                        P.op("dve", lambda hb=hb, o4=o4: V.tensor_copy(
                            out=PIs[:, o4:o4 + 4, 8 * hb:8 * hb + 8, 0:15],
                            in_=B[2][:, :].rearrange("p (a b) -> p a b", a=4)[:, :, 0:120].rearrange("p a (b t) -> p a b t", b=8)),
                            [B[2]], [dM])
        pooled = RA.t[:, 9216:9216 + 8 * NC].rearrange("p (a b) -> p a b", a=8)
        A_T = RA.t[:, 0:8 * NC].rearrange("p (a b) -> p a b", a=8)
        o_T = RA.t[:, 4608:4608 + 8 * NC].rearrange("p (a b) -> p a b", a=8)
        q_T = RA.t[:, 9216:9216 + 8 * NC].rearrange("p (a b) -> p a b", a=8)
        mrg = RA.t[:, 13824:13824 + 16 * NC].rearrange("p (a b) -> p a b", a=16)

        def ev_pool(base_oc):
            def f(oc, s0, sn, ps, bk):
                o = base_oc + oc
                if pi == 0:
                    P.op("act", lambda: A.copy(out=PI[:, o, s0:s0 + sn], in_=ps), [bk], [dM])
                else:
                    pe_ = min(s0 + sn, 512)
                    if s0 < 512:
                        P.op("act", lambda: A.copy(out=PI[:, o, 15 + s0:15 + pe_], in_=ps[:, 0:pe_ - s0]), [bk], [dM])
                    if s0 + sn > 512:
                        a0 = max(s0, 512)
                        P.op("act", lambda: A.copy(out=PIs[:, o, (a0 - 512) // 4:16, 15:19],
                                                   in_=ps[:, a0 - s0:sn].rearrange("p (b t) -> p b t", t=4)), [bk], [dM])
            return f

        Nlin = NU if pi == 0 else NC
        for sidx in range(4):
            sl = next_slab()
            wv = wslab_load(sl, w_in, 0, 16, 256 * sidx, 256, IN_W)
            fm_linear(sl, wv, 16, uT, [dU], 0, Nlin, 2, ev_pool(2 * sidx))
        if pi == 0:
            P.op("dve", lambda: V.tensor_copy(out=poolst[:, :, :], in_=PI[:, :, NPI - 15:NPI]), [dM], [poolst])
        else:
            stg = xt2
            for oc in range(8):
                P.tr(B[3][0:15, 128 * (oc % 4):128 * (oc % 4) + 128], PI[:, oc, 512:527], ident_f[:, :], [dM, ident_f], [B[3]])
                if oc % 4 == 3:
                    P.op("dve", lambda oc=oc: V.tensor_copy(out=stg[0:15, 128 * (oc - 3):128 * (oc - 3) + 512], in_=B[3][0:15, :]), [B[3]], [stg])
            P.store("sp", poolp_o[:, :], stg, stg[0:15, 0:1024])
            for oc in range(8):
                P.op("dve", lambda oc=oc: V.tensor_copy(out=tmpF[0][:, 0:64].rearrange("p (b t) -> p b t", t=4), in_=PIs[:, oc, :, 15:19]), [dM], [tmpF[0]])
                P.tr(B[3][0:64, 128 * (oc % 4):128 * (oc % 4) + 128], tmpF[0][:, 0:64], ident_f[:, :], [tmpF[0], ident_f], [B[3]])
                if oc % 4 == 3:
                    P.op("dve", lambda oc=oc: V.tensor_copy(out=stg[0:64, 1024 + 128 * (oc - 3):1024 + 128 * (oc - 3) + 512], in_=B[3][0:64, :]), [B[3]], [stg])
            for b_ in range(NSB):
                P.store("sp", pools_o[15 * b_ + 11:15 * b_ + 15, :], stg, stg[4 * b_:4 * b_ + 4, 1024:2048], group=(b_ > 0))
        t1 = RA.t[:, 0:2 * 2 * NPI].bitcast(F32).rearrange("p (a b) -> p a b", a=2)
        t2 = RA.t[:, 4608:4608 + 2 * 2 * NPI].bitcast(F32).rearrange("p (a b) -> p a b", a=2)
        segs = [(0, 15 + NCp, 0, NCp)]
        if pi == 0:
            segs = [(0, NPI, 0, NC)]
        for gi in range(4):
            w = 2 << gi
            P.load("sp", xt, xt[:, 0:NCp], dap(t_rc, gi * 1026 + 514 * pi, [[0, 128], [1, NCp]]))
            X = PI[:, 2 * gi:2 * gi + 2, :]
            cur, curd = X, dM
            for k in range(gi + 1):
                sh = 1 << k
                nxt, nxtd = (t1, dA) if k % 2 == 0 else (t2, dO)
                def add_seg(lo, L):
                    P.op("dve", lambda lo=lo, L=L, cur=cur, nxt=nxt, sh=sh: V.tensor_tensor(
                        out=nxt[:, :, lo + sh:lo + L], in0=cur[:, :, lo + sh:lo + L], in1=cur[:, :, lo:lo + L - sh], op=ALU.add),
                        [curd], [nxtd])
                add_seg(0, segs[0][1])
                if pi == 1:
                    P.op("dve", lambda cur=cur, nxt=nxt, sh=sh: V.tensor_tensor(
                        out=nxt[:, :, 527:831].rearrange("p a (b t) -> p a b t", b=16)[:, :, :, sh:19],
                        in0=cur[:, :, 527:831].rearrange("p a (b t) -> p a b t", b=16)[:, :, :, sh:19],
                        in1=cur[:, :, 527:831].rearrange("p a (b t) -> p a b t", b=16)[:, :, :, 0:19 - sh], op=ALU.add),
                        [curd], [nxtd])
                cur, curd = nxt, nxtd
            st, L, po, nt = segs[0]
            other, otherd = (t2, dO) if cur is t1 else (t1, dA)
            P.op("dve", lambda cur=cur, other=other, L=L, nt=nt: V.tensor_tensor(
                out=other[:, :, L - nt:L], in0=cur[:, :, L - nt:L], in1=bcmid(xt[:, 0:nt], 2), op=ALU.mult),
                [curd, xt], [otherd])
            P.op("dve", lambda other=other, X=X, L=L, nt=nt, gi=gi: V.tensor_tensor(
                out=pooled[:, 2 * gi:2 * gi + 2, 0:nt], in0=other[:, :, L - nt:L], in1=X[:, :, L - nt:L], op=ALU.subtract),
                [otherd, dM], [dQ])
            if pi == 1:
                for a_ in range(2):
                    P.op("dve", lambda cur=cur, X=X, gi=gi, w=w, a_=a_: V.scalar_tensor_tensor(
                        out=pooled[:, 2 * gi + a_, 512:576].rearrange("p (b t) -> p b t", b=16),
                        in0=cur[:, a_, 527:831].rearrange("p (b t) -> p b t", b=16)[:, :, 15:19], scalar=1.0 / w,
                        in1=X[:, a_, 527:831].rearrange("p (b t) -> p b t", b=16)[:, :, 15:19],
                        op0=ALU.mult, op1=ALU.subtract), [curd, dM], [dQ])
        for gi in range(4):
            for j in range(2):
                for (s0, sn) in splits(NC):
                    bk = B[fm_linear.rot % 8]; fm_linear.rot += 1
                    for k2 in range(2):
                        P.mm(bk[:, 0:sn], wgrp_sb[:, 2 * gi + k2, 128 * j:128 * j + 128], pooled[:, 2 * gi + k2, s0:s0 + sn],
                             k2 == 0, k2 == 1, [wgrp_sb, dQ], [bk])
                    P.op("dve", lambda bk=bk, gi=gi, j=j, s0=s0, sn=sn: V.tensor_scalar(
                        out=A_T[:, 2 * gi + j, s0:s0 + sn], in0=bk[:, 0:sn], scalar1=pscol[:, 2 * gi + j:2 * gi + j + 1],
                        scalar2=None, op0=ALU.mult), [bk, pscol], [dA])
        return units, uT, dU, NU, c0, NC, NCp, A_T, o_T, q_T, mrg, yrow0

    def norm_T_at(src, n, uT, cc):
        norm_T(src, n, uT[:, :, cc:], gcolA, B[0], B[1])

    pass_state = run_pass(0)

    def pass_qg(pi, st):
        units, uT, dU, NU, c0, NC, NCp, A_T, o_T, q_T, mrg, yrow0 = st
        for gp in range(2):
            for ip in range(2):
                sl = next_slab()
                wv = sl.t[:, 0:4096].rearrange("p (a b) -> p a b", a=16)
                for ii in range(2):
                    i = 2 * ip + ii
                    for hh in range(2):
                        P.load("pool", sl, wv[:, :, 128 * ii + 64 * hh:128 * ii + 64 * hh + 64],
                               dap(w_in, 1024 + (8 * gp + i) * 64 + 256 * hh, [[IN_W, 128], [128 * IN_W, 16], [1, 64]]), group=(ii + hh > 0))
                def ev_q(oc, s0, sn, ps, bk, gp=gp, ip=ip):
                    ch = 4 * gp + 2 * ip + oc
                    P.op("act", lambda: A.copy(out=q_T[:, ch, s0:s0 + sn], in_=ps), [bk], [dQ])
                fm_linear(sl, wv, 16, uT, [dU], c0, NC, 2, ev_q)
        sl = next_slab()
        wv = wslab_load(sl, w_in, 0, 16, 3584, 48, IN_W)
        for ui, (kind, r0, n, cc) in enumerate(units):
            bk = B[fm_linear.rot % 8]; fm_linear.rot += 1
            for kc in range(16):
                P.mm(bk[0:n, 0:48], uT[:, kc, cc:cc + n], wv[:, kc, :], kc == 0, kc == 15, [dU, sl], [bk])
            P.op("act", lambda bk=bk, ui=ui, n=n: A.activation(out=gates[0:n, ui, :], in_=bk[0:n, 0:48], func=AF.Sigmoid), [bk], [gates])

    def pass_merge_ffn(pi, st):
        units, uT, dU, NU, c0, NC, NCp, A_T, o_T, q_T, mrg, yrow0 = st
        for j in range(16):
            s1 = next_slab()
            wg = s1.t[:, 0:4096].rearrange("p (a b) -> p a b", a=16)
            P.load("pool", s1, wg[:, :, 0:128], dap(w_in, 3632 + 128 * j, [[IN_W, 128], [128 * IN_W, 16], [1, 128]]))
            P.load("pool", s1, wg[:, :, 128:256], dap(w_in, 3632 + 2048 + 128 * j, [[IN_W, 128], [128 * IN_W, 16], [1, 128]]), group=True)
            s2 = next_slab()
            wp = s2.t[:, 0:2048].rearrange("p (a b) -> p a b", a=8)
            P.load("pool", s2, wp[:, :, 0:128], dap(w_pp, 128 * j, [[D, 128], [128 * D, 8], [1, 128]]))
            P.load("pool", s2, wp[:, :, 128:256], dap(w_np, 128 * j, [[D, 128], [128 * D, 8], [1, 128]]), group=True)
            for si, (s0, sn) in enumerate(splits(NC)):
                bga, bgb, bp1, bp2 = B[4 * si], B[4 * si + 1], B[4 * si + 2], B[4 * si + 3]
                for kc in range(16):
                    P.mm(bga[:, 0:sn], wg[:, kc, 0:128], uT[:, kc, c0 + s0:c0 + s0 + sn], kc == 0, kc == 15, [s1, dU], [bga])
                for kc in range(16):
                    P.mm(bgb[:, 0:sn], wg[:, kc, 128:256], uT[:, kc, c0 + s0:c0 + s0 + sn], kc == 0, kc == 15, [s1, dU], [bgb])
                for kc in range(8):
                    P.mm(bp1[:, 0:sn], wp[:, kc, 0:128], A_T[:, kc, s0:s0 + sn], kc == 0, kc == 7, [s2, dA], [bp1])
                for kc in range(8):
                    P.mm(bp2[:, 0:sn], wp[:, kc, 128:256], o_T[:, kc, s0:s0 + sn], kc == 0, kc == 7, [s2, dO], [bp2])
                P.op("act", lambda: A.activation(out=tmpF[0][:, 0:sn], in_=bga[:, 0:sn], func=AF.Sigmoid), [bga], [tmpF[0]])
                P.op("act", lambda: A.activation(out=tmpF[1][:, 0:sn], in_=bgb[:, 0:sn], func=AF.Sigmoid), [bgb], [tmpF[1]])
                P.op("dve", lambda: V.tensor_tensor(out=tmpF[0][:, 0:sn], in0=bp1[:, 0:sn], in1=tmpF[0][:, 0:sn], op=ALU.mult),
                     [bp1, tmpF[0]], [tmpF[0]])
                P.op("dve", lambda: V.tensor_tensor(out=tmpF[1][:, 0:sn], in0=bp2[:, 0:sn], in1=tmpF[1][:, 0:sn], op=ALU.mult),
                     [bp2, tmpF[1]], [tmpF[1]])
                P.op("dve", lambda: V.tensor_tensor(out=mrg[:, j, s0:s0 + sn], in0=tmpF[0][:, 0:sn], in1=tmpF[1][:, 0:sn], op=ALU.add),
                     [tmpF[0], tmpF[1]], [dM])
        ysd = P.ddep["y_scr"]; hsd = P.ddep["h_scr"]
        P.load("sp", grep, grep[:], dap(g_qm, 0, [[0, 128], [1, D]]))
        for s_ in range(8):
            sl = next_slab()
            wv = wslab_load(sl, w_out, 0, 16, 256 * s_, 256, D)
            for ui, (kind, r0, n, cc) in enumerate(units):
                bk = B[fm_linear.rot % 8]; fm_linear.rot += 1
                for kc in range(16):
                    P.mm(bk[0:n, 0:256], mrg[:, kc, cc - c0:cc - c0 + n], wv[:, kc, :], kc == 0, kc == 15, [dM, sl], [bk])
                tf = tmpF[ui % 2]
                P.op("act", lambda bk=bk, tf=tf, n=n: A.copy(out=tf[0:n, 0:256], in_=bk[0:n, 0:256]), [bk], [tf])
                cx.dma("sp", y_scr[cc:cc + n, 256 * s_:256 * s_ + 256], tf[0:n, 0:256], owner=tf.d, reads=[tf.d], writes=[ysd])
        nT = uT
        for ui, (kind, r0, n, cc) in enumerate(units):
            P.load("sp", xt, xt[0:n, :], y_scr[cc:cc + n, :], reads=[ysd])
            P.load("sp", xt2, xt2[0:n, :], xs[:, :] if kind == "s" else xb[r0:r0 + n, :])
            rmsnorm_stats(xt, n)
            P.op("dve", lambda n=n: V.scalar_tensor_tensor(out=xt[0:n, :], in0=xt[0:n, :], scalar=small[0:n, 2:3], in1=grep[0:n, :],
                                                          op0=ALU.mult, op1=ALU.mult), [xt, small, grep], [xt])
            P.op("dve", lambda n=n: V.tensor_tensor(out=xt2[0:n, :], in0=xt2[0:n, :], in1=xt[0:n, :], op=ALU.add), [xt, xt2], [xt2])
            cx.dma("sp", h_scr[cc:cc + n, :], xt2[0:n, :], owner=xt2.d, reads=[xt2.d], writes=[hsd])
            rmsnorm_stats(xt2, n)
            P.op("dve", lambda n=n: V.tensor_scalar(out=ub[0:n, :], in0=xt2[0:n, :], scalar1=small[0:n, 2:3], scalar2=None,
                                                    op0=ALU.mult), [xt2, small], [ub])
            transpose_to(ub, n, nT, cc, gcolB, dU)
        actT = RA.t[:, 0:44 * NC].rearrange("p (a b) -> p a b", a=44)
        gbuf = xt.t[:, 0:640]
        gb_d = xt.d
        if pi == 1:
            P.load("sp", xt2, xt2[0:32, :], sconv[:, 0:2048])
            for q4 in range(11):
                if q4 == 4:
                    P.load("sp", xt2, xt2[0:32, :], sconv[:, 2048:4096])
                if q4 == 8:
                    P.load("sp", xt2, xt2[0:32, 0:1536], sconv[:, 4096:5632])
                for i in range(4):
                    fc = 4 * q4 + i
                    lc = fc * 128 - (0 if q4 < 4 else 2048 if q4 < 8 else 4096)
                    P.tr(B[3][:, 32 * i:32 * i + 32], xt2[0:32, lc:lc + 128], ident_f[0:32, 0:32], [xt2, ident_f], [B[3]])
                P.op("dve", lambda q4=q4: V.tensor_copy(out=shist[:, 4 * q4:4 * q4 + 4, :],
                                                        in_=B[3][:, 0:128].rearrange("p (a b) -> p a b", a=4)), [B[3]], [shist])
        for fc in range(44):
            sl = next_slab()
            wv = sl.t[:, 0:4096].rearrange("p (a b) -> p a b", a=16)
            P.load("pool", sl, wv[:, :, 0:128], dap(w_up, 128 * fc, [[2 * DFF, 128], [128 * 2 * DFF, 16], [1, 128]]))
            P.load("pool", sl, wv[:, :, 128:256], dap(w_up, DFF + 128 * fc, [[2 * DFF, 128], [128 * 2 * DFF, 16], [1, 128]]), group=True)
            spl = splits(NC)
            vb = []
            for si, (s0, sn) in enumerate(spl):
                bv_, bg_ = B[4 * (fc % 2) + 2 * si], B[4 * (fc % 2) + 2 * si + 1]
                for kc in range(16):
                    P.mm(bv_[:, 0:sn], wv[:, kc, 0:128], nT[:, kc, c0 + s0:c0 + s0 + sn], kc == 0, kc == 15, [sl, dU], [bv_])
                for kc in range(16):
                    P.mm(bg_[:, 0:sn], wv[:, kc, 128:256], nT[:, kc, c0 + s0:c0 + s0 + sn], kc == 0, kc == 15, [sl, dU], [bg_])
                vb.append((bv_, bg_, s0, sn))
                if pi == 0:
                    P.op("act", lambda bg_=bg_, s0=s0, sn=sn: A.copy(out=gbuf[:, s0:s0 + sn], in_=bg_[:, 0:sn]), [bg_], [gb_d])
                else:
                    pe_ = min(s0 + sn, 512)
                    if s0 < 512:
                        P.op("act", lambda bg_=bg_, s0=s0, pe_=pe_: A.copy(out=gbuf[:, 2 + s0:2 + pe_], in_=bg_[:, 0:pe_ - s0]), [bg_], [gb_d])
                    if s0 + sn > 512:
                        a0 = max(s0, 512)
                        P.op("act", lambda bg_=bg_, s0=s0, sn=sn, a0=a0: A.copy(
                            out=gbuf[:, 514:610].rearrange("p (b t) -> p b t", t=6)[:, (a0 - 512) // 4:16, 2:6],
                            in_=bg_[:, a0 - s0:sn].rearrange("p (b t) -> p b t", t=4)), [bg_], [gb_d])
            if pi == 1:
                P.op("act", lambda fc=fc: A.copy(out=gbuf[:, 0:2], in_=ghist[:, fc, :]), [ghist], [gb_d])
                P.op("act", lambda fc=fc: A.copy(out=gbuf[:, 514:610].rearrange("p (b t) -> p b t", t=6)[:, :, 0:2],
                                                 in_=shist[:, fc, :].rearrange("p (b t) -> p b t", t=2)), [shist], [gb_d])
            else:
                P.op("act", lambda fc=fc: A.copy(out=ghist[:, fc, :], in_=gbuf[:, 512:514]), [gb_d], [ghist])
            cv = xt.t[:, 640:640 + NC]
            def conv_seg(out_ap, g0, g1, g2):
                P.op("dve", lambda: V.tensor_scalar(out=out_ap, in0=g2, scalar1=cwcol[:, fc, 2:3], scalar2=cwcol[:, fc, 3:4],
                                                    op0=ALU.mult, op1=ALU.add), [gb_d, cwcol], [gb_d])
                P.op("dve", lambda: V.scalar_tensor_tensor(out=out_ap, in0=g1, scalar=cwcol[:, fc, 1:2], in1=out_ap,
                                                           op0=ALU.mult, op1=ALU.add), [gb_d, cwcol], [gb_d])
                P.op("dve", lambda: V.scalar_tensor_tensor(out=out_ap, in0=g0, scalar=cwcol[:, fc, 0:1], in1=out_ap,
                                                           op0=ALU.mult, op1=ALU.add), [gb_d, cwcol], [gb_d])
            if pi == 0:
                conv_seg(cv[:, 2:514], gbuf[:, 0:512], gbuf[:, 1:513], gbuf[:, 2:514])
                P.op("dve", lambda: V.memset(cv[:, 0:2], 0.0), [], [gb_d])
            else:
                conv_seg(cv[:, 0:512], gbuf[:, 0:512], gbuf[:, 1:513], gbuf[:, 2:514])
                g6 = gbuf[:, 514:610].rearrange("p (b t) -> p b t", t=6)
                conv_seg(cv[:, 512:576].rearrange("p (b t) -> p b t", t=4), g6[:, :, 0:4], g6[:, :, 1:5], g6[:, :, 2:6])
            P.op("act", lambda: A.activation(out=cv, in_=cv, func=AF.Gelu_apprx_tanh), [gb_d], [gb_d])
            for (bv_, bg_, s0, sn) in vb:
                P.op("dve", lambda bv_=bv_, s0=s0, sn=sn: V.tensor_tensor(out=actT[:, fc, s0:s0 + sn], in0=bv_[:, 0:sn],
                                                                        in1=cv[:, s0:s0 + sn], op=ALU.mult), [bv_, gb_d], RAall)
            if pi == 1:
                P.tr(B[7][0:2, 0:128], gbuf[:, 512:514], ident_f[:, :], [gb_d, ident_f], [B[7]])
                P.op("act", lambda: A.copy(out=tmpF[1][:, 0:32].rearrange("p (b t) -> p b t", t=2),
                                           in_=gbuf[:, 514:610].rearrange("p (b t) -> p b t", t=6)[:, :, 4:6]), [gb_d], [tmpF[1]])
                P.tr(B[7][0:32, 128:256], tmpF[1][:, 0:32], ident_f[:, :], [tmpF[1], ident_f], [B[7]])
                P.op("dve", lambda: V.tensor_copy(out=tmpF[0][0:32, 0:256], in_=B[7][0:32, 0:256]),
                     [B[7]], [tmpF[0]])
                P.store("sp", convp_o[:, 128 * fc:128 * fc + 128], tmpF[0], tmpF[0][0:2, 0:128])
                P.store("sp", convs_o[:, 128 * fc:128 * fc + 128], tmpF[0], tmpF[0][0:32, 128:256], group=True)
        for cg in range(8):
            for kh in range(2):
                sl = next_slab()
                wv = sl.t[:, 0:5632].rearrange("p (a b) -> p a b", a=22)
                P.load("pool", sl, wv, dap(w_down, (22 * kh * 128) * D + 256 * cg, [[D, 128], [128 * D, 22], [1, 256]]))
                for ui, (kind, r0, n, cc) in enumerate(units):
                    bk = B[ui]
                    for k in range(22):
                        P.mm(bk[0:n, 0:256], actT[:, 22 * kh + k, cc - c0:cc - c0 + n], wv[:, k, :], kh == 0 and k == 0, kh == 1 and k == 21,
                             RAall + [sl], [bk])
            for ui, (kind, r0, n, cc) in enumerate(units):
                tf = tmpF[ui % 2]
                P.op("act", lambda ui=ui, tf=tf, n=n: A.copy(out=tf[0:n, 0:256], in_=B[ui][0:n, 0:256]), [B[ui]], [tf])
                cx.dma("sp", y_scr[cc:cc + n, 256 * cg:256 * cg + 256], tf[0:n, 0:256], owner=tf.d, reads=[tf.d], writes=[ysd])
        P.load("sp", grep, grep[:], dap(g_qf, 0, [[0, 128], [1, D]]))
        for ui, (kind, r0, n, cc) in enumerate(units):
            if kind == "h":
                continue
            P.load("sp", xt, xt[0:n, :], y_scr[cc:cc + n, :], reads=[ysd])
            P.load("sp", xt2, xt2[0:n, :], h_scr[cc:cc + n, :], reads=[hsd])
            rmsnorm_stats(xt, n)
            P.op("dve", lambda n=n: V.scalar_tensor_tensor(out=xt[0:n, :], in0=xt[0:n, :], scalar=small[0:n, 2:3], in1=grep[0:n, :],
                                                          op0=ALU.mult, op1=ALU.mult), [xt, small, grep], [xt])
            P.op("dve", lambda n=n: V.tensor_tensor(out=xt2[0:n, :], in0=xt2[0:n, :], in1=xt[0:n, :], op=ALU.add), [xt, xt2], [xt2])
            orow = (cc - c0 + yrow0) if kind == "p" else 1024
            P.store("sp", y_o[orow:orow + n, :], xt2, xt2[0:n, :])


    dens = P.sb("dens", [128, 12, 8], F32)
    den12 = P.sb("den12", [128, 12], F32)
    sc12 = P.sb("sc12", [128, 12], F32)
    selb = P.sb("selb", [128, 40], F32)
    score = P.sb("score", [128, 40], F32)
    impt = P.sb("impt", [128, 64], F32)
    bonus_sb = P.sb("bonus_sb", [128, 32], F32)
    Ct = P.sb("Ct", [128, 16, 64], BF16)
    kbcrep = P.sb("kbcrep", [128, 64], F32)
    msame = P.sb("msame", [16, 16], F32)
    G16 = P.sb("G16", [16, 12], F32); selB = P.sb("selB", [64, 256], F32); Mi = P.sb("Mi", [16, 4], F32)
    kcTb = P.sb("kcTb", [128, 4, 64], BF16); vcb = P.sb("vcb", [64, 4, 64], BF16)
    idx_t = P.sb("idx_t", [128, 256], I32); iot_i = P.sb("iot_i", [128, 1], I32)
    P.load("sp", kbcrep, kbcrep[:], dap(t_kbc, 0, [[0, 128], [1, 64]]))
    P.load("sp", msame, msame[:], t_msame[:, :])
    P.load("sp", selB, selB[:], t_selb[:, :])
    P.load("sp", Mi, Mi[:], t_mi[:, :])
    att_rot = [0]

    tF3 = View(xt.t[:, 0:512], Dep("tF3"))
    tB3 = View(xt.t[:, 1024:1280].bitcast(BF16), Dep("tB3"))
    SBK = [B[0], B[1], B[6]]
    TFS = [tmpF[0], tmpF[1], tF3]
    TBS = [tmpB[0], tmpB[1], tB3]
    xdeps = [xt.d, tF3.d, tB3.d]

    def branch(qh, qdeps, n, groups, o_dst, o_bank, dcol, first_last=(True, True)):
        ng_ = len(groups)
        items = []
        for gi, gr in enumerate(groups):
            def stA(ka, gr=gr, gi=gi):
                Sb, tF, tB = SBK[ka], TFS[ka], TBS[ka]
                nk = gr["nk"]
                P.mm(Sb[0:n, 0:nk], qh, gr["kT"], True, True, list(qdeps) + list(gr["kdeps"]), [Sb])
                gr["bias"](Sb, tF, n, nk)
                P.op("act", lambda: A.activation(out=tB[0:n, 0:nk], in_=tF[0:n, 0:nk], func=AF.Exp,
                                                 accum_out=dens[0:n, dcol, gi:gi + 1]), [tF], [tB, dens])
            def stB1(ka, k, gr=gr, gi=gi):
                tB, pT, PTb = TBS[ka], PT[k], B[2 + k]
                pv = bfv(PTb)
                koff = 0
                for ci, (vap, nkc, vdeps) in enumerate(gr["v"]):
                    P.tr(pv[0:nkc, 128 * ci:128 * ci + n], tB[0:n, koff:koff + nkc], ident_b[0:n, 0:n], [tB, ident_b], [PTb])
                    koff += nkc
                nch = len(gr["v"])
                P.op("act", lambda: A.copy(out=pT[:, 0:128 * nch], in_=pv[:, 0:128 * nch]), [PTb], [pT])
            def stB2(ka, k, gr=gr, gi=gi):
                pT = PT[k]
                nch = len(gr["v"])
                for ci, (vap, nkc, vdeps) in enumerate(gr["v"]):
                    P.mm(o_dst, pT[0:nkc, 128 * ci:128 * ci + n], vap, gi == 0 and ci == 0, gi == ng_ - 1 and ci == nch - 1,
                         [pT] + list(vdeps), [o_bank])
            items.append((stA, stB1, stB2))
        return items

    def run_items(items):
        base, ni = att_rot[0], len(items)
        for i in range(-2, ni):
            if i >= 0:
                items[i][1]((base + i) % 3, (base + i) % 2)
            if i + 2 < ni:
                items[i + 2][0]((base + i + 2) % 3)
            if i >= 0:
                items[i][2]((base + i) % 3, (base + i) % 2)
        att_rot[0] = base + ni

    def rank_select(n, nb, dead_ap, out_ap=None, out_dep=None, sc_ap=None, sc_dep=None):
        sc = score[0:n, 0:nb] if sc_ap is None else sc_ap
        scd = score if sc_dep is None else sc_dep
        rkv = rk.t[0:n, 0:nb * nb].rearrange("p (a b) -> p a b", a=nb)
        P.op("dve", lambda: V.tensor_tensor(out=rkv, in0=bcmid(sc, nb), in1=bc(sc, nb), op=ALU.is_gt), [scd], [rk])
        P.op("dve", lambda: V.tensor_reduce(out=selb[0:n, 0:nb], in_=rkv, axis=AX.X, op=ALU.add), [rk], [selb])
        P.op("dve", lambda: V.tensor_scalar(out=selb[0:n, 0:nb], in0=selb[0:n, 0:nb], scalar1=15.5, scalar2=NEG,
                                            op0=ALU.is_gt, op1=ALU.mult), [selb], [selb])
        if dead_ap is not None:
            P.op("dve", lambda: V.tensor_tensor(out=selb[0:n, 0:nb], in0=selb[0:n, 0:nb], in1=dead_ap, op=ALU.add), [selb, sdead], [selb])
        if out_ap is not None:
            P.op("dve", lambda: V.tensor_copy(out=out_ap, in_=selb[0:n, 0:nb]), [selb], [out_dep])

    def finalize(n, gate_ap, gdeps, o_tok_ap, otdep):
        P.op("dve", lambda: V.tensor_reduce(out=den12[0:n, :], in_=dens[0:n, :, :], axis=AX.X, op=ALU.add), [dens], [den12])
        P.op("dve", lambda: V.tensor_scalar(out=den12[0:n, :], in0=den12[0:n, :], scalar1=1e-30, scalar2=None, op0=ALU.max), [den12], [den12])
        P.op("dve", lambda: V.reciprocal(out=den12[0:n, :], in_=den12[0:n, :]), [den12], [den12])
        P.op("dve", lambda: V.tensor_tensor(out=sc12[0:n, :].rearrange("p (b i) -> p b i", b=3), in0=den12[0:n, :].rearrange("p (b i) -> p b i", b=3),
                                            in1=gate_ap, op=ALU.mult), [den12] + list(gdeps), [sc12])
        t12 = tmpF[0].t[0:n, :].rearrange("p (a b) -> p a b", a=8)
        t4 = tmpF[1].t[0:n, 0:256].rearrange("p (a b) -> p a b", a=4)
        P.op("dve", lambda: V.tensor_tensor(out=t12, in0=B[4][0:n, :].rearrange("p (a b) -> p a b", a=8), in1=bc(sc12[0:n, 0:8], 64), op=ALU.mult),
             [B[4], sc12], [tmpF[0]])
        P.op("dve", lambda: V.tensor_tensor(out=t4, in0=B[5][0:n, 0:256].rearrange("p (a b) -> p a b", a=4), in1=bc(sc12[0:n, 8:12], 64), op=ALU.mult),
             [B[5], sc12], [tmpF[1]])
        P.op("dve", lambda: V.tensor_tensor(out=t4, in0=t4, in1=t12[:, 0:4, :], op=ALU.add), [tmpF[0], tmpF[1]], [tmpF[1]])
        P.op("dve", lambda: V.tensor_tensor(out=o_tok_ap, in0=t4, in1=t12[:, 4:8, :], op=ALU.add), [tmpF[0], tmpF[1]], [otdep])

    def cmp_branch(n, qhs, qdeps, kc_t, g, gpar, bias_ap, bdeps, vc_t, nb_sel, is_sample):
        Sb = B[6]
        for i in range(4):
            P.mm(Sb[0:n, 64 * i:64 * i + 64], qhs[i], kc_t[64 * gpar:64 * gpar + 64, g, :], True, True, list(qdeps) + [kc_t], [Sb])
        E = tmpF[0]
        Ev = E.t[0:n, 0:256].rearrange("p (a b) -> p a b", a=4)
        P.op("dve", lambda: V.scalar_tensor_tensor(out=Ev, in0=Sb[0:n, 0:256].rearrange("p (a b) -> p a b", a=4), scalar=0.125, in1=bias_ap,
                                                   op0=ALU.mult, op1=ALU.add), [Sb] + list(bdeps), [E])
        P.op("act", lambda: A.activation(out=E[0:n, 0:256], in_=E[0:n, 0:256], func=AF.Exp), [E], [E])
        P.op("dve", lambda: V.tensor_reduce(out=small[0:n, 8:12], in_=Ev, axis=AX.X, op=ALU.add), [E], [small])
        P.op("dve", lambda: V.tensor_scalar(out=small[0:n, 8:12], in0=small[0:n, 8:12], scalar1=1e-30, scalar2=None, op0=ALU.max), [small], [small])
        P.op("dve", lambda: V.reciprocal(out=small[0:n, 12:16], in_=small[0:n, 8:12]), [small], [small])
        P.op("dve", lambda: V.tensor_tensor(out=Ev, in0=Ev, in1=bc(small[0:n, 12:16], 64), op=ALU.mult), [E, small], [E])
        Pb = tmpB[0]
        P.op("act", lambda: A.copy(out=Pb[0:n, 0:256], in_=E[0:n, 0:256]), [E], [Pb])
        P.op("dve", lambda: V.tensor_reduce(out=impt[0:n, 0:64], in_=E.t[0:n, 0:256].rearrange("p (a b) -> p b a", a=4), axis=AX.X, op=ALU.add),
             [E], [impt])
        pv = bfv(B[2])
        for i in range(4):
            P.tr(pv[0:64, 128 * i:128 * i + n], Pb[0:n, 64 * i:64 * i + 64], ident_b[0:n, 0:n], [Pb, ident_b], [B[2]])
        P.op("dve", lambda: V.tensor_copy(out=PT[0][0:64, 0:512], in_=pv[0:64, 0:512]), [B[2]], [PT[0]])
        for i in range(4):
            P.mm(B[4][0:n, 64 * i:64 * i + 64], PT[0][0:64, 128 * i:128 * i + n], vc_t[0:64, g, :], True, True, [PT[0], vc_t], [B[4]])

    def attention(pi, st):
        units, uT, dU, NU, c0, NC, NCp, A_T, o_T, q_T, mrg, yrow0 = st
        join(xdeps)
        for ui, (kind, r0, n, cc) in enumerate(units):
            if kind == "s":
                sample_attention(st, ui)
                continue
            cq, ro = r0 // 128, r0 % 128
            qc = cc - c0
            def add_near(tF, n_, h, ro=ro):
                if ro == 0:
                    P.op("dve", lambda: V.tensor_tensor(out=tF[0:n_, 0:256], in0=tF[0:n_, 0:256], in1=Nt[0:n_, h, :], op=ALU.add), [tF, Nt], [tF])
                else:
                    P.op("dve", lambda: V.tensor_tensor(out=tF[0:n_, ro:256], in0=tF[0:n_, ro:256], in1=Nt[0:n_, h, 0:256 - ro], op=ALU.add), [tF, Nt], [tF])
            def add_far(tF, n_, ro=ro):
                if ro == 0:
                    P.op("dve", lambda: V.tensor_tensor(out=tF[0:n_, 0:128], in0=tF[0:n_, 0:128], in1=Tfar[0:n_, :], op=ALU.add), [tF, Tfar], [tF])
                else:
                    P.op("dve", lambda: V.tensor_scalar(out=tF[0:n_, 0:ro], in0=tF[0:n_, 0:ro], scalar1=NEG, scalar2=None, op0=ALU.add), [tF], [tF])
                    P.op("dve", lambda: V.tensor_tensor(out=tF[0:n_, ro:128], in0=tF[0:n_, ro:128], in1=Tfar[0:n_, 0:128 - ro], op=ALU.add), [tF, Tfar], [tF])
            P.load("sp", bonus_sb, bonus_sb[0:n, :], t_bonus[r0 - 1022:r0 - 1022 + n, :])
            cc0 = F0 + 128 * cq - 2047 + ro
            P.load("sp", Hk, Hk[0:64, :, 0:n], dap(frow, cc0, [[32, 64], [WF, 16], [1, n]]), reads=[fdep])
            for h4 in range(4):
                for i in range(4):
                    P.mm(B[6][0:n, 64 * i:64 * i + 64], Hk[0:64, 4 * h4 + i, 0:n], Jx[0:64, 64:128], True, True, [Hk, Jx], [B[6]])
                P.op("dve", lambda h4=h4: V.tensor_tensor(out=Ct[0:n, 4 * h4:4 * h4 + 4, :], in0=B[6][0:n, 0:256].rearrange("p (a b) -> p a b", a=4),
                                                          in1=bcmid(kbcrep[0:n, :], 4), op=ALU.add), [B[6], kbcrep], [Ct])
            o_tok = ub
            for g in range(4):
                gpar, gp = g % 2, g // 2
                qhs = [q_T[64 * gpar:64 * gpar + 64, 4 * gp + i, qc:qc + n] for i in range(4)]
                P.op("dve", lambda: V.memset(dens[:, :, :], 0.0), [], [dens])
                P.op("dve", lambda: V.memset(dens[:, 0:4, 0:1], 1.0), [], [dens])
                cmp_branch(n, qhs, [dQ], kcT, g, gpar, Ct[0:n, 4 * g:4 * g + 4, :], [Ct], vc, 32, False)
                P.op("dve", lambda: V.tensor_reduce(out=score[0:n, 0:32], in_=impt[0:n, 0:64].rearrange("p (a b) -> p a b", b=2), axis=AX.X, op=ALU.add),
                     [impt], [score])
                P.op("dve", lambda: V.tensor_tensor(out=score[0:n, 0:32], in0=score[0:n, 0:32], in1=bonus_sb[0:n, :], op=ALU.add), [score, bonus_sb], [score])
                rank_select(n, 32, sdead[0:n, :])
                items = []
                for i in range(4):
                    h = 4 * g + i
                    groups = []
                    far_chunks = list(range(0, cq - 1))
                    for k0 in range(0, len(far_chunks), 4):
                        chs = far_chunks[k0:k0 + 4]
                        nch = len(chs)
                        def bias_far(Sb, tF, n_, nk, chs=chs, nch=nch):
                            P.op("dve", lambda: V.scalar_tensor_tensor(
                                out=tF[0:n_, 0:nk].rearrange("p (a b) -> p a b", b=64), in0=Sb[0:n_, 0:nk].rearrange("p (a b) -> p a b", b=64),
                                scalar=0.125, in1=bc(selb[0:n_, 2 * chs[0]:2 * chs[0] + 2 * nch], 64), op0=ALU.mult, op1=ALU.add), [Sb, selb], [tF])
                        groups.append(dict(nk=128 * nch, kT=kselT[64 * gpar:64 * gpar + 64, gp, 128 * chs[0]:128 * chs[0] + 128 * nch], kdeps=[kselT],
                                           bias=bias_far, v=[(vsel[:, c, 64 * g:64 * g + 64], 128, [vsel]) for c in chs]))
                    def bias_near(Sb, tF, n_, nk, h=h):
                        P.op("dve", lambda: V.scalar_tensor_tensor(
                            out=tF[0:n_, 0:256].rearrange("p (a b) -> p a b", b=64), in0=Sb[0:n_, 0:256].rearrange("p (a b) -> p a b", b=64),
                            scalar=0.125, in1=bc(selb[0:n_, 2 * cq - 2:2 * cq + 2], 64), op0=ALU.mult, op1=ALU.add), [Sb, selb], [tF])
                        add_near(tF, n_, h)
                    groups.append(dict(nk=256, kT=kselT[64 * gpar:64 * gpar + 64, gp, 128 * (cq - 1):128 * (cq + 1)], kdeps=[kselT],
                                       bias=bias_near, v=[(vsel[:, c, 64 * g:64 * g + 64], 128, [vsel]) for c in (cq - 1, cq)]))
                    items += branch(qhs[i], [dQ], n, groups, B[4][0:n, 256 + 64 * i:256 + 64 * i + 64], B[4], 4 + i)
                    def bias_wfar(Sb, tF, n_, nk):
                        P.op("dve", lambda: V.scalar_tensor_tensor(out=tF[0:n_, 0:384].rearrange("p (a b) -> p a b", b=128), in0=Sb[0:n_, 0:384].rearrange("p (a b) -> p a b", b=128), scalar=0.125,
                                                                   in1=bc(kbrep[0:n_, cq - 4:cq - 1], 128), op0=ALU.mult, op1=ALU.add), [Sb, kbrep], [tF])
                        add_far(tF, n_)
                    def bias_wnear(Sb, tF, n_, nk, h=h):
                        P.op("dve", lambda: V.scalar_tensor_tensor(out=tF[0:n_, 0:256].rearrange("p (a b) -> p a b", b=128),
                                                                   in0=Sb[0:n_, 0:256].rearrange("p (a b) -> p a b", b=128), scalar=0.125,
                                                                   in1=bc(kbrep[0:n_, cq - 1:cq + 1], 128), op0=ALU.mult, op1=ALU.add), [Sb, kbrep], [tF])
                        add_near(tF, n_, h)
                    wg = [dict(nk=384, kT=kwinT[64 * gpar:64 * gpar + 64, gp, 128 * (cq - 4):128 * (cq - 1)], kdeps=[kwinT], bias=bias_wfar,
                               v=[(vwin[:, c, 64 * g:64 * g + 64], 128, [vwin]) for c in (cq - 4, cq - 3, cq - 2)]),
                          dict(nk=256, kT=kwinT[64 * gpar:64 * gpar + 64, gp, 128 * (cq - 1):128 * (cq + 1)], kdeps=[kwinT], bias=bias_wnear,
                               v=[(vwin[:, c, 64 * g:64 * g + 64], 128, [vwin]) for c in (cq - 1, cq)])]
                    items += branch(qhs[i], [dQ], n, wg, B[5][0:n, 64 * i:64 * i + 64], B[5], 8 + i)
                run_items(items)
                gate_ap = gates[0:n, ui, :].rearrange("p (b g i) -> p b g i", b=3, g=4)[:, :, g, :]
                finalize(n, gate_ap, [gates], o_tok[0:n, 256 * g:256 * g + 256].rearrange("p (a b) -> p a b", a=4), o_tok.d)
            pv = bfv(B[6])
            for kc in range(8):
                P.tr(pv[:, 128 * kc:128 * kc + n], o_tok[0:n, 128 * kc:128 * kc + 128], ident_b[0:n, 0:n], [o_tok, ident_b], [B[6]])
            P.op("act", lambda pv=pv, qc=qc, n=n: A.copy(out=o_T[:, :, qc:qc + n], in_=pv[:, :].rearrange("p (a b) -> p a b", a=8)[:, :, 0:n]), [B[6]], [dO])

    def join(deps):
        P.op("pool", lambda: G.memset(small[0:1, 60:61], 0.0), [], list(deps) + [small])

    def sample_attention(st, ui):
        units, uT, dU, NU, c0, NC, NCp, A_T, o_T, q_T, mrg, yrow0 = st
        cx.dma("sp", dap(wins_o, 0, [[512 * 512, NSB], [1, 508 * 512]]), dap(swin, 4 * 512, [[512 * 512, NSB], [1, 508 * 512]]),
               owner=P.ddep["wins"], is_output=True)
        cx.dma("sp", dap(pools_o, 0, [[15 * 1024, NSB], [1, 11 * 1024]]), dap(spool, 4 * 1024, [[15 * 1024, NSB], [1, 11 * 1024]]),
               owner=P.ddep["pools"], is_output=True)
        P.load("sp", idx_t, idx_t[:], dap(ptab, 0, [[0, 128], [1, 256]]))
        iof = tmpF[0]
        P.op("pool", lambda: G.iota(iot_i[:, :], pattern=[[0, 1]], base=0, channel_multiplier=1), [], [iot_i])
        P.op("dve", lambda: V.tensor_copy(out=small[:, 20:21], in_=iot_i[:, :]), [iot_i], [small])
        P.op("dve", lambda: V.tensor_copy(out=iof[:, 0:256], in_=idx_t[:, :]), [idx_t], [iof])
        P.op("dve", lambda: V.tensor_scalar(out=iof[:, 0:256], in0=iof[:, 0:256], scalar1=128.0, scalar2=small[:, 20:21], op0=ALU.mult, op1=ALU.add),
             [iof, small], [iof])
        P.op("dve", lambda: V.tensor_copy(out=idx_t[:, :], in_=iof[:, 0:256]), [iof], [idx_t])
        kcmpTb = RA.t[:, 13824:17920].rearrange("p (a b) -> p a b", a=2)
        vcmpTb = RA.t[:, 17920:22016].rearrange("p (a b) -> p a b", a=2)
        ksTb = RA.t[:, 22016:26128].rearrange("p (a b) -> p a b", a=2)
        kwTb = RA.t[:, 26128:27168].rearrange("p (a b) -> p a b", a=2)
        vs_b = slabs[0].t[:, 0:4096].rearrange("p (a b) -> p a b", a=16)
        kwt = slabs[0].t[:, 4096:5120].rearrange("p (a b) -> p a b", a=4)
        pgb = [slabs[1].t[:, 0:1024], slabs[1].t[:, 1024:2048], slabs[1].t[:, 4608:5632]]
        vw_b = slabs[1].t[:, 2048:3072].rearrange("p (a b) -> p a b", a=4)
        vnew = slabs[1].t[:, 3072:4608]
        d_pg = [Dep("pg0"), Dep("pg1"), Dep("pg2")]; d_vs = Dep("vs"); d_kwt = Dep("kwt"); d_vw = Dep("vw"); d_vn = Dep("vnew")
        d_kcm = Dep("kcmb"); d_vcm = Dep("vcmb"); d_ks = Dep("ksb"); d_kw = Dep("kwb")
        newdeps = d_pg + [d_vs, d_kwt, d_vw, d_vn, d_kcm, d_vcm, d_ks, d_kw]
        join([slabs[0].d, slabs[1].d, dM] + newdeps)
        o16 = tmpB[1]
        qS = ub.t[:, 0:512].rearrange("p (g b x) -> p g b x", g=2, b=16)
        for gp_ in range(2):
            P.op("dve", lambda gp_=gp_: V.tensor_copy(
                out=qS[:, gp_, :, :].rearrange("p b (i t) -> p b i t", i=4),
                in_=q_T[:, 4 * gp_:4 * gp_ + 4, 512:576].rearrange("p i (b t) -> p b i t", t=4)), [dQ], [ub])
        for b_ in range(NSB):
            def gather(pg):
                k = pg % 3
                j = 16 * b_ + pg
                cx.dma("pool", None, None, owner=d_pg[k], reads=[idx_t.d], writes=[d_pg[k]],
                       fn=lambda: G.indirect_dma_start(out=pgb[k], out_offset=None, in_=cache[:, :],
                                                       in_offset=bass.IndirectOffsetOnAxis(ap=idx_t[:, j:j + 1], axis=0)))
            gather(0)
            gather(1)
            for pg in range(16):
                k = pg % 3
                if pg + 2 < 16:
                    gather(pg + 2)
                bX, bY = (B[7], B[6]) if pg % 2 == 0 else (B[5], B[4])
                pvx, pvy = bfv(bX), bfv(bY)
                for si in range(2):
                    for gp in range(2):
                        ii = 2 * si + gp
                        P.tr(pvx[:, 128 * ii:128 * ii + 128], pgb[k][:, 256 * si + 128 * gp:256 * si + 128 * gp + 128], ident_b[:, :],
                             [d_pg[k], ident_b], [bX])
                for gp in range(2):
                    P.tr(pvy[:, 128 * gp:128 * gp + 128], pgb[k][:, 512 + 128 * gp:512 + 128 * gp + 128], ident_b[:, :], [d_pg[k], ident_b], [bY])
                def pview(si, pvx=pvx):
                    return pvx[:, 256 * si:256 * si + 256].rearrange("p (a b) -> p a b", a=2)
                P.op("dve", lambda pg=pg, pview=pview: V.tensor_tensor(out=kcmpTb[:, :, 128 * pg:128 * pg + 128], in0=pview(0),
                                                                       in1=bcmid(peT[:, 0, :], 2), op=ALU.add), [bX, peT], [d_kcm])
                P.op("dve", lambda pg=pg, pview=pview: V.tensor_tensor(out=vcmpTb[:, :, 128 * pg:128 * pg + 128], in0=pview(1),
                                                                       in1=bcmid(peT[:, 1, :], 2), op=ALU.add), [bX, peT], [d_vcm])
                P.op("act", lambda pg=pg, pvy=pvy: A.copy(out=ksTb[:, :, 128 * pg:128 * pg + 128],
                                                          in_=pvy[:, 0:256].rearrange("p (a b) -> p a b", a=2)), [bY], [d_ks])
                P.op("act", lambda pg=pg, k=k: A.copy(out=vs_b[:, pg, :], in_=pgb[k][:, 768:1024]), [d_pg[k]], [d_vs])
            cx.dma("pool", kwt, dap(swin, 512 * 512 * b_, [[512, 128], [128 * 512, 4], [1, 256]]), owner=d_kwt, writes=[d_kwt])
            cx.dma("pool", vw_b, dap(swin, 512 * 512 * b_ + 256, [[512, 128], [128 * 512, 4], [1, 256]]), owner=d_vw, writes=[d_vw])
            cx.dma("pool", vnew[0:4, :], kvs_scr[4 * b_:4 * b_ + 4, :], owner=d_vn, reads=[kvsd], writes=[d_vn])
            pv = bfv(B[7])
            for c in range(4):
                for gp in range(2):
                    ii = 2 * c + gp
                    P.tr(pv[:, 128 * ii:128 * ii + 128], kwt[:, c, 128 * gp:128 * gp + 128], ident_b[:, :], [d_kwt, ident_b], [B[7]])
            P.op("act", lambda pv=pv: A.copy(out=kwTb[:, :, 0:512].rearrange("p g (c k) -> p c g k", c=4),
                                             in_=pv[:, 0:1024].rearrange("p (c g k) -> p c g k", c=4, g=2)), [B[7]], [d_kw])
            P.op("dve", lambda b_=b_: V.tensor_copy(out=ksTb[:, :, 2048:2052], in_=ksT_new[:, :, 4 * b_:4 * b_ + 4]), [ksT_new], [d_ks])
            P.op("dve", lambda b_=b_: V.tensor_copy(out=kwTb[:, :, 512:516], in_=kwT_new[:, :, 4 * b_:4 * b_ + 4]), [kwT_new], [d_kw])
            P.mm(B[7][0:16, 0:48], selB[:, 16 * b_:16 * b_ + 16], gates[0:64, ui, :], True, True, [selB, gates], [B[7]])
            P.op("dve", lambda: V.tensor_tensor(out=tmpF[1][0:16, 0:48].rearrange("p (a b) -> p a b", b=4),
                                                in0=B[7][0:16, 0:48].rearrange("p (a b) -> p a b", b=4), in1=bcmid(Mi[:, :], 12), op=ALU.mult),
                 [B[7], Mi], [tmpF[1]])
            P.op("dve", lambda: V.tensor_reduce(out=G16[:, :], in_=tmpF[1][0:16, 0:48].rearrange("p (a b) -> p a b", b=4), axis=AX.X, op=ALU.add),
                 [tmpF[1]], [G16])
            compress(kcmpTb, d_kcm, w1k_sb, True, kcTb)
            compress(vcmpTb, d_vcm, w1v_sb, False, vcb)
            selb4 = Ct.t[0:16, :, :].rearrange("p a b -> p (a b)").bitcast(F32)[:, 0:160].rearrange("p (g x) -> p g x", g=4)
            P.op("dve", lambda: V.memset(dens[0:16, :, :], 0.0), [], [dens])
            P.op("dve", lambda: V.memset(dens[0:16, 0:12:3, 0:1], 1.0), [], [dens])
            items = []
            Sb = B[7]
            E = tmpF[0]
            G4 = range(4)
            SbG = [B[7], B[3]]
            for g in G4:
                gpar, gp = g % 2, g // 2
                P.mm(SbG[gpar][0:16, 64 * g:64 * g + 64], qS[64 * gpar:64 * gpar + 64, gp, b_, :], kcTb[64 * gpar:64 * gpar + 64, g, :], True, True,
                     [ub, kcTb], [SbG[gpar]])
            for g in G4:
                P.op("dve", lambda g=g: V.scalar_tensor_tensor(out=E[0:16, 64 * g:64 * g + 64], in0=SbG[g % 2][0:16, 64 * g:64 * g + 64], scalar=0.125,
                                                               in1=Cs[:, g, :], op0=ALU.mult, op1=ALU.add), [SbG[g % 2], Cs], [E])
            for g in G4:
                P.op("act", lambda g=g: A.activation(out=E[0:16, 64 * g:64 * g + 64], in_=E[0:16, 64 * g:64 * g + 64], func=AF.Exp,
                                                     accum_out=small[0:16, 8 + g:9 + g]), [E], [E, small])
            for g in G4:
                P.op("dve", lambda g=g: V.reciprocal(out=small[0:16, 12 + g:13 + g], in_=small[0:16, 8 + g:9 + g]), [small], [small])
            for g in G4:
                P.op("dve", lambda g=g: V.tensor_scalar(out=E[0:16, 64 * g:64 * g + 64], in0=E[0:16, 64 * g:64 * g + 64], scalar1=small[0:16, 12 + g:13 + g],
                                                        scalar2=None, op0=ALU.mult), [E, small], [E])
            for g in G4:
                P.op("act", lambda g=g: A.copy(out=tmpB[0][0:16, 64 * g:64 * g + 64], in_=E[0:16, 64 * g:64 * g + 64]), [E], [tmpB[0]])
            for g in G4:
                P.mm(B[7][0:16, 256 + 64 * g:256 + 64 * g + 64], msame[:, :], E[0:16, 64 * g:64 * g + 64], True, True, [msame, E], [B[7]])
            scs = [tmpF[1].t[0:16, 40 * g:40 * g + 33] for g in G4]
            for g in G4:
                P.op("dve", lambda g=g: V.tensor_reduce(out=scs[g][:, 0:32], in_=B[7][0:16, 256 + 64 * g:256 + 64 * g + 64].rearrange("p (a b) -> p a b", b=2),
                                                        axis=AX.X, op=ALU.add), [B[7]], [tmpF[1]])
            for g in G4:
                P.op("dve", lambda g=g: V.memset(scs[g][:, 32:33], 0.0), [], [tmpF[1]])
            for g in G4:
                P.op("dve", lambda g=g: V.tensor_tensor(out=scs[g], in0=scs[g], in1=sbonus[:, :], op=ALU.add), [tmpF[1], sbonus], [tmpF[1]])
            pv2 = bfv(B[2])
            for g in G4:
                P.tr(pv2[0:64, 16 * g:16 * g + 16], tmpB[0][0:16, 64 * g:64 * g + 64], ident_b[0:16, 0:16], [tmpB[0], ident_b], [B[2]])
            for g in G4:
                P.op("dve", lambda g=g, pv2=pv2: V.tensor_copy(out=PT[0][0:64, 16 * g:16 * g + 16], in_=pv2[0:64, 16 * g:16 * g + 16]), [B[2]], [PT[0]])
            for g in G4:
                P.mm(B[4 + g // 2][0:16, 192 * (g % 2):192 * (g % 2) + 64], PT[0][0:64, 16 * g:16 * g + 16], vcb[0:64, g, :], True, True,
                     [PT[0], vcb], [B[4 + g // 2]])
            for g in G4:
                rank_select(16, 33, None, selb4[:, g, 0:33], Ct.d, sc_ap=scs[g], sc_dep=tmpF[1])
            for g in range(4):
                gpar, gp = g % 2, g // 2
                q16 = qS[64 * gpar:64 * gpar + 64, gp, b_, :]
                n = 16
                Bacc = B[4 + g // 2]
                ab = 192 * (g % 2)
                groups = []
                for chs in ([0, 1, 2, 3], [4, 5, 6, 7], [8, 9, 10, 11], [12, 13, 14]):
                    nch = len(chs)
                    def bias_far(Sb, tF, n_, nk, chs=chs, nch=nch, g=g):
                        P.op("dve", lambda: V.scalar_tensor_tensor(
                            out=tF[0:n_, 0:nk].rearrange("p (a b) -> p a b", b=64), in0=Sb[0:n_, 0:nk].rearrange("p (a b) -> p a b", b=64),
                            scalar=0.125, in1=bc(selb4[0:n_, g, 2 * chs[0]:2 * chs[0] + 2 * nch], 64), op0=ALU.mult, op1=ALU.add), [Sb, Ct], [tF])
                    groups.append(dict(nk=128 * nch, kT=ksTb[64 * gpar:64 * gpar + 64, gp, 128 * chs[0]:128 * chs[0] + 128 * nch], kdeps=[d_ks],
                                       bias=bias_far, v=[(vs_b[:, c, 64 * g:64 * g + 64], 128, [d_vs]) for c in chs]))
                def bias_near(Sb, tF, n_, nk, g=g):
                    P.op("dve", lambda: V.scalar_tensor_tensor(
                        out=tF[0:n_, 0:128].rearrange("p (a b) -> p a b", b=64), in0=Sb[0:n_, 0:128].rearrange("p (a b) -> p a b", b=64),
                        scalar=0.125, in1=bc(selb4[0:n_, g, 30:32], 64), op0=ALU.mult, op1=ALU.add), [Sb, Ct], [tF])
                    P.op("dve", lambda: V.scalar_tensor_tensor(out=tF[0:n_, 128:132], in0=Sb[0:n_, 128:132], scalar=0.125,
                                                               in1=bc(selb4[0:n_, g, 32:33], 4), op0=ALU.mult, op1=ALU.add), [Sb, Ct], [tF])
                    P.op("dve", lambda: V.tensor_tensor(out=tF[0:n_, 0:132], in0=tF[0:n_, 0:132], in1=Ns[:, g, :], op=ALU.add), [tF, Ns], [tF])
                groups.append(dict(nk=132, kT=ksTb[64 * gpar:64 * gpar + 64, gp, 1920:2052], kdeps=[d_ks], bias=bias_near,
                                   v=[(vs_b[:, 15, 64 * g:64 * g + 64], 128, [d_vs]), (vnew[0:4, 768 + 64 * g:768 + 64 * g + 64], 4, [d_vn])]))
                items += branch(q16, [ub], 16, groups, Bacc[0:16, ab + 64:ab + 128], Bacc, 3 * g + 1)
                def bias_wfar(Sb, tF, n_, nk):
                    P.op("dve", lambda: V.tensor_scalar(out=tF[0:n_, 0:384], in0=Sb[0:n_, 0:384], scalar1=0.125, scalar2=None, op0=ALU.mult), [Sb], [tF])
                    P.op("dve", lambda: V.tensor_tensor(out=tF[0:n_, 0:128], in0=tF[0:n_, 0:128], in1=Tfs[:, :], op=ALU.add), [tF, Tfs], [tF])
                def bias_wnear(Sb, tF, n_, nk, g=g):
                    P.op("dve", lambda: V.scalar_tensor_tensor(out=tF[0:n_, 0:132], in0=Sb[0:n_, 0:132], scalar=0.125, in1=Ns[:, g, :],
                                                               op0=ALU.mult, op1=ALU.add), [Sb, Ns], [tF])
                wg = [dict(nk=384, kT=kwTb[64 * gpar:64 * gpar + 64, gp, 0:384], kdeps=[d_kw], bias=bias_wfar,
                           v=[(vw_b[:, c, 64 * g:64 * g + 64], 128, [d_vw]) for c in range(3)]),
                      dict(nk=132, kT=kwTb[64 * gpar:64 * gpar + 64, gp, 384:516], kdeps=[d_kw], bias=bias_wnear,
                           v=[(vw_b[:, 3, 64 * g:64 * g + 64], 128, [d_vw]), (vnew[0:4, 1280 + 64 * g:1280 + 64 * g + 64], 4, [d_vn])])]
                items += branch(q16, [ub], 16, wg, Bacc[0:16, ab + 128:ab + 192], Bacc, 3 * g + 2)
            run_items(items)
            P.op("dve", lambda: V.tensor_reduce(out=den12[0:16, :], in_=dens[0:16, :, :], axis=AX.X, op=ALU.add), [dens], [den12])
            P.op("dve", lambda: V.tensor_scalar(out=den12[0:16, :], in0=den12[0:16, :], scalar1=1e-30, scalar2=None, op0=ALU.max), [den12], [den12])
            P.op("dve", lambda: V.reciprocal(out=den12[0:16, :], in_=den12[0:16, :]), [den12], [den12])
            P.op("dve", lambda: V.tensor_tensor(out=sc12[0:16, :].rearrange("p (g r) -> p g r", g=4), in0=den12[0:16, :].rearrange("p (g r) -> p g r", g=4),
                                                in1=G16[:, :].rearrange("p (r g) -> p g r", r=3), op=ALU.mult), [den12, G16], [sc12])
            for hb in range(2):
                t6 = tmpF[hb].t[0:16, 0:384].rearrange("p (a b) -> p a b", a=6)
                P.op("dve", lambda hb=hb, t6=t6: V.tensor_tensor(out=t6, in0=B[4 + hb][0:16, 0:384].rearrange("p (a b) -> p a b", a=6),
                                                               in1=bc(sc12[0:16, 6 * hb:6 * hb + 6], 64), op=ALU.mult), [B[4 + hb], sc12], [tmpF[hb]])
                t23 = tmpF[hb].t[0:16, 0:384].rearrange("p (g r d) -> p g r d", g=2, r=3)
                P.op("dve", lambda t23=t23, hb=hb: V.tensor_tensor(out=t23[:, :, 0, :], in0=t23[:, :, 0, :], in1=t23[:, :, 1, :], op=ALU.add), [tmpF[hb]], [tmpF[hb]])
                P.op("dve", lambda t23=t23, hb=hb: V.tensor_tensor(out=o16[0:16, 128 * hb:128 * hb + 128].rearrange("p (g d) -> p g d", g=2),
                                                                  in0=t23[:, :, 0, :], in1=t23[:, :, 2, :], op=ALU.add), [tmpF[hb]], [o16])
            pv3 = bfv(B[3])
            for g in range(4):
                P.tr(pv3[0:64, 16 * g:16 * g + 16], o16[0:16, 64 * g:64 * g + 64], ident_b[0:16, 0:16], [o16, ident_b], [B[3]])
            pq = pv3[0:64, 0:64].rearrange("p (gj two t) -> p gj two t", two=2, t=4)
            P.op("dve", lambda b_=b_, pq=pq: V.tensor_copy(out=oS[0:64, :, 4 * b_:4 * b_ + 4], in_=pq[:, :, 0, :]), [B[3]], [oS])
            P.op("dve", lambda b_=b_, pq=pq: V.tensor_copy(out=oS[64:128, :, 4 * b_:4 * b_ + 4], in_=pq[:, :, 1, :]), [B[3]], [oS])
        join([slabs[0].d, slabs[1].d, dM] + newdeps)
        P.op("act", lambda: A.copy(out=o_T[:, :, 512:576], in_=oS[:, :, :]), [oS], [dO])

    ATT = int(os.environ.get("KATT", "1"))
    for pi in range(2):
        st = pass_state if pi == 0 else run_pass(1)
        pass_qg(pi, st)
        if ATT:
            attention(pi, st)
            join(xdeps)
        else:
            P.op("pool", lambda: G.memset(RA.t[:, 4608:9216], 0.0), [], [dO])
        pass_merge_ffn(pi, st)
    return P, locals()


def _shared_inputs(inp):
    f = lambda a: np.ascontiguousarray(np.asarray(a, dtype=np.float32))
    sh = dict(
        w_in=f(inp["w_in"][0]), w_pp=f(inp["w_pool_proj"][0]), w_np=f(inp["w_nsa_proj"][0]),
        w_out=f(inp["w_out"][0]), w_up=f(inp["w_up"][0]), w_down=f(inp["w_down"][0]),
        w1k=f(inp["w1_cmp_k"][0]).reshape(32, 4096), w1v=f(inp["w1_cmp_v"][0]).reshape(32, 4096),
        w2k=f(inp["w2_cmp_k"][0]), w2v=f(inp["w2_cmp_v"][0]),
        pek=f(inp["pe_cmp_k"][0]), pev=f(inp["pe_cmp_v"][0]),
        relb=f(inp["rel_bias"]), wgrp=f(inp["w_pool_grp"][0]).reshape(1024, 256),
        g_pm=f(inp["g_pre_mix"]), g_qm=f(inp["g_post_mix"]), g_pf=f(inp["g_pre_ffn"]), g_qf=f(inp["g_post_ffn"]),
        pscale=f(inp["pool_scale"]), conv_w=f(inp["conv_w"][0]), conv_b=f(inp["conv_b"]),
        t_sbonus=_sample_bonus(),
        t_selb=_selb_table(), t_mi=np.ascontiguousarray((np.arange(16)[:, None] // 4 == np.arange(4)[None, :]).astype(np.float32)),
        t_msame=np.ascontiguousarray((np.arange(16)[:, None] % 4 == np.arange(16)[None, :] % 4).astype(np.float32)),
    )
    return sh


def _core_inputs(inp, c, shared, tables, cache2d):
    b, half = c // 2, c % 2
    xp = np.asarray(inp["x_prompt"], dtype=np.float32)
    if half == 0:
        xb = np.concatenate([np.zeros((1024, D), np.float32), xp[b, :1024]], axis=0)
    else:
        xb = xp[b]
    sl = slice(NSB * c, NSB * c + NSB)
    m = dict(shared)
    m.update(
        xb=np.ascontiguousarray(xb),
        xs=np.ascontiguousarray(np.asarray(inp["x_sample"], dtype=np.float32)[sl].reshape(64, D)),
        cache=cache2d,
        ptab=np.ascontiguousarray(np.asarray(inp["page_table"], dtype=np.int32)[sl].reshape(1, 256)),
        swin=np.ascontiguousarray(np.asarray(inp["state_kv_win"], dtype=np.float32)[0, sl].reshape(NSB * 512, 512)),
        spool=np.ascontiguousarray(np.asarray(inp["state_pool"], dtype=np.float32)[0, sl].reshape(NSB * 15, 1024)),
        sconv=np.ascontiguousarray(np.asarray(inp["state_conv"], dtype=np.float32)[0, sl].reshape(NSB * 2, DFF)),
    )
    t = tables[half]
    m.update(t_oh=t["oh"], t_far=t["far"], t_kb=t["kb"], t_kbc=t["kbc"], t_bonus=t["bonus"], t_sdead=t["sdead"], t_rc=t["rc"])
    return m


def _assemble(res):
    y_p = np.zeros((4, 2048, D), np.float32); y_s = np.zeros((128, 4, D), np.float32)
    kv_p = np.zeros((1, 4, 2048, 4, 4, 64), np.float32); kv_s = np.zeros((1, 128, 4, 4, 4, 64), np.float32)
    win_p = np.zeros((1, 4, 512, 2, 4, 64), np.float32); win_s = np.zeros((1, 128, 512, 2, 4, 64), np.float32)
    pool_p = np.zeros((1, 4, 15, 1024), np.float32); pool_s = np.zeros((1, 128, 15, 1024), np.float32)
    conv_p = np.zeros((1, 4, 2, DFF), np.float32); conv_s = np.zeros((1, 128, 2, DFF), np.float32)
    for c, r in res.items():
        b, half = c // 2, c % 2
        sl = slice(NSB * c, NSB * c + NSB)
        y_p[b, 1024 * half:1024 * half + 1024] = r["y"][:1024]
        y_s[sl] = r["y"][1024:].reshape(NSB, 4, D)
        kv_p[0, b, 1024 * half:1024 * half + 1024] = r["kvrows"][:1024].reshape(1024, 4, 4, 64)
        kv_s[0, sl] = r["kvrows"][1024:].reshape(NSB, 4, 4, 4, 64)
        win_s[0, sl] = r["wins"].reshape(NSB, 512, 2, 4, 64)
        pool_s[0, sl] = r["pools"].reshape(NSB, 15, 1024)
        conv_s[0, sl] = r["convs"].reshape(NSB, 2, DFF)
        if half == 1:
            win_p[0, b] = r["winp"].reshape(512, 2, 4, 64)
            pool_p[0, b] = r["poolp"]
            conv_p[0, b] = r["convp"]
    return (y_p, y_s, kv_p, kv_s, win_p, win_s, pool_p, pool_s, conv_p, conv_s)


def kernel(**inputs):
    P, _ = build_program(NPOOL_PAGES)
    P.cx.finish()
    shared = _shared_inputs(inputs)
    tables = [_const_tables(0), _const_tables(1)]
    cache2d = np.ascontiguousarray(np.asarray(inputs["cache_kv"], dtype=np.float32)[0].reshape(NPOOL_PAGES * 128, 1024))
    in_maps = [_core_inputs(inputs, c, shared, tables, cache2d) for c in range(8)]
    res = run_bass_kernel_spmd(P.nc, in_maps, core_ids=list(range(8)))
    return _assemble({c: res.results[c] for c in range(8)})
```
